# Optimizing a Trainium2 kernel written in Bass

```python
import math
import jax
import jax.numpy as jnp
from jax import lax
import numpy as np

D_MODEL = 1024
BATCH = 16
SEQ = 2048
DEPTH = 4

GRID_W = 64
CTX_LEN = 256
N_MIXERS = 3
N_MOD = 6
EPS = 1e-6

SSM_EXPAND = 2
D_INNER = SSM_EXPAND * D_MODEL
SSM_HEAD_DIM = 64
SSM_HEADS = D_INNER // SSM_HEAD_DIM
SSM_GROUPS = 8
SSM_STATE = 128
SSM_CONV = 4
SSM_CHUNK = 128
SSM_BC = SSM_GROUPS * SSM_STATE
SSM_CONV_CH = D_INNER + 2 * SSM_BC
SSM_IN = D_INNER + SSM_CONV_CH + 2 * SSM_HEADS

DA_HEAD_DIM = 64
DA_HEADS = D_MODEL // (2 * DA_HEAD_DIM)
DA_Q_BLOCK = 128
ROPE_THETA = 10000.0

CONF_KERNEL = 31

FFN_HIDDEN = 2816
FFN_CONV = 3

N_A = len(range(0, DEPTH, N_MIXERS))
N_B = len(range(1, DEPTH, N_MIXERS))
N_C = len(range(2, DEPTH, N_MIXERS))

kernel_name = 'hybrid_ssd_diffattn_conformer_trunk'


def rms_norm(x, g):
    xf = x.astype(jnp.float32)
    y = xf * lax.rsqrt(jnp.mean(xf * xf, axis=-1, keepdims=True) + EPS)
    return (y * g.astype(jnp.float32)).astype(x.dtype)


def layer_norm(x, g, b):
    xf = x.astype(jnp.float32)
    mu = jnp.mean(xf, axis=-1, keepdims=True)
    xc = xf - mu
    y = xc * lax.rsqrt(jnp.mean(xc * xc, axis=-1, keepdims=True) + EPS)
    return (y * g.astype(jnp.float32) + b.astype(jnp.float32)).astype(x.dtype)


def modulate(x, g, shift, scale):
    return rms_norm(x, g) * (1 + scale) + shift


def dwconv(x, w, b, pad_l, pad_r):
    y = lax.conv_general_dilated(x, w[:, None, :].astype(x.dtype), window_strides=(1,),
                                 padding=[(pad_l, pad_r)],
                                 dimension_numbers=('NWC', 'WIO', 'NWC'),
                                 feature_group_count=x.shape[-1])
    return y + b.astype(x.dtype)


def flip_seq(t, rev):
    return jnp.flip(t, axis=1) if rev else t


def ssd_chunked(x, dt, a, bm, cm, h0):
    b, s, _, p = x.shape
    g, n = bm.shape[2], bm.shape[3]
    j = SSM_HEADS // g
    nc = s // SSM_CHUNK
    xr = (x * dt[..., None]).reshape(b, nc, SSM_CHUNK, g, j, p)
    la = (dt * a).reshape(b, nc, SSM_CHUNK, g, j)
    ac = jnp.moveaxis(jnp.cumsum(la, axis=2), 2, -1)
    br = bm.reshape(b, nc, SSM_CHUNK, g, n)
    cr = cm.reshape(b, nc, SSM_CHUNK, g, n)
    mask = jnp.tril(jnp.ones((SSM_CHUNK, SSM_CHUNK), dtype=bool))
    seg = ac[..., :, None] - ac[..., None, :]
    lmat = jnp.exp(jnp.where(mask, seg, -jnp.inf))
    cb = jnp.einsum('bclgn,bcsgn->bcgls', cr, br)
    y_diag = jnp.einsum('bcgls,bcgjls,bcsgjp->bclgjp', cb, lmat, xr)
    decay_to_end = jnp.exp(ac[..., -1:] - ac)
    states = jnp.einsum('bclgn,bcgjl,bclgjp->bcgjpn', br, decay_to_end, xr)
    chunk_decay = jnp.exp(ac[..., -1])

    def step(h, inp):
        st, dec = inp
        return h * dec[..., None, None] + st, h

    h_final, h_prev = lax.scan(step, h0.reshape(b, g, j, p, n),
                               (jnp.moveaxis(states, 1, 0), jnp.moveaxis(chunk_decay, 1, 0)))
    h_prev = jnp.moveaxis(h_prev, 0, 1)
    y_off = jnp.einsum('bclgn,bcgjpn,bcgjl->bclgjp', cr, h_prev, jnp.exp(ac))
    y = (y_diag + y_off).reshape(b, s, SSM_HEADS, p)
    return y, h_final.reshape(b, SSM_HEADS, p, n)


def ssm_prep(xbc, dt_raw, conv_w, conv_b, dt_bias):
    u = jax.nn.silu(dwconv(xbc, conv_w, conv_b, SSM_CONV - 1, 0))
    b, n = u.shape[0], u.shape[1]
    xs = u[..., :D_INNER].reshape(b, n, SSM_HEADS, SSM_HEAD_DIM)
    bm = u[..., D_INNER:D_INNER + SSM_BC].reshape(b, n, SSM_GROUPS, SSM_STATE)
    cm = u[..., D_INNER + SSM_BC:].reshape(b, n, SSM_GROUPS, SSM_STATE)
    dt = jax.nn.softplus(dt_raw + dt_bias)
    return xs, dt, bm, cm


def ssm_direction(xbc_ctx, dt_ctx, xbc_lat, dt_lat, conv_w, conv_b, dt_bias, a_log, d_skip):
    a = -jnp.exp(a_log)
    xc, dtc, bc, cc = ssm_prep(xbc_ctx, dt_ctx, conv_w, conv_b, dt_bias)
    h0 = jnp.zeros((xc.shape[0], SSM_HEADS, SSM_HEAD_DIM, SSM_STATE), xc.dtype)
    y_c, h_c = ssd_chunked(xc, dtc, a, bc, cc, h0)
    xl, dtl, bl, cl = ssm_prep(xbc_lat, dt_lat, conv_w, conv_b, dt_bias)
    y_l, _ = ssd_chunked(xl, dtl, a, bl, cl, h_c)
    skip = d_skip[:, None]
    return y_c + skip * xc, y_l + skip * xl


def mamba_mixer(h_lat, h_ctx, w_in, conv_w, conv_b, dt_bias, a_log, d_skip, norm_g, w_out, need_ctx):
    p_lat = h_lat @ w_in
    p_ctx = h_ctx @ w_in
    o_x, o_dt = D_INNER, D_INNER + SSM_CONV_CH
    y_lat = 0.0
    y_ctx = 0.0
    for dr in range(2):
        rev = dr == 1
        sl = slice(o_dt + dr * SSM_HEADS, o_dt + (dr + 1) * SSM_HEADS)
        yc, yl = ssm_direction(flip_seq(p_ctx[..., o_x:o_dt], rev), flip_seq(p_ctx[..., sl], rev),
                               flip_seq(p_lat[..., o_x:o_dt], rev), flip_seq(p_lat[..., sl], rev),
                               conv_w[dr], conv_b[dr], dt_bias[dr], a_log[dr], d_skip[dr])
        y_ctx = y_ctx + flip_seq(yc, rev)
        y_lat = y_lat + flip_seq(yl, rev)

    def out(y, z):
        b, n = y.shape[0], y.shape[1]
        yg = (y.reshape(b, n, D_INNER) * jax.nn.silu(z)).reshape(b, n, SSM_GROUPS, D_INNER // SSM_GROUPS)
        yg = rms_norm(yg, norm_g.reshape(SSM_GROUPS, -1))
        return yg.reshape(b, n, D_INNER) @ w_out

    out_ctx = out(y_ctx, p_ctx[..., :o_x]) if need_ctx else None
    return out(y_lat, p_lat[..., :o_x]), out_ctx


def axial_rope_angles(n_tokens):
    rows = n_tokens // GRID_W
    row = jnp.repeat(jnp.arange(rows), GRID_W).astype(jnp.float32)
    col = jnp.tile(jnp.arange(GRID_W), rows).astype(jnp.float32)
    n_freq = DA_HEAD_DIM // 4
    inv = ROPE_THETA ** (-jnp.arange(n_freq, dtype=jnp.float32) / n_freq)
    return jnp.stack([row[:, None] * inv, col[:, None] * inv], axis=1)


def apply_axial_rope(x, ang):
    sh = x.shape
    xr = x.reshape(sh[:-1] + (2, 2, sh[-1] // 4))
    cos = jnp.cos(ang)[None, :, None, None]
    sin = jnp.sin(ang)[None, :, None, None]
    x1, x2 = xr[..., 0, :], xr[..., 1, :]
    out = jnp.stack([x1 * cos - x2 * sin, x1 * sin + x2 * cos], axis=-2)
    return out.reshape(sh).astype(x.dtype)


def diff_attention(h_lat, h_ctx, w_in, lam_p, norm_g, w_out, layer_idx, need_ctx):
    b, s, _ = h_lat.shape
    scale = DA_HEAD_DIM ** -0.5

    def qkv(h):
        n = h.shape[1]
        p = h @ w_in
        q = p[..., :D_MODEL].reshape(b, n, DA_HEADS, 2, DA_HEAD_DIM) * scale
        k = p[..., D_MODEL:2 * D_MODEL].reshape(b, n, DA_HEADS, 2, DA_HEAD_DIM)
        v = p[..., 2 * D_MODEL:].reshape(b, n, DA_HEADS, 2 * DA_HEAD_DIM)
        return q, k, v

    q_l, k_l, v_l = qkv(h_lat)
    q_c, k_c, v_c = qkv(h_ctx)
    ang = axial_rope_angles(s)
    q_l = apply_axial_rope(q_l, ang)
    k_l = apply_axial_rope(k_l, ang)

    lam_init = 0.8 - 0.6 * math.exp(-0.3 * layer_idx)
    lp = lam_p.astype(jnp.float32)
    lam = jnp.exp(jnp.sum(lp[0] * lp[1])) - jnp.exp(jnp.sum(lp[2] * lp[3])) + lam_init

    def attend(q, k, v):
        sc = jnp.einsum('bqhed,bkhed->bheqk', q, k).astype(jnp.float32)
        pr = jax.nn.softmax(sc, axis=-1)
        w = pr[:, :, 0] - lam * pr[:, :, 1]
        return jnp.einsum('bhqk,bkhv->bqhv', w.astype(v.dtype), v)

    def finish(o):
        o = rms_norm(o, norm_g) * (1.0 - lam_init)
        return o.reshape(o.shape[0], o.shape[1], D_MODEL) @ w_out

    k_all = jnp.concatenate([k_c, k_l], axis=1)
    v_all = jnp.concatenate([v_c, v_l], axis=1)
    qb = jnp.swapaxes(q_l.reshape(b, s // DA_Q_BLOCK, DA_Q_BLOCK, DA_HEADS, 2, DA_HEAD_DIM), 0, 1)
    o_l = lax.map(lambda qq: attend(qq, k_all, v_all), qb)
    o_l = jnp.swapaxes(o_l, 0, 1).reshape(b, s, DA_HEADS, 2 * DA_HEAD_DIM)
    out_ctx = finish(attend(q_c, k_c, v_c)) if need_ctx else None
    return finish(o_l), out_ctx


def conformer_conv(h, w_pw1, b_pw1, dw_w, dw_b, ln_g, ln_b, w_pw2, b_pw2):
    a = h @ w_pw1 + b_pw1
    u = a[..., :D_MODEL] * jax.nn.sigmoid(a[..., D_MODEL:])
    u = dwconv(u, dw_w, dw_b, CONF_KERNEL // 2, CONF_KERNEL // 2)
    u = jax.nn.silu(layer_norm(u, ln_g, ln_b))
    return u @ w_pw2 + b_pw2


def conv_ffn(h, w_up, conv_w, conv_b, w_down):
    u = dwconv(h @ w_up, conv_w, conv_b, FFN_CONV // 2, FFN_CONV // 2)
    return (jax.nn.silu(u[..., :FFN_HIDDEN]) * u[..., FFN_HIDDEN:]) @ w_down


def setup_inputs(seed: int = 0) -> dict:
    key = jax.random.key(seed)
    ks = iter(jax.random.split(key, 48))
    D = D_MODEL

    def nrm(shape, scale=1.0):
        return jax.random.normal(next(ks), shape, jnp.float32) * scale

    def gain(shape):
        return 1.0 + nrm(shape, 0.02)

    dt0 = jnp.exp(jax.random.uniform(next(ks), (N_A, 2, SSM_HEADS), jnp.float32,
                                     minval=math.log(1e-3), maxval=math.log(1e-1)))
    dt_bias = dt0 + jnp.log(-jnp.expm1(-dt0))
    a_log = jnp.log(jax.random.uniform(next(ks), (N_A, 2, SSM_HEADS), jnp.float32, minval=1.0, maxval=16.0))
    return {
        'x': nrm((BATCH, SEQ, D)),
        'c': nrm((BATCH, D)),
        'ctx': nrm((BATCH, CTX_LEN, D)),
        'c_ctx': nrm((D,)),
        'mod_w': nrm((DEPTH, D, N_MOD * D), D ** -0.5),
        'mod_b': nrm((DEPTH, N_MOD * D), 0.01),
        'norm1_g': gain((DEPTH, D)),
        'norm2_g': gain((DEPTH, D)),
        'ffn_w_up': nrm((DEPTH, D, 2 * FFN_HIDDEN), D ** -0.5),
        'ffn_conv_w': nrm((DEPTH, FFN_CONV, 2 * FFN_HIDDEN), FFN_CONV ** -0.5),
        'ffn_conv_b': nrm((DEPTH, 2 * FFN_HIDDEN), 0.01),
        'ffn_w_down': nrm((DEPTH, FFN_HIDDEN, D), FFN_HIDDEN ** -0.5),
        'ssm_w_in': nrm((N_A, D, SSM_IN), D ** -0.5),
        'ssm_conv_w': nrm((N_A, 2, SSM_CONV, SSM_CONV_CH), SSM_CONV ** -0.5),
        'ssm_conv_b': nrm((N_A, 2, SSM_CONV_CH), 0.01),
        'ssm_dt_bias': dt_bias,
        'ssm_a_log': a_log,
        'ssm_d': 1.0 + nrm((N_A, 2, SSM_HEADS), 0.1),
        'ssm_norm_g': gain((N_A, D_INNER)),
        'ssm_w_out': nrm((N_A, D_INNER, D), D_INNER ** -0.5),
        'attn_w_in': nrm((N_B, D, 3 * D), D ** -0.5),
        'attn_lambda': nrm((N_B, 4, DA_HEAD_DIM), 0.1),
        'attn_norm_g': gain((N_B, 2 * DA_HEAD_DIM)),
        'attn_w_out': nrm((N_B, D, D), D ** -0.5),
        'conf_w_pw1': nrm((N_C, D, 2 * D), D ** -0.5),
        'conf_b_pw1': nrm((N_C, 2 * D), 0.01),
        'conf_dw_w': nrm((N_C, CONF_KERNEL, D), CONF_KERNEL ** -0.5),
        'conf_dw_b': nrm((N_C, D), 0.01),
        'conf_ln_g': gain((N_C, D)),
        'conf_ln_b': nrm((N_C, D), 0.01),
        'conf_w_pw2': nrm((N_C, D, D), D ** -0.5),
        'conf_b_pw2': nrm((N_C, D), 0.01),
        'final_g': gain((D,)),
    }


def reference(x, c, ctx, c_ctx, mod_w, mod_b, norm1_g, norm2_g, ffn_w_up, ffn_conv_w, ffn_conv_b,
              ffn_w_down, ssm_w_in, ssm_conv_w, ssm_conv_b, ssm_dt_bias, ssm_a_log, ssm_d, ssm_norm_g,
              ssm_w_out, attn_w_in, attn_lambda, attn_norm_g, attn_w_out, conf_w_pw1, conf_b_pw1,
              conf_dw_w, conf_dw_b, conf_ln_g, conf_ln_b, conf_w_pw2, conf_b_pw2, final_g):
    xc = ctx
    s_lat = jax.nn.silu(c)
    s_ctx = jax.nn.silu(c_ctx)
    for i in range(DEPTH):
        kind, j = i % N_MIXERS, i // N_MIXERS
        need_ctx = i < DEPTH - 1
        ml = jnp.split((s_lat @ mod_w[i] + mod_b[i])[:, None, :], N_MOD, axis=-1)
        mc = jnp.split((s_ctx @ mod_w[i] + mod_b[i])[None, None, :], N_MOD, axis=-1)
        h_lat = modulate(x, norm1_g[i], ml[0], ml[1])
        h_ctx = modulate(xc, norm1_g[i], mc[0], mc[1])
        if kind == 0:
            y_lat, y_ctx = mamba_mixer(h_lat, h_ctx, ssm_w_in[j], ssm_conv_w[j], ssm_conv_b[j], ssm_dt_bias[j],
                                       ssm_a_log[j], ssm_d[j], ssm_norm_g[j], ssm_w_out[j], need_ctx)
        elif kind == 1:
            y_lat, y_ctx = diff_attention(h_lat, h_ctx, attn_w_in[j], attn_lambda[j], attn_norm_g[j],
                                          attn_w_out[j], i, need_ctx)
        else:
            conf = (conf_w_pw1[j], conf_b_pw1[j], conf_dw_w[j], conf_dw_b[j], conf_ln_g[j], conf_ln_b[j],
                    conf_w_pw2[j], conf_b_pw2[j])
            y_lat = conformer_conv(h_lat, *conf)
            y_ctx = conformer_conv(h_ctx, *conf) if need_ctx else None
        ffn = (ffn_w_up[i], ffn_conv_w[i], ffn_conv_b[i], ffn_w_down[i])
        x = x + ml[2] * y_lat
        x = x + ml[5] * conv_ffn(modulate(x, norm2_g[i], ml[3], ml[4]), *ffn)
        if need_ctx:
            xc = xc + mc[2] * y_ctx
            xc = xc + mc[5] * conv_ffn(modulate(xc, norm2_g[i], mc[3], mc[4]), *ffn)
    return rms_norm(x, final_g)
```

```python
import math
from contextlib import ExitStack
import numpy as np
import concourse.bass as bass
import concourse.mybir as mybir
from concourse.bass_utils import run_bass_kernel_spmd

F32 = mybir.dt.float32
BF16 = mybir.dt.bfloat16
AF = mybir.ActivationFunctionType
ALU = mybir.AluOpType
AX = mybir.AxisListType

EPS = 1e-6
D = 1024
SEQ = 2048
CTX = 256
SEG = CTX + SEQ
T = 2 * SEG
NCORES = 8
FH = 2816
NEG = -30000.0


class Buf:
    def __init__(self, t, name, psum=False):
        self.t = t
        self.name = name
        self.psum = psum
        self.last_w = None
        self.readers = []

    def __getitem__(self, k):
        return self.t[k]


class Tag:
    __slots__ = ("sem", "val", "eng")

    def __init__(self, sem, val, eng):
        self.sem, self.val, self.eng = sem, val, eng


class Eng:
    def __init__(self, name, h, kind):
        self.name, self.h, self.kind = name, h, kind
        self.sem = None
        self.count = 0
        self.seen = {}
        self.dsems = []
        self.dnext = 0
        self.n_ins = 0

    def _wait(self, tag):
        key = id(tag.sem)
        if self.seen.get(key, 0) >= tag.val:
            return
        self.h.wait_ge(tag.sem, tag.val)
        self.seen[key] = tag.val

    def _deps(self, reads, writes):
        for b in reads:
            t = b.last_w
            if t is not None:
                if t.eng is self and self.kind == "pe":
                    continue
                self._wait(t)
            if b.psum:
                for r in b.readers:
                    if r.eng is not self:
                        self._wait(r)
        for b in writes:
            t = b.last_w
            if t is not None and not (t.eng is self and self.kind == "pe"):
                self._wait(t)
            for r in b.readers:
                self._wait(r)

    def _record(self, tag, reads, writes):
        for b in reads:
            if tag.eng is not None:
                b.readers = [r for r in b.readers if r.eng is not tag.eng]
            b.readers.append(tag)
        for b in writes:
            b.last_w = tag
            b.readers = []

    def op(self, fn, reads=(), writes=(), inc=True):
        self._deps(reads, writes)
        ins = fn(self.h)
        self.n_ins += 1
        if inc:
            self.count += 1
            ins.then_inc(self.sem, 1)
            tag = Tag(self.sem, self.count, self)
        else:
            tag = Tag(self.sem, self.count + 1, self)
        self._record(tag, reads, writes)
        return ins

    def dma(self, out, in_, reads=(), writes=(), **kw):
        self._deps(reads, writes)
        i = self.dnext % len(self.dsems)
        self.dnext += 1
        sem, cnt = self.dsems[i]
        if cnt > 0:
            self._wait(Tag(sem, cnt, None))
        cnt += 16
        self.dsems[i] = (sem, cnt)
        self.h.dma_start(out=out, in_=in_, **kw).then_inc(sem, 16)
        self.n_ins += 1
        tag = Tag(sem, cnt, None)
        self._record(tag, reads, writes)
        return tag


class FW:
    def __init__(self, nc, stack, n_dma_sems=8):
        self.nc = nc
        self.pe = Eng("pe", nc.tensor, "pe")
        self.act = Eng("act", nc.scalar, "act")
        self.dve = Eng("dve", nc.vector, "dve")
        self.pool = Eng("pool", nc.gpsimd, "pool")
        self.sp = Eng("sp", nc.sync, "sp")
        self.engs = [self.pe, self.act, self.dve, self.pool, self.sp]
        for e in self.engs:
            e.sem = stack.enter_context(nc.semaphore("s_" + e.name))
        for e in (self.sp, self.pool, self.act):
            for i in range(n_dma_sems):
                s = stack.enter_context(nc.semaphore(f"d_{e.name}{i}"))
                e.dsems.append((s, 0))
        self._uid = 0

    def sb(self, cm, shape, dtype, name="t"):
        self._uid += 1
        name = f"{name}_{self._uid}"
        return Buf(cm.enter_context(self.nc.sbuf_tensor(name, list(shape), dtype)), name)

    def ps(self, cm, shape, dtype, name="p"):
        self._uid += 1
        name = f"{name}_{self._uid}"
        return Buf(cm.enter_context(self.nc.psum_tensor(name, list(shape), dtype)), name, psum=True)

    def barrier(self):
        tags = []
        for e in self.engs:
            if e.count > 0:
                tags.append(Tag(e.sem, e.count, e))
            for (s, c) in e.dsems:
                if c > 0:
                    tags.append(Tag(s, c, None))
        for e in self.engs:
            for t in tags:
                if t.eng is e:
                    continue
                e._wait(t)


class PsumPool:
    def __init__(self, bufs):
        self.bufs = bufs
        self.i = 0

    def get(self):
        b = self.bufs[self.i % len(self.bufs)]
        self.i += 1
        return b


class Ctx:
    pass


def blocks(include_ctx=True):
    out = []
    for b in range(2):
        base = b * SEG
        if include_ctx:
            out.append((2, base, CTX, True))
        for q in range(4):
            out.append((b, base + CTX + q * 512, 512, False))
    return out


def bc_mid(ap2d, n):
    (ps, pn), (fs, fn) = ap2d.ap
    return bass.AP(ap2d.tensor, ap2d.offset, [[ps, pn], [0, n], [fs, fn]])


def load_vecT(K, cm, rows_ap, n, name="vT"):
    fw = K.fw
    dst = fw.sb(cm, [128, n], F32, name)
    with ExitStack() as st:
        tmp = fw.sb(st, [128, 128], F32, "vrow")
        fw.sp.dma(tmp[0:n, :], rows_ap, writes=[tmp])
        ps = K.psum.get()
        fw.pe.op(lambda e: e.transpose(out=ps[:, 0:n], in_=tmp[0:n, :], identity=K.ident[0:n, 0:n]),
                 reads=[tmp, K.ident], writes=[ps])
        fw.dve.op(lambda e: e.tensor_copy(out=dst[:, :], in_=ps[:, 0:n]), reads=[ps], writes=[dst])
        fw.barrier()
    return dst


def norm_block(K, W, xin, N, A, sh, v, hT, hoff=0):
    fw = K.fw
    fw.act.op(lambda e: e.activation(out=W.sq[:, :, :N], in_=xin[:, :, :N], func=AF.Square),
              reads=[xin], writes=[W.sq])
    ps = K.psum.get()
    for k in range(8):
        fw.pe.op(lambda e: e.matmul(ps[:, :N], lhsT=K.ones_bf[:, :], rhs=W.sq[:, k, :N], start=(k == 0), stop=(k == 7)),
                 reads=[W.sq, K.ones_bf], writes=[ps], inc=(k == 7))
    fw.act.op(lambda e: e.activation(out=W.rstd[:, :N], in_=ps[:, :N], func=AF.Ln, bias=K.epsc[:, 0:1], scale=1.0 / D),
              reads=[ps], writes=[W.rstd])
    fw.act.op(lambda e: e.activation(out=W.rstd[:, :N], in_=W.rstd[:, :N], func=AF.Exp, scale=-0.5),
              reads=[W.rstd], writes=[W.rstd])
    for k in range(8):
        tmp = W.ntmp[k % 2]
        fw.dve.op(lambda e: e.scalar_tensor_tensor(out=tmp[:, :N], in0=xin[:, k, :N], scalar=A[:, k, v:v + 1],
                                                   in1=W.rstd[:, :N], op0=ALU.mult, op1=ALU.mult),
                  reads=[xin, W.rstd], writes=[tmp])
        fw.act.op(lambda e: e.activation(out=hT[:, k, hoff:hoff + N], in_=tmp[:, :N], func=AF.Identity,
                                         bias=sh[:, k, v:v + 1], scale=1.0),
                  reads=[tmp], writes=[hT])


def alloc_norm_work(K, cm):
    fw = K.fw
    W = Ctx()
    W.sq = fw.sb(cm, [128, 8, 512], BF16, "sq")
    W.rstd = fw.sb(cm, [128, 512], F32, "rstd")
    W.ntmp = [fw.sb(cm, [128, 512], F32, "ntmp") for _ in range(2)]
    return W


def xT_view(K):
    return K.xT_d.rearrange("(k p) t -> p k t", p=128)


def phase_init(K):
    fw = K.fw
    xv = xT_view(K)
    with ExitStack() as st:
        tin = [fw.sb(st, [128, D], F32, "tin") for _ in range(2)]
        tout = [fw.sb(st, [128, 8, 128], F32, "tout") for _ in range(2)]
        it = 0
        for b in range(2):
            for (src, n, off) in ((K.ctx_in, CTX, 0), (K.x_in, SEQ, CTX)):
                for j in range(n // 128):
                    ti, to = tin[it % 2], tout[it % 2]
                    it += 1
                    fw.sp.dma(ti[:, :], src[b, j * 128:(j + 1) * 128, :], writes=[ti])
                    for half in range(2):
                        ps = K.psum.get()
                        for q in range(4):
                            k = half * 4 + q
                            fw.pe.op(lambda e: e.transpose(out=ps[:, q * 128:(q + 1) * 128], in_=ti[:, k * 128:(k + 1) * 128],
                                                           identity=K.ident[:, :]),
                                     reads=[ti, K.ident], writes=[ps], inc=(q == 3))
                        eng = fw.dve if half == 0 else fw.act
                        if half == 0:
                            fw.dve.op(lambda e: e.tensor_copy(out=to[:, 0:4, :], in_=ps[:, :].rearrange("p (q t) -> p q t", q=4)),
                                      reads=[ps], writes=[to])
                        else:
                            fw.act.op(lambda e: e.activation(out=to[:, 4:8, :], in_=ps[:, :].rearrange("p (q t) -> p q t", q=4),
                                                             func=AF.Identity),
                                      reads=[ps], writes=[to])
                    t0 = b * SEG + off + j * 128
                    fw.sp.dma(xv[:, :, t0:t0 + 128], to[:, :, :], reads=[to])
        fw.barrier()


def phase_mod(K, layers):
    fw = K.fw
    K.modT, K.A1, K.A2 = {}, {}, {}
    for i in layers:
        K.modT[i] = fw.sb(K.st, [128, 48, 3], F32, f"modT{i}")
        K.A1[i] = fw.sb(K.st, [128, 8, 3], F32, f"A1_{i}")
        K.A2[i] = fw.sb(K.st, [128, 8, 3], F32, f"A2_{i}")
    with ExitStack() as st:
        wsb = fw.sb(st, [128, 8, 6144], BF16, "modw")
        c24 = fw.sb(st, [24, 128], F32, "c24")
        sT = fw.sb(st, [128, 24], BF16, "sT")
        fw.sp.dma(c24[:, :], K.c3.rearrange("v (k p) -> (v k) p", p=128), writes=[c24])
        ps = K.psum.get()
        fw.pe.op(lambda e: e.transpose(out=ps[:, 0:24], in_=c24[:, :], identity=K.ident[0:24, 0:24]),
                 reads=[c24, K.ident], writes=[ps])
        fw.act.op(lambda e: e.activation(out=sT[:, :], in_=ps[:, 0:24], func=AF.Silu), reads=[ps], writes=[sT])
        for i in layers:
            with ExitStack() as st2:
                mb = load_vecT(K, st2, K.w["mod_b"][i].rearrange("(c p) -> c p", p=128), 48, "mb")
                g1 = load_vecT(K, st2, K.w["norm1_g"][i].rearrange("(c p) -> c p", p=128), 8, "g1")
                g2 = load_vecT(K, st2, K.w["norm2_g"][i].rearrange("(c p) -> c p", p=128), 8, "g2")
                tmp = fw.sb(st2, [128, 8, 3], F32, "mtmp")
                fw.pool.dma(wsb[:, :, :], K.w["mod_w"][i].rearrange("(k p) n -> p k n", p=128), writes=[wsb])
                ps = K.psum.get()
                for cc in range(48):
                    for k in range(8):
                        fw.pe.op(lambda e: e.matmul(ps[:, cc * 3:(cc + 1) * 3], lhsT=wsb[:, k, cc * 128:(cc + 1) * 128],
                                                    rhs=sT[:, k:24:8], start=(k == 0), stop=(k == 7)),
                                 reads=[wsb, sT], writes=[ps], inc=(k == 7))
                mT = K.modT[i]
                fw.dve.op(lambda e: e.tensor_tensor(out=mT[:, :, :], in0=ps[:, 0:144].rearrange("p (c v) -> p c v", v=3),
                                                    in1=mb[:, :].unsqueeze(2).to_broadcast([128, 48, 3]), op=ALU.add),
                          reads=[ps, mb], writes=[mT])
                for (A, g, c0) in ((K.A1[i], g1, 8), (K.A2[i], g2, 32)):
                    fw.dve.op(lambda e: e.tensor_scalar(out=tmp[:, :, :], in0=mT[:, c0:c0 + 8, :], scalar1=1.0, scalar2=None, op0=ALU.add),
                              reads=[mT], writes=[tmp])
                    fw.dve.op(lambda e: e.tensor_tensor(out=A[:, :, :], in0=tmp[:, :, :],
                                                        in1=g[:, :].unsqueeze(2).to_broadcast([128, 8, 3]), op=ALU.mult),
                              reads=[tmp, g], writes=[A])
                fw.barrier()
        fw.barrier()


def phase_ffn_up(K, i, include_ctx):
    fw = K.fw
    xv = xT_view(K)
    gv = K.gT_d.rearrange("(c p) t -> p c t", p=128)
    with ExitStack() as st:
        cw = [load_vecT(K, st, K.w["ffn_conv_w"][i, kk].rearrange("(c p) -> c p", p=128), 44, "fcw") for kk in range(3)]
        cb = load_vecT(K, st, K.w["ffn_conv_b"][i].rearrange("(c p) -> c p", p=128), 44, "fcb")
        hT = fw.sb(st, [128, 8, T], BF16, "hT")
        with ExitStack() as st2:
            W = alloc_norm_work(K, st2)
            xin = [fw.sb(st2, [128, 8, 512], F32, "xin") for _ in range(2)]
            for bi, (v, t0, N, isc) in enumerate(blocks(include_ctx)):
                xi = xin[bi % 2]
                fw.sp.dma(xi[:, :, :N], xv[:, :, t0:t0 + N], writes=[xi])
                norm_block(K, W, xi, N, K.A2[i], K.modT[i][:, 24:32, :], v, hT, hoff=t0)
            fw.barrier()
        wb = [fw.sb(st, [128, 8, 256], BF16, "wup") for _ in range(3)]
        up = [[fw.sb(st, [128, SEQ + 2], F32, "upre") for _ in range(2)] for _ in range(2)]
        accs = [[fw.sb(st, [128, SEQ], F32, "acc") for _ in range(2)] for _ in range(2)]
        sils = [fw.sb(st, [128, SEQ], F32, "sil") for _ in range(2)]
        gt = [fw.sb(st, [128, SEQ], BF16, "gt") for _ in range(2)]
        for u2 in up:
            for u in u2:
                fw.pool.op(lambda e: e.memset(u[:, :], 0.0), writes=[u])
        wsrc = K.w["ffn_w_up"][i].rearrange("(k p) n -> p k n", p=128)
        segs = []
        for b in range(2):
            if include_ctx:
                segs.append((b * SEG, CTX))
            segs.append((b * SEG + CTX, SEQ))
        it = 0

        def wload(c):
            w = wb[c % 3]
            fw.pool.dma(w[:, :, 0:128], wsrc[:, :, c * 128:(c + 1) * 128], writes=[w])
            fw.pool.dma(w[:, :, 128:256], wsrc[:, :, FH + c * 128:FH + (c + 1) * 128], writes=[w])

        wload(0)
        wload(1)
        for c in range(22):
            if c + 2 < 22:
                wload(c + 2)
            w = wb[c % 3]
            for (s0, L) in segs:
                ub = up[it % 2]
                g = gt[it % 2]
                acc = accs[it % 2]
                sil = sils[it % 2]
                if L == SEQ:
                    it += 1
                for half in range(2):
                    u = ub[half]
                    cc = c + 22 * half
                    if L != SEQ:
                        fw.pool.op(lambda e: e.memset(u[:, L + 1:L + 2], 0.0), writes=[u])
                    for n0 in range(0, L, 512):
                        n = min(512, L - n0)
                        ps = K.psum.get()
                        for k in range(8):
                            fw.pe.op(lambda e: e.matmul(ps[:, :n], lhsT=w[:, k, half * 128:(half + 1) * 128],
                                                        rhs=hT[:, k, s0 + n0:s0 + n0 + n], start=(k == 0), stop=(k == 7)),
                                     reads=[w, hT], writes=[ps], inc=(k == 7))
                        fw.act.op(lambda e: e.activation(out=u[:, 1 + n0:1 + n0 + n], in_=ps[:, :n], func=AF.Identity),
                                  reads=[ps], writes=[u])
                    a = acc[half]
                    fw.act.op(lambda e: e.activation(out=a[:, :L], in_=u[:, 0:L], func=AF.Identity, scale=cw[0][:, cc:cc + 1],
                                                     bias=cb[:, cc:cc + 1]),
                              reads=[u], writes=[a])
                    for kk in (1, 2):
                        fw.dve.op(lambda e: e.scalar_tensor_tensor(out=a[:, :L], in0=u[:, kk:kk + L], scalar=cw[kk][:, cc:cc + 1],
                                                                   in1=a[:, :L], op0=ALU.mult, op1=ALU.add),
                                  reads=[u, a], writes=[a])
                fw.act.op(lambda e: e.activation(out=sil[:, :L], in_=acc[0][:, :L], func=AF.Silu), reads=[acc[0]], writes=[sil])
                fw.pool.op(lambda e: e.tensor_tensor(out=g[:, :L], in0=sil[:, :L], in1=acc[1][:, :L], op=ALU.mult),
                           reads=[sil, acc[1]], writes=[g])
                fw.sp.dma(gv[:, c, s0:s0 + L], g[:, :L], reads=[g])
        fw.barrier()


def phase_ffn_down(K, i, include_ctx, final):
    fw = K.fw
    xv = xT_view(K)
    gv = K.gT_d.rearrange("(c p) t -> p c t", p=128)
    gate = K.modT[i][:, 40:48, :]
    with ExitStack() as st:
        wd = fw.sb(st, [128, 22, D], BF16, "wdown")
        fw.pool.dma(wd[:, :, :], K.w["ffn_w_down"][i].rearrange("(k p) n -> p k n", p=128), writes=[wd])
        xin = [fw.sb(st, [128, 8, 512], F32, "xin") for _ in range(2)]
        gin = [fw.sb(st, [128, 22, 512], BF16, "gin") for _ in range(2)]
        if final:
            W = alloc_norm_work(K, st)
            fg = load_vecT(K, st, K.w["final_g"].rearrange("(c p) -> c p", p=128), 8, "fg")
            xn = fw.sb(st, [128, 8, 512], F32, "xn")
            otile = [fw.sb(st, [128, D], F32, "otile") for _ in range(2)]
        blks = blocks(include_ctx)

        def load(bi):
            v, t0, N, isc = blks[bi]
            fw.sp.dma(xin[bi % 2][:, :, :N], xv[:, :, t0:t0 + N], writes=[xin[bi % 2]])
            fw.sp.dma(gin[bi % 2][:, :, :N], gv[:, :, t0:t0 + N], writes=[gin[bi % 2]])

        load(0)
        oi = 0
        for bi, (v, t0, N, isc) in enumerate(blks):
            if bi + 1 < len(blks):
                load(bi + 1)
            xi, gi = xin[bi % 2], gin[bi % 2]
            for dc in range(8):
                ps = K.psum.get()
                for k in range(22):
                    fw.pe.op(lambda e: e.matmul(ps[:, :N], lhsT=wd[:, k, dc * 128:(dc + 1) * 128], rhs=gi[:, k, :N],
                                                start=(k == 0), stop=(k == 21)),
                             reads=[wd, gi], writes=[ps], inc=(k == 21))
                fw.dve.op(lambda e: e.scalar_tensor_tensor(out=xi[:, dc, :N], in0=ps[:, :N], scalar=gate[:, dc, v:v + 1],
                                                           in1=xi[:, dc, :N], op0=ALU.mult, op1=ALU.add),
                          reads=[ps, xi], writes=[xi])
            if (not final) or isc:
                fw.sp.dma(xv[:, :, t0:t0 + N], xi[:, :, :N], reads=[xi])
                continue
            fw.act.op(lambda e: e.activation(out=W.sq[:, :, :N], in_=xi[:, :, :N], func=AF.Square), reads=[xi], writes=[W.sq])
            ps = K.psum.get()
            for k in range(8):
                fw.pe.op(lambda e: e.matmul(ps[:, :N], lhsT=K.ones_bf[:, :], rhs=W.sq[:, k, :N], start=(k == 0), stop=(k == 7)),
                         reads=[W.sq, K.ones_bf], writes=[ps], inc=(k == 7))
            fw.act.op(lambda e: e.activation(out=W.rstd[:, :N], in_=ps[:, :N], func=AF.Ln, bias=K.epsc[:, 0:1], scale=1.0 / D),
                      reads=[ps], writes=[W.rstd])
            fw.act.op(lambda e: e.activation(out=W.rstd[:, :N], in_=W.rstd[:, :N], func=AF.Exp, scale=-0.5),
                      reads=[W.rstd], writes=[W.rstd])
            for k in range(8):
                fw.dve.op(lambda e: e.scalar_tensor_tensor(out=xn[:, k, :N], in0=xi[:, k, :N], scalar=fg[:, k:k + 1],
                                                           in1=W.rstd[:, :N], op0=ALU.mult, op1=ALU.mult),
                          reads=[xi, W.rstd], writes=[xn])
            b = t0 // SEG
            tl = t0 - b * SEG - CTX
            for j in range(N // 128):
                ot = otile[oi % 2]
                oi += 1
                for half in range(2):
                    ps = K.psum.get()
                    for q in range(4):
                        k = half * 4 + q
                        fw.pe.op(lambda e: e.transpose(out=ps[:, q * 128:(q + 1) * 128], in_=xn[:, k, j * 128:(j + 1) * 128],
                                                       identity=K.ident[:, :]),
                                 reads=[xn, K.ident], writes=[ps], inc=(q == 3))
                    if half == 0:
                        fw.dve.op(lambda e: e.tensor_copy(out=ot[:, 0:512], in_=ps[:, :]), reads=[ps], writes=[ot])
                    else:
                        fw.act.op(lambda e: e.activation(out=ot[:, 512:1024], in_=ps[:, :], func=AF.Identity), reads=[ps], writes=[ot])
                fw.sp.dma(K.out[b, tl + j * 128:tl + (j + 1) * 128, :], ot[:, :], reads=[ot])
        fw.barrier()


W_SHAPES = {
    "mod_w": [4, D, 6144], "mod_b": [4, 6144], "norm1_g": [4, D], "norm2_g": [4, D],
    "ffn_w_up": [4, D, 2 * FH], "ffn_conv_w": [4, 3, 2 * FH], "ffn_conv_b": [4, 2 * FH], "ffn_w_down": [4, FH, D],
    "ssm_w_in": [2, D, 6208], "ssm_conv_w": [2, 2, 4, 4096], "ssm_conv_b": [2, 2, 4096], "ssm_dt_bias": [2, 2, 32],
    "ssm_a_log": [2, 2, 32], "ssm_d": [2, 2, 32], "ssm_norm_g": [2, 2048], "ssm_w_out": [2, 2048, D],
    "attn_w_in": [1, D, 3 * D], "attn_lambda": [1, 4, 64], "attn_norm_g": [1, 128], "attn_w_out": [1, D, D],
    "conf_w_pw1": [1, D, 2 * D], "conf_b_pw1": [1, 2 * D], "conf_dw_w": [1, 31, D], "conf_dw_b": [1, D],
    "conf_ln_g": [1, D], "conf_ln_b": [1, D], "conf_w_pw2": [1, D, D], "conf_b_pw2": [1, D],
    "final_g": [D],
}
CONST_SHAPES = {"ident_f": [128, 128], "rope_cos": [128, SEQ], "rope_sin": [128, SEQ], "attn_w_perm": [D, 2 * D],
                "tri": [2, 128, 128], "maskneg": [2, 128, 128], "selA": [128, 32, 128]}


def build_program(steps, dbg=False):
    nc = bass.Bass("TRN2", target_bir_lowering=False)
    K = Ctx()
    K.nc = nc
    K.x_in = nc.dram_tensor("x", [2, SEQ, D], F32, kind="ExternalInput").ap()
    K.ctx_in = nc.dram_tensor("ctx", [2, CTX, D], F32, kind="ExternalInput").ap()
    K.c3 = nc.dram_tensor("c3", [3, D], F32, kind="ExternalInput").ap()
    K.w = {n: nc.dram_tensor(n, s, F32, kind="ExternalInput").ap() for n, s in W_SHAPES.items()}
    K.cst = {n: nc.dram_tensor(n, s, F32, kind="ExternalInput").ap() for n, s in CONST_SHAPES.items()}
    K.out = nc.dram_tensor("out", [2, SEQ, D], F32, kind="ExternalOutput").ap()
    skind = "ExternalOutput" if dbg else "Internal"
    K.xT_d = nc.dram_tensor("xT_d", [D, T], F32, kind=skind).ap()
    K.gT_d = nc.dram_tensor("gT_d", [FH, T], BF16, kind="Internal").ap()
    K.uT_d = nc.dram_tensor("uT_d", [D, T], BF16, kind="Internal").ap()
    K.qT_d = nc.dram_tensor("qT_d", [D, T], BF16, kind="Internal").ap()
    K.xbc_d = nc.dram_tensor("xbc_d", [4096, T], BF16, kind=skind).ap()
    K.z_d = nc.dram_tensor("z_d", [T, 2048], BF16, kind=skind).ap()
    K.dtla_d = nc.dram_tensor("dtla_d", [T, 128], F32, kind=skind).ap()
    K.y_d = nc.dram_tensor("y_d", [2, T, 2048], F32, kind=skind).ap()
    K.kT_d = nc.dram_tensor("kT_d", [D, T], BF16, kind="Internal").ap()
    K.oT_d = nc.dram_tensor("oT_d", [D, T], BF16, kind="Internal").ap()
    K.v_d = nc.dram_tensor("v_d", [T, D], BF16, kind="Internal").ap()
    layers = sorted({i for (_, i) in steps})
    with ExitStack() as st:
        K.st = st
        fw = K.fw = FW(nc, st)
        st.enter_context(nc.Block())
        K.psum = PsumPool([fw.ps(st, [128, 512], F32, f"bank{j}") for j in range(8)])
        K.ident = fw.sb(st, [128, 128], F32, "ident")
        K.ident_bf = fw.sb(st, [128, 128], BF16, "identb")
        K.ones_bf = fw.sb(st, [128, 128], BF16, "onesb")
        K.epsc = fw.sb(st, [128, 1], F32, "epsc")
        fw.sp.dma(K.ident[:, :], K.cst["ident_f"], writes=[K.ident])
        fw.dve.op(lambda e: e.tensor_copy(out=K.ident_bf[:, :], in_=K.ident[:, :]), reads=[K.ident], writes=[K.ident_bf])
        fw.pool.op(lambda e: e.memset(K.ones_bf[:, :], 1.0), writes=[K.ones_bf])
        fw.pool.op(lambda e: e.memset(K.epsc[:, :], EPS), writes=[K.epsc])
        K.onec = fw.sb(st, [128, 1], F32, "onec")
        fw.pool.op(lambda e: e.memset(K.onec[:, :], 1.0), writes=[K.onec])
        fw.barrier()
        phase_init(K)
        phase_mod(K, layers)
        last_ffn = max([n for n, (kind, _) in enumerate(steps) if kind == "ffn"], default=-1)
        K.marks = [("start", 0), ("init", 0)]
        K.marks.append(("mod_done", fw.pe.n_ins))
        for n, (kind, i) in enumerate(steps):
            need_ctx = i < 3
            K.marks.append((f"{kind}{i}", fw.pe.n_ins))
            if kind == "mix":
                MIXERS[i % 3](K, i, need_ctx)
            else:
                phase_ffn_up(K, i, need_ctx)
                K.marks.append((f"  FD{i}", fw.pe.n_ins))
                phase_ffn_down(K, i, need_ctx, final=(n == last_ffn))
        fw.barrier()
        K.marks.append(("end", fw.pe.n_ins))
        K.n_ins = {e.name: e.n_ins for e in fw.engs}
    return nc, K


def mixer_todo(K, i, need_ctx):
    raise NotImplementedError


MIXERS = {0: mixer_todo, 1: mixer_todo, 2: mixer_todo}


ROPE_PERM64 = np.concatenate([np.arange(16, 32), np.arange(0, 16), np.arange(48, 64), np.arange(32, 48)])


def host_consts():
    t = np.arange(SEQ)
    row, col = (t // 64).astype(np.float32), (t % 64).astype(np.float32)
    inv = (10000.0 ** (-np.arange(16, dtype=np.float32) / 16)).astype(np.float32)
    ang = np.stack([row[None, :] * inv[:, None], col[None, :] * inv[:, None]], axis=0)
    cos64 = np.zeros((64, SEQ), np.float32)
    sin64 = np.zeros((64, SEQ), np.float32)
    for ax in range(2):
        c, s = np.cos(ang[ax]), np.sin(ang[ax])
        cos64[ax * 32:ax * 32 + 16] = c
        cos64[ax * 32 + 16:ax * 32 + 32] = c
        sin64[ax * 32:ax * 32 + 16] = -s
        sin64[ax * 32 + 16:ax * 32 + 32] = s
    r = np.arange(128)
    tri = np.stack([(r[:, None] <= r[None, :]), (r[:, None] >= r[None, :])]).astype(np.float32)
    mneg = np.stack([np.where(r[None, :] < r[:, None], NEG, 0.0), np.where(r[None, :] > r[:, None], NEG, 0.0)]).astype(np.float32)
    selA = np.zeros((128, 32, 128), np.float32)
    for h in range(32):
        selA[h, h, :] = 1.0
    return {"tri": tri, "maskneg": mneg, "selA": selA, "ident_f": np.eye(128, dtype=np.float32),
            "rope_cos": np.ascontiguousarray(np.tile(cos64, (2, 1))), "rope_sin": np.ascontiguousarray(np.tile(sin64, (2, 1)))}


FULL_STEPS = [("mix", 0), ("ffn", 0), ("mix", 1), ("ffn", 1), ("mix", 2), ("ffn", 2), ("mix", 3), ("ffn", 3)]


def make_in_maps(inputs, ncores=NCORES):
    cst = host_consts()
    wa = np.asarray(inputs["attn_w_in"][0], dtype=np.float32)
    perm = (np.arange(2 * D) // 64) * 64 + ROPE_PERM64[np.arange(2 * D) % 64]
    cst["attn_w_perm"] = np.ascontiguousarray(wa[:, perm])
    maps = []
    wts = {n: np.ascontiguousarray(inputs[n], dtype=np.float32) for n in W_SHAPES}
    for cidx in range(ncores):
        b0 = 2 * cidx
        m = dict(wts)
        m.update(cst)
        m["x"] = np.ascontiguousarray(inputs["x"][b0:b0 + 2])
        m["ctx"] = np.ascontiguousarray(inputs["ctx"][b0:b0 + 2])
        m["c3"] = np.ascontiguousarray(np.concatenate([inputs["c"][b0:b0 + 2], inputs["c_ctx"][None, :]], axis=0))
        maps.append(m)
    return maps


_CACHE = {}


def kernel(**inputs):
    if "nc" not in _CACHE:
        _CACHE["nc"] = build_program(FULL_STEPS)[0]
    nc = _CACHE["nc"]
    maps = make_in_maps(inputs)
    res = run_bass_kernel_spmd(nc, maps, core_ids=list(range(NCORES)))
    return np.concatenate([r["out"] for r in res.results], axis=0).astype(np.float32)


def load_rowsT(K, cm, mat_ap, nrows, name="rT"):
    fw = K.fw
    dst = fw.sb(cm, [128, nrows], F32, name)
    with ExitStack() as st:
        tmp = fw.sb(st, [128, 128], F32, "vrow")
        for r0 in range(0, nrows, 128):
            n = min(128, nrows - r0)
            fw.sp.dma(tmp[0:n, :], mat_ap[r0:r0 + n, :], writes=[tmp])
            ps = K.psum.get()
            fw.pe.op(lambda e: e.transpose(out=ps[:, 0:n], in_=tmp[0:n, :], identity=K.ident[0:n, 0:n]),
                     reads=[tmp, K.ident], writes=[ps])
            fw.dve.op(lambda e: e.tensor_copy(out=dst[:, r0:r0 + n], in_=ps[:, 0:n]), reads=[ps], writes=[dst])
        fw.barrier()
    return dst


def vecT(K, cm, vec_ap, name="vT"):
    n = vec_ap.shape[0] // 128
    return load_rowsT(K, cm, vec_ap.rearrange("(c p) -> c p", p=128), n, name)


def mixer_conf(K, i, need_ctx):
    fw = K.fw
    xv = xT_view(K)
    uv = K.uT_d.rearrange("(c p) t -> p c t", p=128)
    blks = blocks(need_ctx)
    with ExitStack() as st:
        w1 = fw.sb(st, [128, 8, 2048], BF16, "w1")
        fw.pool.dma(w1[:, :, :], K.w["conf_w_pw1"][0].rearrange("(k p) n -> p k n", p=128), writes=[w1])
        b1 = vecT(K, st, K.w["conf_b_pw1"][0], "b1")
        W = alloc_norm_work(K, st)
        xin = [fw.sb(st, [128, 8, 512], F32, "xin") for _ in range(2)]
        hT = [fw.sb(st, [128, 8, 512], BF16, "hT") for _ in range(2)]
        sig = [fw.sb(st, [128, 512], F32, "sig") for _ in range(2)]
        ust = [fw.sb(st, [128, 8, 512], BF16, "ust") for _ in range(2)]
        def xload(bi):
            _, tn, Nn, _ = blks[bi]
            fw.sp.dma(xin[bi % 2][:, :, :Nn], xv[:, :, tn:tn + Nn], writes=[xin[bi % 2]])

        def nrm(bi):
            v_, _, N_, _ = blks[bi]
            norm_block(K, W, xin[bi % 2], N_, K.A1[i], K.modT[i][:, 0:8, :], v_, hT[bi % 2])

        xload(0)
        xload(1)
        nrm(0)
        for bi, (v, t0, N, isc) in enumerate(blks):
            if bi + 1 < len(blks):
                nrm(bi + 1)
            if bi + 2 < len(blks):
                xload(bi + 2)
            h, us = hT[bi % 2], ust[bi % 2]
            for c in range(8):
                ps1, ps2 = K.psum.get(), K.psum.get()
                for (ps, cc) in ((ps1, c), (ps2, c + 8)):
                    for k in range(8):
                        fw.pe.op(lambda e: e.matmul(ps[:, :N], lhsT=w1[:, k, cc * 128:(cc + 1) * 128], rhs=h[:, k, :N],
                                                    start=(k == 0), stop=(k == 7)),
                                 reads=[w1, h], writes=[ps], inc=(k == 7))
                sg = sig[c % 2]
                fw.act.op(lambda e: e.activation(out=sg[:, :N], in_=ps2[:, :N], func=AF.Sigmoid, bias=b1[:, c + 8:c + 9], scale=1.0),
                          reads=[ps2], writes=[sg])
                fw.dve.op(lambda e: e.scalar_tensor_tensor(out=us[:, c, :N], in0=ps1[:, :N], scalar=b1[:, c:c + 1], in1=sg[:, :N],
                                                           op0=ALU.add, op1=ALU.mult),
                          reads=[ps1, sg], writes=[us])
            fw.sp.dma(uv[:, :, t0:t0 + N], us[:, :, :N], reads=[us])
        fw.barrier()
    K.marks.append((f"  CB{i}", fw.pe.n_ins))
    with ExitStack() as st:
        dwT = load_rowsT(K, st, K.w["conf_dw_w"][0].rearrange("k (c p) -> (k c) p", p=128), 248, "dwT")
        dwb = vecT(K, st, K.w["conf_dw_b"][0], "dwb")
        lng = vecT(K, st, K.w["conf_ln_g"][0], "lng")
        lnb = vecT(K, st, K.w["conf_ln_b"][0], "lnb")
        b2 = vecT(K, st, K.w["conf_b_pw2"][0], "b2")
        gate = K.modT[i][:, 16:24, :]
        gb = fw.sb(st, [128, 8, 3], F32, "gb")
        fw.dve.op(lambda e: e.tensor_tensor(out=gb[:, :, :], in0=gate, in1=b2[:, :].unsqueeze(2).to_broadcast([128, 8, 3]), op=ALU.mult),
                  reads=[b2], writes=[gb])
        diag = fw.sb(st, [128, 8, 31, 128], BF16, "diag")
        n = 0
        for c in range(8):
            for kk in range(31):
                eng = fw.dve if n % 2 == 0 else fw.pool
                n += 1
                col = kk * 8 + c
                eng.op(lambda e: e.tensor_scalar(out=diag[:, c, kk, :], in0=K.ident[:, :], scalar1=dwT[:, col:col + 1], scalar2=None,
                                                 op0=ALU.mult),
                       reads=[dwT, K.ident], writes=[diag])
        w2 = fw.sb(st, [128, 8, D], BF16, "w2")
        fw.pool.dma(w2[:, :, :], K.w["conf_w_pw2"][0].rearrange("(k p) n -> p k n", p=128), writes=[w2])
        xin = [fw.sb(st, [128, 8, 512], F32, "xin") for _ in range(2)]
        uin = [fw.sb(st, [128, 8, 542], BF16, "uin") for _ in range(2)]
        vt = fw.sb(st, [128, 8, 512], F32, "vt")
        vb = fw.sb(st, [128, 8, 512], BF16, "vb")
        sq = fw.sb(st, [128, 8, 512], BF16, "sq")
        sT = fw.sb(st, [128, 8, 512], BF16, "sT")
        mean = fw.sb(st, [128, 512], F32, "mean")
        msq = fw.sb(st, [128, 512], F32, "msq")
        rstd = fw.sb(st, [128, 512], F32, "rstd")
        tmp = [fw.sb(st, [128, 512], F32, "ctmp") for _ in range(2)]

        def load(bi):
            v, t0, N, isc = blks[bi]
            s0 = (t0 // SEG) * SEG + (0 if isc else CTX)
            s1 = s0 + (CTX if isc else SEQ)
            lo, hi = max(t0 - 15, s0), min(t0 + N + 15, s1)
            ui = uin[bi % 2]
            if lo != t0 - 15 or hi != t0 + N + 15:
                fw.pool.op(lambda e: e.memset(ui[:, :, :], 0.0), writes=[ui])
            fw.sp.dma(ui[:, :, lo - (t0 - 15):hi - (t0 - 15)], uv[:, :, lo:hi], writes=[ui])
            fw.sp.dma(xin[bi % 2][:, :, :N], xv[:, :, t0:t0 + N], writes=[xin[bi % 2]])

        load(0)
        for bi, (v, t0, N, isc) in enumerate(blks):
            if bi + 1 < len(blks):
                load(bi + 1)
            xi, ui = xin[bi % 2], uin[bi % 2]
            for c in range(8):
                ps = K.psum.get()
                for kk in range(31):
                    fw.pe.op(lambda e: e.matmul(ps[:, :N], lhsT=diag[:, c, kk, :], rhs=ui[:, c, kk:kk + N], start=(kk == 0), stop=(kk == 30)),
                             reads=[diag, ui], writes=[ps], inc=(kk == 30))
                fw.act.op(lambda e: e.activation(out=vt[:, c, :N], in_=ps[:, :N], func=AF.Identity, bias=dwb[:, c:c + 1], scale=1.0),
                          reads=[ps], writes=[vt])
            fw.pool.op(lambda e: e.tensor_copy(out=vb[:, :, :N], in_=vt[:, :, :N]), reads=[vt], writes=[vb])
            fw.act.op(lambda e: e.activation(out=sq[:, :, :N], in_=vt[:, :, :N], func=AF.Square), reads=[vt], writes=[sq])
            p1, p2 = K.psum.get(), K.psum.get()
            for (ps, src) in ((p1, vb), (p2, sq)):
                for k in range(8):
                    fw.pe.op(lambda e: e.matmul(ps[:, :N], lhsT=K.ones_bf[:, :], rhs=src[:, k, :N], start=(k == 0), stop=(k == 7)),
                             reads=[src, K.ones_bf], writes=[ps], inc=(k == 7))
            fw.dve.op(lambda e: e.tensor_scalar(out=mean[:, :N], in0=p1[:, :N], scalar1=1.0 / D, scalar2=None, op0=ALU.mult),
                      reads=[p1], writes=[mean])
            fw.dve.op(lambda e: e.tensor_tensor(out=msq[:, :N], in0=mean[:, :N], in1=mean[:, :N], op=ALU.mult), reads=[mean], writes=[msq])
            fw.dve.op(lambda e: e.scalar_tensor_tensor(out=rstd[:, :N], in0=p2[:, :N], scalar=1.0 / D, in1=msq[:, :N],
                                                       op0=ALU.mult, op1=ALU.subtract),
                      reads=[p2, msq], writes=[rstd])
            fw.act.op(lambda e: e.activation(out=rstd[:, :N], in_=rstd[:, :N], func=AF.Ln, bias=K.epsc[:, 0:1], scale=1.0),
                      reads=[rstd], writes=[rstd])
            fw.act.op(lambda e: e.activation(out=rstd[:, :N], in_=rstd[:, :N], func=AF.Exp, scale=-0.5), reads=[rstd], writes=[rstd])
            for k in range(8):
                ta, tb = tmp[0], tmp[1]
                fw.pool.op(lambda e: e.tensor_tensor(out=ta[:, :N], in0=vt[:, k, :N], in1=mean[:, :N], op=ALU.subtract),
                           reads=[vt, mean], writes=[ta])
                fw.dve.op(lambda e: e.scalar_tensor_tensor(out=tb[:, :N], in0=ta[:, :N], scalar=lng[:, k:k + 1], in1=rstd[:, :N],
                                                           op0=ALU.mult, op1=ALU.mult),
                          reads=[ta, rstd], writes=[tb])
                fw.act.op(lambda e: e.activation(out=sT[:, k, :N], in_=tb[:, :N], func=AF.Silu, bias=lnb[:, k:k + 1], scale=1.0),
                          reads=[tb], writes=[sT])
            for dc in range(8):
                ps = K.psum.get()
                for k in range(8):
                    fw.pe.op(lambda e: e.matmul(ps[:, :N], lhsT=w2[:, k, dc * 128:(dc + 1) * 128], rhs=sT[:, k, :N], start=(k == 0), stop=(k == 7)),
                             reads=[w2, sT], writes=[ps], inc=(k == 7))
                fw.dve.op(lambda e: e.scalar_tensor_tensor(out=xi[:, dc, :N], in0=ps[:, :N], scalar=gate[:, dc, v:v + 1], in1=xi[:, dc, :N],
                                                           op0=ALU.mult, op1=ALU.add),
                          reads=[ps, xi], writes=[xi])
                fw.pool.op(lambda e: e.tensor_scalar(out=xi[:, dc, :N], in0=xi[:, dc, :N], scalar1=gb[:, dc, v:v + 1], scalar2=None, op0=ALU.add),
                           reads=[xi, gb], writes=[xi])
            fw.sp.dma(xv[:, :, t0:t0 + N], xi[:, :, :N], reads=[xi])
        fw.barrier()


MIXERS[2] = mixer_conf


def mixer_attn(K, i, need_ctx):
    fw = K.fw
    xv = xT_view(K)
    qv = K.qT_d.rearrange("(c p) t -> p c t", p=128)
    kv = K.kT_d.rearrange("(c p) t -> p c t", p=128)
    ov = K.oT_d.rearrange("(c p) t -> p c t", p=128)
    vv = K.v_d.rearrange("(s p) d -> p s d", p=128)
    lam_init = 0.8 - 0.6 * math.exp(-0.3 * i)
    with ExitStack() as st:
        wq = fw.sb(st, [128, 8, 3072], BF16, "wq")
        wp = fw.sb(st, [128, 8, 2048], BF16, "wp")
        fw.pool.dma(wq[:, :, :], K.w["attn_w_in"][0].rearrange("(k p) n -> p k n", p=128), writes=[wq])
        fw.pool.dma(wp[:, :, :], K.cst["attn_w_perm"].rearrange("(k p) n -> p k n", p=128), writes=[wp])
        cosT = fw.sb(st, [128, SEQ], F32, "cosT")
        sinT = fw.sb(st, [128, SEQ], F32, "sinT")
        fw.sp.dma(cosT[:, :], K.cst["rope_cos"], writes=[cosT])
        fw.sp.dma(sinT[:, :], K.cst["rope_sin"], writes=[sinT])
        W = alloc_norm_work(K, st)
        xin = [fw.sb(st, [128, 8, 512], F32, "xin") for _ in range(2)]
        hTs = [fw.sb(st, [128, 8, 512], BF16, "hT") for _ in range(2)]
        qst = fw.sb(st, [128, 8, 512], BF16, "qst")
        kst = fw.sb(st, [128, 8, 512], BF16, "kst")
        vst = fw.sb(st, [128, 4, D], BF16, "vst")
        t1 = [fw.sb(st, [128, 512], F32, "rt1") for _ in range(2)]
        t2 = [fw.sb(st, [128, 512], F32, "rt2") for _ in range(2)]
        blks = blocks(True)

        def xload(bi):
            _, tn, Nn, _ = blks[bi]
            fw.sp.dma(xin[bi % 2][:, :, :Nn], xv[:, :, tn:tn + Nn], writes=[xin[bi % 2]])

        def nrm(bi):
            v_, _, N_, _ = blks[bi]
            norm_block(K, W, xin[bi % 2], N_, K.A1[i], K.modT[i][:, 0:8, :], v_, hTs[bi % 2])

        xload(0)
        xload(1)
        nrm(0)
        n = 0
        for bi, (v, t0, N, isc) in enumerate(blks):
            hT = hTs[bi % 2]
            if bi + 1 < len(blks):
                nrm(bi + 1)
            if bi + 2 < len(blks):
                xload(bi + 2)
            tl = t0 - (t0 // SEG) * SEG - CTX
            for c in range(8):
                for (cb, stg) in ((0, qst), (1024, kst)):
                    psA = K.psum.get()
                    for k in range(8):
                        fw.pe.op(lambda e: e.matmul(psA[:, :N], lhsT=wq[:, k, cb + c * 128:cb + (c + 1) * 128], rhs=hT[:, k, :N],
                                                    start=(k == 0), stop=(k == 7)),
                                 reads=[wq, hT], writes=[psA], inc=(k == 7))
                    if isc:
                        fw.act.op(lambda e: e.activation(out=stg[:, c, :N], in_=psA[:, :N], func=AF.Identity), reads=[psA], writes=[stg])
                        continue
                    psB = K.psum.get()
                    for k in range(8):
                        fw.pe.op(lambda e: e.matmul(psB[:, :N], lhsT=wp[:, k, cb + c * 128:cb + (c + 1) * 128], rhs=hT[:, k, :N],
                                                    start=(k == 0), stop=(k == 7)),
                                 reads=[wp, hT], writes=[psB], inc=(k == 7))
                    a, b_ = t1[n % 2], t2[n % 2]
                    n += 1
                    fw.dve.op(lambda e: e.tensor_tensor(out=a[:, :N], in0=psA[:, :N], in1=cosT[:, tl:tl + N], op=ALU.mult),
                              reads=[psA, cosT], writes=[a])
                    fw.dve.op(lambda e: e.tensor_tensor(out=b_[:, :N], in0=psB[:, :N], in1=sinT[:, tl:tl + N], op=ALU.mult),
                              reads=[psB, sinT], writes=[b_])
                    fw.pool.op(lambda e: e.tensor_tensor(out=stg[:, c, :N], in0=a[:, :N], in1=b_[:, :N], op=ALU.add),
                               reads=[a, b_], writes=[stg])
            for sub in range(N // 128):
                for half in range(2):
                    ps = K.psum.get()
                    for k in range(8):
                        fw.pe.op(lambda e: e.matmul(ps[:, :], lhsT=hT[:, k, sub * 128:(sub + 1) * 128],
                                                    rhs=wq[:, k, 2048 + half * 512:2048 + (half + 1) * 512], start=(k == 0), stop=(k == 7)),
                                 reads=[wq, hT], writes=[ps], inc=(k == 7))
                    fw.act.op(lambda e: e.activation(out=vst[:, sub, half * 512:(half + 1) * 512], in_=ps[:, :], func=AF.Identity),
                              reads=[ps], writes=[vst])
            fw.sp.dma(qv[:, :, t0:t0 + N], qst[:, :, :N], reads=[qst])
            fw.sp.dma(kv[:, :, t0:t0 + N], kst[:, :, :N], reads=[kst])
            fw.sp.dma(vv[:, t0 // 128:(t0 + N) // 128, :], vst[:, :N // 128, :], reads=[vst])
        fw.barrier()
    K.marks.append((f"  AB{i}", fw.pe.n_ins))
    with ExitStack() as st:
        hsel = [fw.sb(st, [128, 128], BF16, "hsel") for _ in range(2)]
        for e_ in range(2):
            fw.pool.op(lambda e: e.memset(hsel[e_][:, :], 0.0), writes=[hsel[e_]])
            fw.pool.op(lambda e: e.memset(hsel[e_][e_ * 64:(e_ + 1) * 64, :], 1.0), writes=[hsel[e_]])
        lp = fw.sb(st, [1, 256], F32, "lp")
        lpp = fw.sb(st, [1, 128], F32, "lpp")
        ls = fw.sb(st, [1, 4], F32, "ls")
        ones1 = fw.sb(st, [1, 128], F32, "ones1")
        neglam = fw.sb(st, [128, 1], F32, "neglam")
        gbc = fw.sb(st, [128, 128], F32, "gbc")
        fw.sp.dma(lp[:, :], K.w["attn_lambda"][0].rearrange("(o a) d -> o (a d)", o=1), writes=[lp])
        fw.sp.dma(gbc[:, :], K.w["attn_norm_g"][0].partition_broadcast(128), writes=[gbc])
        fw.pool.op(lambda e: e.memset(ones1[:, :], 1.0), writes=[ones1])
        fw.dve.op(lambda e: e.tensor_tensor(out=lpp[:, :].rearrange("o (a d) -> o a d", a=2),
                                            in0=lp[:, :].rearrange("o (a b d) -> o a b d", a=2, b=2)[:, :, 0, :],
                                            in1=lp[:, :].rearrange("o (a b d) -> o a b d", a=2, b=2)[:, :, 1, :], op=ALU.mult),
                  reads=[lp], writes=[lpp])
        fw.dve.op(lambda e: e.tensor_reduce(out=ls[:, 0:2], in_=lpp[:, :].rearrange("o (a d) -> o a d", a=2), axis=AX.X, op=ALU.add),
                  reads=[lpp], writes=[ls])
        fw.act.op(lambda e: e.activation(out=ls[:, 0:2], in_=ls[:, 0:2], func=AF.Exp), reads=[ls], writes=[ls])
        fw.dve.op(lambda e: e.tensor_tensor(out=ls[:, 2:3], in0=ls[:, 1:2], in1=ls[:, 0:1], op=ALU.subtract), reads=[ls], writes=[ls])
        fw.dve.op(lambda e: e.tensor_scalar(out=ls[:, 3:4], in0=ls[:, 2:3], scalar1=-lam_init, scalar2=None, op0=ALU.add),
                  reads=[ls], writes=[ls])
        ps = K.psum.get()
        fw.pe.op(lambda e: e.matmul(ps[:, 0:1], lhsT=ones1[:, :], rhs=ls[:, 3:4], start=True, stop=True), reads=[ones1, ls], writes=[ps])
        fw.dve.op(lambda e: e.tensor_copy(out=neglam[:, :], in_=ps[:, 0:1]), reads=[ps], writes=[neglam])
        fw.act.op(lambda e: e.activation(out=gbc[:, :], in_=gbc[:, :], func=AF.Identity, scale=(1.0 - lam_init)), reads=[gbc], writes=[gbc])

        kz = [[fw.sb(st, [128, SEG], BF16, "kz") for _ in range(2)] for _ in range(2)]
        qh = [fw.sb(st, [128, SEG], BF16, "qh") for _ in range(2)]
        va = [fw.sb(st, [128, 18, 129], BF16, "va") for _ in range(2)]
        for p_ in range(2):
            for e_ in range(2):
                fw.pool.op(lambda e: e.memset(kz[p_][e_][:, :], 0.0), writes=[kz[p_][e_]])
            fw.pool.op(lambda e: e.memset(va[p_][:, :, 128:129], 1.0), writes=[va[p_]])
        sqt = fw.sb(st, [128, SEG], BF16, "sqt")
        mx = fw.sb(st, [128, 4, 5], F32, "mx")
        mm = fw.sb(st, [128, 4], F32, "mm")
        negb = fw.sb(st, [128, 2], F32, "negb")
        E = [[fw.sb(st, [128, 18, 512], BF16, "E") for _ in range(2)] for _ in range(2)]
        rss = [fw.sb(st, [128, 2], F32, "rs") for _ in range(2)]
        tts = [fw.sb(st, [128, 128], F32, "tt") for _ in range(2)]
        os_ = [fw.sb(st, [128, 128], F32, "o") for _ in range(2)]
        junk = fw.sb(st, [128, 128], F32, "junk")
        sss = [fw.sb(st, [128, 1], F32, "ss") for _ in range(2)]
        ons = [fw.sb(st, [128, 128], BF16, "on") for _ in range(2)]
        oTst = [fw.sb(st, [128, 512], BF16, "oTst") for _ in range(3)]
        pend = []
        pp = [0]
        cblk = [(0, 512), (512, 512), (1024, 512), (1536, 512), (2048, 256)]

        def loads(it):
            b, hd = it // 8, it % 8
            p_ = it % 2
            s0 = b * SEG
            fw.sp.dma(kz[p_][0][0:64, :], kv[0:64, hd, s0:s0 + SEG], writes=[kz[p_][0]])
            fw.sp.dma(kz[p_][1][64:128, :], kv[64:128, hd, s0:s0 + SEG], writes=[kz[p_][1]])
            fw.sp.dma(qh[p_][:, :], qv[:, hd, s0:s0 + SEG], writes=[qh[p_]])
            fw.sp.dma(va[p_][:, :, 0:128], vv[:, s0 // 128:s0 // 128 + 18, hd * 128:(hd + 1) * 128], writes=[va[p_]])

        loads(0)
        oi = 0
        for it in range(16):
            if it + 1 < 16:
                loads(it + 1)
            b, hd = it // 8, it % 8
            p_ = it % 2
            s0 = b * SEG
            kz0, kz1, q_, v_ = kz[p_][0], kz[p_][1], qh[p_], va[p_]
            for (src, lh, col0) in ((q_, hsel, 0), (kz0, None, 2), (kz1, None, 3)):
                fw.act.op(lambda e: e.activation(out=sqt[:, :], in_=src[:, :], func=AF.Square), reads=[src], writes=[sqt])
                for e_ in (range(2) if lh is not None else range(1)):
                    for bj, (c0, cn) in enumerate(cblk):
                        ps = K.psum.get()
                        lhs = lh[e_] if lh is not None else K.ones_bf
                        fw.pe.op(lambda e: e.matmul(ps[:, :cn], lhsT=lhs[:, :], rhs=sqt[:, c0:c0 + cn], start=True, stop=True),
                                 reads=[lhs, sqt], writes=[ps])
                        fw.dve.op(lambda e: e.tensor_reduce(out=mx[:, col0 + e_, bj:bj + 1], in_=ps[:, :cn], axis=AX.X, op=ALU.max),
                                  reads=[ps], writes=[mx])
            fw.dve.op(lambda e: e.tensor_reduce(out=mm[:, :], in_=mx[:, :, :], axis=AX.X, op=ALU.max), reads=[mx], writes=[mm])
            fw.dve.op(lambda e: e.tensor_tensor(out=negb[:, :], in0=mm[:, 0:2], in1=mm[:, 2:4], op=ALU.mult), reads=[mm], writes=[negb])
            fw.act.op(lambda e: e.activation(out=negb[:, :], in_=negb[:, :], func=AF.Sqrt), reads=[negb], writes=[negb])
            fw.dve.op(lambda e: e.tensor_scalar(out=negb[:, :], in0=negb[:, :], scalar1=-0.125 * 1.02, scalar2=None, op0=ALU.mult),
                      reads=[negb], writes=[negb])
            qbs = ([(0, CTX, 2)] if need_ctx else []) + [(CTX + 512 * n_, 512, 18) for n_ in range(4)]

            def stage1(qi):
                q0, nq, nkt = qbs[qi]
                for e_ in range(2):
                    kz_e = kz0 if e_ == 0 else kz1
                    Eb = E[qi % 2][e_]
                    for j in range(nkt):
                        ps = K.psum.get()
                        fw.pe.op(lambda e: e.matmul(ps[:, :nq], lhsT=kz_e[:, j * 128:(j + 1) * 128], rhs=q_[:, q0:q0 + nq], start=True, stop=True),
                                 reads=[kz_e, q_], writes=[ps])
                        fw.act.op(lambda e: e.activation(out=Eb[:, j, :nq], in_=ps[:, :nq], func=AF.Exp, bias=negb[:, e_:e_ + 1], scale=0.125),
                                  reads=[ps, negb], writes=[Eb])

            def stage2(qi):
                nonlocal oi
                q0, nq, nkt = qbs[qi]
                ost = oTst[oi % 3]
                oi += 1
                ntile = nq // 128
                for i_ in range(ntile):
                    par = pp[0] % 2
                    pp[0] += 1
                    rs, tt, o_, ss, on = rss[par], tts[par], os_[par], sss[par], ons[par]
                    ps = K.psum.get()
                    for e_ in range(2):
                        Eb = E[qi % 2][e_]
                        for j in range(nkt):
                            fw.pe.op(lambda e: e.matmul(ps[:, e_ * 129:(e_ + 1) * 129], lhsT=Eb[:, j, i_ * 128:(i_ + 1) * 128], rhs=v_[:, j, :],
                                                        start=(j == 0), stop=(j == nkt - 1)),
                                     reads=[Eb, v_], writes=[ps], inc=(j == nkt - 1))
                    if pend:
                        pend.pop()()
                    fw.dve.op(lambda e: e.reciprocal(out=rs[:, 0:2], in_=ps[:, 128:258:129]), reads=[ps], writes=[rs])
                    fw.dve.op(lambda e: e.tensor_scalar(out=tt[:, :], in0=ps[:, 129:257], scalar1=rs[:, 1:2], scalar2=neglam[:, 0:1],
                                                        op0=ALU.mult, op1=ALU.mult),
                              reads=[ps, rs, neglam], writes=[tt])
                    fw.dve.op(lambda e: e.scalar_tensor_tensor(out=o_[:, :], in0=ps[:, 0:128], scalar=rs[:, 0:1], in1=tt[:, :],
                                                               op0=ALU.mult, op1=ALU.add),
                              reads=[ps, rs, tt], writes=[o_])
                    fw.act.op(lambda e: e.activation(out=junk[:, :], in_=o_[:, :], func=AF.Square, accum_out=ss[:, 0:1]),
                              reads=[o_], writes=[junk, ss])
                    fw.act.op(lambda e: e.activation(out=ss[:, :], in_=ss[:, :], func=AF.Ln, bias=K.epsc[:, 0:1], scale=1.0 / 128),
                              reads=[ss], writes=[ss])
                    fw.act.op(lambda e: e.activation(out=ss[:, :], in_=ss[:, :], func=AF.Exp, scale=-0.5), reads=[ss], writes=[ss])
                    fw.dve.op(lambda e: e.scalar_tensor_tensor(out=on[:, :], in0=o_[:, :], scalar=ss[:, 0:1], in1=gbc[:, :],
                                                               op0=ALU.mult, op1=ALU.mult),
                              reads=[o_, ss, gbc], writes=[on])

                    def fin(on=on, ost=ost, i_=i_, last=(i_ == ntile - 1), q0=q0, nq=nq, hd=hd, s0=s0):
                        ps2 = K.psum.get()
                        pv = ps2[:, :].bitcast(BF16)
                        fw.pe.op(lambda e: e.transpose(out=pv[:, 0:128], in_=on[:, :], identity=K.ident_bf[:, :]),
                                 reads=[on, K.ident_bf], writes=[ps2])
                        fw.act.op(lambda e: e.activation(out=ost[:, i_ * 128:(i_ + 1) * 128], in_=pv[:, 0:128], func=AF.Identity),
                                  reads=[ps2], writes=[ost])
                        if last:
                            fw.sp.dma(ov[:, hd, s0 + q0:s0 + q0 + nq], ost[:, :nq], reads=[ost])
                    pend.append(fin)

            stage1(0)
            for qi in range(len(qbs)):
                if qi + 1 < len(qbs):
                    stage1(qi + 1)
                stage2(qi)
        while pend:
            pend.pop()()
        fw.barrier()
    K.marks.append((f"  AC{i}", fw.pe.n_ins))
    with ExitStack() as st:
        wo = fw.sb(st, [128, 8, D], BF16, "wo")
        fw.pool.dma(wo[:, :, :], K.w["attn_w_out"][0].rearrange("(k p) n -> p k n", p=128), writes=[wo])
        xin = [fw.sb(st, [128, 8, 512], F32, "xin") for _ in range(2)]
        oin = [fw.sb(st, [128, 8, 512], BF16, "oin") for _ in range(2)]
        gate = K.modT[i][:, 16:24, :]
        blks = blocks(need_ctx)

        def load(bi):
            v, t0, N, isc = blks[bi]
            fw.sp.dma(xin[bi % 2][:, :, :N], xv[:, :, t0:t0 + N], writes=[xin[bi % 2]])
            fw.sp.dma(oin[bi % 2][:, :, :N], ov[:, :, t0:t0 + N], writes=[oin[bi % 2]])

        load(0)
        for bi, (v, t0, N, isc) in enumerate(blks):
            if bi + 1 < len(blks):
                load(bi + 1)
            xi, oi_ = xin[bi % 2], oin[bi % 2]
            for dc in range(8):
                ps = K.psum.get()
                for k in range(8):
                    fw.pe.op(lambda e: e.matmul(ps[:, :N], lhsT=wo[:, k, dc * 128:(dc + 1) * 128], rhs=oi_[:, k, :N], start=(k == 0), stop=(k == 7)),
                             reads=[wo, oi_], writes=[ps], inc=(k == 7))
                fw.dve.op(lambda e: e.scalar_tensor_tensor(out=xi[:, dc, :N], in0=ps[:, :N], scalar=gate[:, dc, v:v + 1], in1=xi[:, dc, :N],
                                                           op0=ALU.mult, op1=ALU.add),
                          reads=[ps, xi], writes=[xi])
            fw.sp.dma(xv[:, :, t0:t0 + N], xi[:, :, :N], reads=[xi])
        fw.barrier()


MIXERS[1] = mixer_attn


def mixer_mamba(K, i, need_ctx):
    fw = K.fw
    j = i // 3
    xv = xT_view(K)
    xbcv = K.xbc_d.rearrange("(c p) t -> p c t", p=128)
    zv = K.z_d
    with ExitStack() as st:
        w = fw.sb(st, [128, 8, 6208], BF16, "win")
        fw.pool.dma(w[:, :, :], K.w["ssm_w_in"][j].rearrange("(k p) n -> p k n", p=128), writes=[w])
        dtb = fw.sb(st, [128, 64], F32, "dtb")
        aneg = fw.sb(st, [128, 64], F32, "aneg")
        fw.sp.dma(dtb[:, :], K.w["ssm_dt_bias"][j].rearrange("a h -> (a h)").partition_broadcast(128), writes=[dtb])
        fw.sp.dma(aneg[:, :], K.w["ssm_a_log"][j].rearrange("a h -> (a h)").partition_broadcast(128), writes=[aneg])
        fw.act.op(lambda e: e.activation(out=aneg[:, :], in_=aneg[:, :], func=AF.Exp), reads=[aneg], writes=[aneg])
        fw.dve.op(lambda e: e.tensor_scalar(out=aneg[:, :], in0=aneg[:, :], scalar1=-1.0, scalar2=None, op0=ALU.mult), reads=[aneg], writes=[aneg])
        W = alloc_norm_work(K, st)
        xin = [fw.sb(st, [128, 8, 512], F32, "xin") for _ in range(2)]
        hTs = [fw.sb(st, [128, 8, 512], BF16, "hT") for _ in range(2)]
        xst = [fw.sb(st, [128, 4, 512], BF16, "xst") for _ in range(2)]
        zst = [fw.sb(st, [128, 2048], BF16, "zst") for _ in range(2)]
        dtl = [fw.sb(st, [128, 128], F32, "dtl") for _ in range(2)]
        dtmp = fw.sb(st, [128, 64], F32, "dtmp")
        blks = blocks(True)

        def xload(bi):
            _, tn, Nn, _ = blks[bi]
            fw.sp.dma(xin[bi % 2][:, :, :Nn], xv[:, :, tn:tn + Nn], writes=[xin[bi % 2]])

        def nrm(bi):
            v_, _, N_, _ = blks[bi]
            norm_block(K, W, xin[bi % 2], N_, K.A1[i], K.modT[i][:, 0:8, :], v_, hTs[bi % 2])

        xload(0)
        xload(1)
        nrm(0)
        zi = 0
        for bi, (v, t0, N, isc) in enumerate(blks):
            hT = hTs[bi % 2]
            if bi + 1 < len(blks):
                nrm(bi + 1)
            if bi + 2 < len(blks):
                xload(bi + 2)
            for cc in range(32):
                xs_ = xst[(cc // 4) % 2]
                ps = K.psum.get()
                for k in range(8):
                    fw.pe.op(lambda e: e.matmul(ps[:, :N], lhsT=w[:, k, 2048 + cc * 128:2048 + (cc + 1) * 128], rhs=hT[:, k, :N],
                                                start=(k == 0), stop=(k == 7)),
                             reads=[w, hT], writes=[ps], inc=(k == 7))
                if cc % 2 == 0:
                    fw.act.op(lambda e: e.activation(out=xs_[:, cc % 4, :N], in_=ps[:, :N], func=AF.Identity), reads=[ps], writes=[xs_])
                else:
                    fw.dve.op(lambda e: e.tensor_copy(out=xs_[:, cc % 4, :N], in_=ps[:, :N]), reads=[ps], writes=[xs_])
                if cc % 4 == 3:
                    fw.sp.dma(xbcv[:, cc - 3:cc + 1, t0:t0 + N], xs_[:, :, :N], reads=[xs_])
            for sub in range(N // 128):
                zs_ = zst[zi % 2]
                dl = dtl[zi % 2]
                zi += 1
                for q4 in range(4):
                    ps = K.psum.get()
                    for k in range(8):
                        fw.pe.op(lambda e: e.matmul(ps[:, :], lhsT=hT[:, k, sub * 128:(sub + 1) * 128], rhs=w[:, k, q4 * 512:(q4 + 1) * 512],
                                                    start=(k == 0), stop=(k == 7)),
                                 reads=[w, hT], writes=[ps], inc=(k == 7))
                    fw.act.op(lambda e: e.activation(out=zs_[:, q4 * 512:(q4 + 1) * 512], in_=ps[:, :], func=AF.Silu), reads=[ps], writes=[zs_])
                ps = K.psum.get()
                for k in range(8):
                    fw.pe.op(lambda e: e.matmul(ps[:, 0:64], lhsT=hT[:, k, sub * 128:(sub + 1) * 128], rhs=w[:, k, 6144:6208],
                                                start=(k == 0), stop=(k == 7)),
                             reads=[w, hT], writes=[ps], inc=(k == 7))
                fw.dve.op(lambda e: e.tensor_tensor(out=dtmp[:, :], in0=ps[:, 0:64], in1=dtb[:, :], op=ALU.add), reads=[ps, dtb], writes=[dtmp])
                fw.act.op(lambda e: e.activation(out=dtmp[:, :], in_=dtmp[:, :], func=AF.Exp), reads=[dtmp], writes=[dtmp])
                fw.act.op(lambda e: e.activation(out=dl[:, 0:64], in_=dtmp[:, :], func=AF.Ln, bias=K.onec[:, 0:1], scale=1.0), reads=[dtmp], writes=[dl])
                fw.dve.op(lambda e: e.tensor_tensor(out=dl[:, 64:128], in0=dl[:, 0:64], in1=aneg[:, :], op=ALU.mult), reads=[dl, aneg], writes=[dl])
                ts = t0 + sub * 128
                fw.sp.dma(zv[ts:ts + 128, :], zs_[:, :], reads=[zs_])
                fw.sp.dma(K.dtla_d[ts:ts + 128, :], dl[:, :], reads=[dl])
        fw.barrier()
    K.marks.append((f"  MB{i}", fw.pe.n_ins))
    with ExitStack() as st:
        tri = [fw.sb(st, [128, 128], F32, "tri") for _ in range(2)]
        mneg = [fw.sb(st, [128, 128], BF16, "mneg") for _ in range(2)]
        selA = fw.sb(st, [128, 32, 128], BF16, "selA")
        onesf = fw.sb(st, [128, 128], F32, "onesf")
        fw.pool.op(lambda e: e.memset(onesf[:, :], 1.0), writes=[onesf])
        for dr in range(2):
            fw.sp.dma(tri[dr][:, :], K.cst["tri"][dr], writes=[tri[dr]])
            fw.pool.dma(mneg[dr][:, :], K.cst["maskneg"][dr], writes=[mneg[dr]])
        fw.pool.dma(selA[:, :, :], K.cst["selA"], writes=[selA])
        diag, cbT, Dbc = [], [], []
        for dr in range(2):
            cwT = load_rowsT(K, st, K.w["ssm_conv_w"][j, dr].rearrange("k (c p) -> (k c) p", p=128), 128, "cwT")
            cbT.append(vecT(K, st, K.w["ssm_conv_b"][j, dr], "cbT"))
            dg = fw.sb(st, [128, 32, 4, 128], BF16, "sdiag")
            n = 0
            for cc in range(32):
                for kk in range(4):
                    eng = fw.dve if n % 2 == 0 else fw.pool
                    n += 1
                    col = kk * 32 + cc
                    eng.op(lambda e: e.tensor_scalar(out=dg[:, cc, kk, :], in0=K.ident[:, :], scalar1=cwT[:, col:col + 1], scalar2=None, op0=ALU.mult),
                           reads=[cwT, K.ident], writes=[dg])
            diag.append(dg)
            db = fw.sb(st, [128, 32], F32, "Dbc")
            fw.sp.dma(db[:, :], K.w["ssm_d"][j, dr].partition_broadcast(128), writes=[db])
            Dbc.append(db)
        xbi = [fw.sb(st, [128, 32, 131], BF16, "xbi") for _ in range(2)]
        dli = [fw.sb(st, [128, 128], F32, "dli") for _ in range(2)]
        uT = fw.sb(st, [128, 32, 128], BF16, "uT")
        xs = fw.sb(st, [128, 2048], BF16, "xs")
        xr = fw.sb(st, [128, 2048], BF16, "xr")
        xd = fw.sb(st, [128, 2048], BF16, "xd")
        xsD = fw.sb(st, [128, 2048], F32, "xsD")
        Btok = fw.sb(st, [128, 8, 128], BF16, "Btok")
        acs = fw.sb(st, [128, 32], F32, "acs")
        dsb = fw.sb(st, [128, 32], F32, "dsb")
        dte = fw.sb(st, [128, 32], F32, "dte")
        expac = fw.sb(st, [128, 32], F32, "expac")
        decay = fw.sb(st, [128, 32], F32, "decay")
        w2 = fw.sb(st, [128, 32], F32, "w2")
        aT = [fw.sb(st, [128, 128], BF16, "aT") for _ in range(4)]
        hif = fw.sb(st, [32, 128], F32, "hif")
        for a_ in aT:
            fw.pool.op(lambda e: e.memset(a_[:, :], 0.0), writes=[a_])
        Lexp = [fw.sb(st, [128, 512], F32, "Lexp") for _ in range(2)]
        G = [fw.sb(st, [128, 4, 128], BF16, "G") for _ in range(2)]
        yt = [fw.sb(st, [128, 256], F32, "yt") for _ in range(2)]
        yt2 = [fw.sb(st, [128, 256], F32, "yt2") for _ in range(2)]
        ych = [fw.sb(st, [128, 2048], F32, "ych") for _ in range(2)]
        hst = [fw.sb(st, [128, 256], F32, "hst") for _ in range(8)]
        hb = [fw.sb(st, [128, 256], BF16, "hb") for _ in range(8)]

        pa_bank, pcb_bank = K.psum.bufs[0], K.psum.bufs[1]
        rot = PsumPool(K.psum.bufs[2:])
        sched = []
        for b in range(2):
            for dr in range(2):
                cl = [(b * SEG + c * 128, b * SEG, b * SEG + CTX) for c in range(2)] + \
                     [(b * SEG + CTX + c * 128, b * SEG + CTX, (b + 1) * SEG) for c in range(16)]
                if dr == 1:
                    cl = cl[0:2][::-1] + cl[2:][::-1]
                for ci, (t0, s0, s1) in enumerate(cl):
                    sched.append((b, dr, t0, s0, s1, ci == 0))

        def load(n):
            b, dr, t0, s0, s1, first = sched[n]
            xb, dl = xbi[n % 2], dli[n % 2]
            if dr == 0:
                if t0 == s0:
                    fw.pool.op(lambda e: e.memset(xb[:, :, 0:3], 0.0), writes=[xb])
                    fw.sp.dma(xb[:, :, 3:131], xbcv[:, :, t0:t0 + 128], writes=[xb])
                else:
                    fw.sp.dma(xb[:, :, 0:131], xbcv[:, :, t0 - 3:t0 + 128], writes=[xb])
            else:
                if t0 + 128 == s1:
                    fw.pool.op(lambda e: e.memset(xb[:, :, 128:131], 0.0), writes=[xb])
                    fw.sp.dma(xb[:, :, 0:128], xbcv[:, :, t0:t0 + 128], writes=[xb])
                else:
                    fw.sp.dma(xb[:, :, 0:131], xbcv[:, :, t0:t0 + 131], writes=[xb])
            fw.sp.dma(dl[:, :], K.dtla_d[t0:t0 + 128, :], writes=[dl])

        load(0)
        for n, (b, dr, t0, s0, s1, first) in enumerate(sched):
            if n + 1 < len(sched):
                load(n + 1)
            xb, dl = xbi[n % 2], dli[n % 2]
            yc = ych[n % 2]
            if first:
                for g in range(8):
                    fw.pool.op(lambda e: e.memset(hst[g][:, :], 0.0), writes=[hst[g]])
                    fw.pool.op(lambda e: e.memset(hb[g][:, :], 0.0), writes=[hb[g]])
            dt = dl[:, dr * 32:(dr + 1) * 32]
            la = dl[:, 64 + dr * 32:64 + (dr + 1) * 32]
            pa = pa_bank
            fw.pe.op(lambda e: e.matmul(pa[:, 0:32], lhsT=tri[dr][:, :], rhs=la, start=True, stop=True), reads=[tri[dr], dl], writes=[pa])
            fw.pe.op(lambda e: e.matmul(pa[:, 32:64], lhsT=onesf[:, :], rhs=la, start=True, stop=True), reads=[onesf, dl], writes=[pa])
            fw.pe.op(lambda e: e.matmul(pa[0:32, 64:192], lhsT=la, rhs=tri[dr][:, :], start=True, stop=True), reads=[tri[dr], dl], writes=[pa])
            fw.dve.op(lambda e: e.tensor_copy(out=acs[:, :], in_=pa[:, 0:32]), reads=[pa], writes=[acs])
            fw.dve.op(lambda e: e.tensor_tensor(out=dsb[:, :], in0=pa[:, 32:64], in1=acs[:, :], op=ALU.subtract), reads=[pa, acs], writes=[dsb])
            fw.dve.op(lambda e: e.tensor_copy(out=aT[0][0:32, :], in_=pa[0:32, 64:192]), reads=[pa], writes=[aT[0]])
            fw.dve.op(lambda e: e.tensor_copy(out=hif[:, :], in_=aT[0][0:32, :]), reads=[aT[0]], writes=[hif])
            fw.dve.op(lambda e: e.tensor_tensor(out=aT[1][0:32, :], in0=pa[0:32, 64:192], in1=hif[:, :], op=ALU.subtract), reads=[pa, hif], writes=[aT[1]])
            fw.dve.op(lambda e: e.tensor_scalar(out=aT[2][0:32, :], in0=aT[0][0:32, :], scalar1=-1.0, scalar2=None, op0=ALU.mult), reads=[aT[0]], writes=[aT[2]])
            fw.dve.op(lambda e: e.tensor_scalar(out=aT[3][0:32, :], in0=aT[1][0:32, :], scalar1=-1.0, scalar2=None, op0=ALU.mult), reads=[aT[1]], writes=[aT[3]])
            for c4 in range(8):
                ps = rot.get()
                for q in range(4):
                    cc = c4 * 4 + q
                    for kk in range(4):
                        off = kk if dr == 0 else 3 - kk
                        fw.pe.op(lambda e: e.matmul(ps[:, q * 128:(q + 1) * 128], lhsT=diag[dr][:, cc, kk, :], rhs=xb[:, cc, off:off + 128],
                                                    start=(kk == 0), stop=(kk == 3)),
                                 reads=[diag[dr], xb], writes=[ps], inc=(kk == 3))
                for q in range(4):
                    cc = c4 * 4 + q
                    fw.act.op(lambda e: e.activation(out=uT[:, cc, :], in_=ps[:, q * 128:(q + 1) * 128], func=AF.Silu, bias=cbT[dr][:, cc:cc + 1], scale=1.0),
                              reads=[ps], writes=[uT])
            fw.act.op(lambda e: e.activation(out=dte[:, :], in_=dsb[:, :], func=AF.Exp), reads=[dsb], writes=[dte])
            fw.act.op(lambda e: e.activation(out=expac[:, :], in_=acs[:, :], func=AF.Exp), reads=[acs], writes=[expac])
            fw.act.op(lambda e: e.activation(out=decay[:, :], in_=pa[:, 32:64], func=AF.Exp), reads=[pa], writes=[decay])
            fw.dve.op(lambda e: e.tensor_tensor(out=w2[:, :], in0=dt, in1=dte[:, :], op=ALU.mult), reads=[dl, dte], writes=[w2])
            for hx in range(2):
                ps = rot.get()
                pv = ps[:, :].bitcast(BF16)
                for q in range(8):
                    fw.pe.op(lambda e: e.transpose(out=pv[:, q * 128:(q + 1) * 128], in_=uT[:, hx * 8 + q, :], identity=K.ident_bf[:, :]),
                             reads=[uT, K.ident_bf], writes=[ps], inc=(q == 7))
                fw.dve.op(lambda e: e.tensor_copy(out=xs[:, hx * 1024:(hx + 1) * 1024], in_=pv[:, :]), reads=[ps], writes=[xs])
            ps = rot.get()
            pv = ps[:, :].bitcast(BF16)
            for q in range(8):
                fw.pe.op(lambda e: e.transpose(out=pv[:, q * 128:(q + 1) * 128], in_=uT[:, 16 + q, :], identity=K.ident_bf[:, :]),
                         reads=[uT, K.ident_bf], writes=[ps], inc=(q == 7))
            fw.act.op(lambda e: e.activation(out=Btok[:, :, :], in_=pv[:, :].rearrange("p (g n) -> p g n", g=8), func=AF.Identity), reads=[ps], writes=[Btok])
            xs3 = xs[:, :].rearrange("p (h d) -> p h d", h=32)
            fw.dve.op(lambda e: e.tensor_tensor(out=xr[:, :].rearrange("p (h d) -> p h d", h=32), in0=xs3,
                                                in1=dt.unsqueeze(2).to_broadcast([128, 32, 64]), op=ALU.mult), reads=[xs, dl], writes=[xr])
            fw.pool.op(lambda e: e.tensor_tensor(out=xd[:, :].rearrange("p (h d) -> p h d", h=32), in0=xs3,
                                                 in1=w2[:, :].unsqueeze(2).to_broadcast([128, 32, 64]), op=ALU.mult), reads=[xs, w2], writes=[xd])
            fw.pool.op(lambda e: e.tensor_tensor(out=xsD[:, :].rearrange("p (h d) -> p h d", h=32), in0=xs3,
                                                 in1=Dbc[dr][:, :].unsqueeze(2).to_broadcast([128, 32, 64]), op=ALU.mult), reads=[xs, Dbc[dr]], writes=[xsD])
            pcb = None
            for g in range(8):
                if g % 4 == 0:
                    pcb = pcb_bank
                    for q in range(4):
                        fw.pe.op(lambda e: e.matmul(pcb[:, q * 128:(q + 1) * 128], lhsT=uT[:, 16 + g + q, :], rhs=uT[:, 24 + g + q, :], start=True, stop=True),
                                 reads=[uT], writes=[pcb], inc=(q == 3))
                pS = rot.get()
                fw.pe.op(lambda e: e.matmul(pS[:, :].rearrange("p (j l) -> p j l", j=4), lhsT=K.ident_bf[:, :], rhs=bc_mid(mneg[dr][:, :], 4),
                                            start=True, stop=False, skip_group_check=True),
                         reads=[K.ident_bf, mneg[dr]], writes=[pS], inc=False)
                for jj in range(4):
                    hj = 4 * g + jj
                    o_ap = pS[:, jj * 128:(jj + 1) * 128]
                    fw.pe.op(lambda e: e.matmul(o_ap, lhsT=selA[:, hj, :], rhs=aT[0][:, :], start=False, stop=False, skip_group_check=True),
                             reads=[selA, aT[0]], writes=[pS], inc=False)
                    fw.pe.op(lambda e: e.matmul(o_ap, lhsT=selA[:, hj, :], rhs=aT[1][:, :], start=False, stop=False, skip_group_check=True),
                             reads=[selA, aT[1]], writes=[pS], inc=False)
                    fw.pe.op(lambda e: e.matmul(o_ap, lhsT=aT[2][:, :], rhs=selA[:, hj, :], start=False, stop=False, skip_group_check=True),
                             reads=[selA, aT[2]], writes=[pS], inc=False)
                    fw.pe.op(lambda e: e.matmul(o_ap, lhsT=aT[3][:, :], rhs=selA[:, hj, :], start=False, stop=True, skip_group_check=True),
                             reads=[selA, aT[3]], writes=[pS], inc=(jj == 3))
                Le, Gg = Lexp[g % 2], G[g % 2]
                fw.act.op(lambda e: e.activation(out=Le[:, :], in_=pS[:, :], func=AF.Exp), reads=[pS], writes=[Le])
                fw.dve.op(lambda e: e.tensor_tensor(out=Gg[:, :, :], in0=Le[:, :].rearrange("p (j l) -> p j l", j=4),
                                                    in1=bc_mid(pcb[:, (g % 4) * 128:(g % 4 + 1) * 128], 4), op=ALU.mult),
                          reads=[Le, pcb], writes=[Gg])
                pY = rot.get()
                for jj in range(4):
                    hj = 4 * g + jj
                    fw.pe.op(lambda e: e.matmul(pY[:, jj * 64:(jj + 1) * 64], lhsT=Gg[:, jj, :], rhs=xr[:, hj * 64:(hj + 1) * 64], start=True, stop=True),
                             reads=[Gg, xr], writes=[pY], inc=False)
                fw.pe.op(lambda e: e.matmul(pY[:, 256:512], lhsT=uT[:, 24 + g, :], rhs=hb[g][:, :], start=True, stop=True),
                         reads=[uT, hb[g]], writes=[pY])
                pT = rot.get()
                fw.pe.op(lambda e: e.matmul(pT[:, 0:256], lhsT=Btok[:, g, :], rhs=xd[:, g * 256:(g + 1) * 256], start=True, stop=True),
                         reads=[Btok, xd], writes=[pT])
                ya, yb_ = yt[g % 2], yt2[g % 2]
                fw.dve.op(lambda e: e.tensor_tensor(out=ya[:, :].rearrange("p (j d) -> p j d", j=4), in0=pY[:, 256:512].rearrange("p (j d) -> p j d", j=4),
                                                    in1=expac[:, 4 * g:4 * g + 4].unsqueeze(2).to_broadcast([128, 4, 64]), op=ALU.mult),
                          reads=[pY, expac], writes=[ya])
                fw.dve.op(lambda e: e.tensor_tensor(out=yb_[:, :], in0=pY[:, 0:256], in1=ya[:, :], op=ALU.add), reads=[pY, ya], writes=[yb_])
                fw.pool.op(lambda e: e.tensor_tensor(out=yc[:, g * 256:(g + 1) * 256], in0=yb_[:, :], in1=xsD[:, g * 256:(g + 1) * 256], op=ALU.add),
                           reads=[yb_, xsD], writes=[yc])
                fw.pool.op(lambda e: e.tensor_tensor(out=hst[g][:, :].rearrange("p (j d) -> p j d", j=4), in0=hst[g][:, :].rearrange("p (j d) -> p j d", j=4),
                                                     in1=decay[:, 4 * g:4 * g + 4].unsqueeze(2).to_broadcast([128, 4, 64]), op=ALU.mult),
                           reads=[hst[g], decay], writes=[hst[g]])
                fw.dve.op(lambda e: e.tensor_tensor(out=hst[g][:, :], in0=pT[:, 0:256], in1=hst[g][:, :], op=ALU.add), reads=[pT, hst[g]], writes=[hst[g]])
                fw.act.op(lambda e: e.activation(out=hb[g][:, :], in_=hst[g][:, :], func=AF.Identity), reads=[hst[g]], writes=[hb[g]])
            fw.sp.dma(K.y_d[dr, t0:t0 + 128, :], yc[:, :], reads=[yc])
        fw.barrier()
    K.marks.append((f"  MC{i}", fw.pe.n_ins))
    with ExitStack() as st:
        wo = fw.sb(st, [128, 16, D], BF16, "wo")
        fw.pool.dma(wo[:, :, :], K.w["ssm_w_out"][j].rearrange("(k p) n -> p k n", p=128), writes=[wo])
        gnb = fw.sb(st, [128, 2048], F32, "gnb")
        fw.sp.dma(gnb[:, :], K.w["ssm_norm_g"][j].partition_broadcast(128), writes=[gnb])
        xin = [fw.sb(st, [128, 8, 512], F32, "xin") for _ in range(2)]
        yf = [fw.sb(st, [128, 2048], F32, "yf") for _ in range(3)]
        yb = [fw.sb(st, [128, 2048], F32, "yb") for _ in range(3)]
        zs = [fw.sb(st, [128, 2048], BF16, "zs") for _ in range(3)]
        ygs = [fw.sb(st, [128, 2048], F32, "yg") for _ in range(3)]
        sqjs = [fw.sb(st, [128, 2048], F32, "sqj") for _ in range(1)]
        ssqs = [fw.sb(st, [128, 8], F32, "ssq") for _ in range(3)]
        ynbs = [fw.sb(st, [128, 2048], BF16, "ynb") for _ in range(2)]
        ynT = fw.sb(st, [128, 16, 512], BF16, "ynT")
        gate = K.modT[i][:, 16:24, :]
        blks = blocks(need_ctx)
        subs = [(bi, s) for bi, (v, t0, N, isc) in enumerate(blks) for s in range(N // 128)]

        def load(si):
            bi, s = subs[si]
            v, t0, N, isc = blks[bi]
            ts = t0 + s * 128
            fw.sp.dma(yf[si % 3][:, :], K.y_d[0, ts:ts + 128, :], writes=[yf[si % 3]])
            fw.sp.dma(yb[si % 3][:, :], K.y_d[1, ts:ts + 128, :], writes=[yb[si % 3]])
            fw.sp.dma(zs[si % 3][:, :], zv[ts:ts + 128, :], writes=[zs[si % 3]])
            if s == 0:
                fw.sp.dma(xin[bi % 2][:, :, :N], xv[:, :, t0:t0 + N], writes=[xin[bi % 2]])

        def stA(si):
            a, b_, z_ = yf[si % 3], yb[si % 3], zs[si % 3]
            yg, ssq, sqj = ygs[si % 3], ssqs[si % 3], sqjs[0]
            fw.pool.op(lambda e: e.tensor_tensor(out=a[:, :], in0=a[:, :], in1=b_[:, :], op=ALU.add), reads=[a, b_], writes=[a])
            fw.dve.op(lambda e: e.tensor_tensor(out=yg[:, :], in0=a[:, :], in1=z_[:, :], op=ALU.mult), reads=[a, z_], writes=[yg])
            fw.act.op(lambda e: e.activation(out=sqj[:, :], in_=yg[:, :], func=AF.Square), reads=[yg], writes=[sqj])
            fw.dve.op(lambda e: e.tensor_reduce(out=ssq[:, :], in_=sqj[:, :].rearrange("p (g d) -> p g d", g=8), axis=AX.X, op=ALU.add),
                      reads=[sqj], writes=[ssq])
            fw.act.op(lambda e: e.activation(out=ssq[:, :], in_=ssq[:, :], func=AF.Ln, bias=K.epsc[:, 0:1], scale=1.0 / 256), reads=[ssq], writes=[ssq])
            fw.act.op(lambda e: e.activation(out=ssq[:, :], in_=ssq[:, :], func=AF.Exp, scale=-0.5), reads=[ssq], writes=[ssq])

        def stB(si):
            bi, s = subs[si]
            v, t0, N, isc = blks[bi]
            yg, ssq, ynb = ygs[si % 3], ssqs[si % 3], ynbs[si % 2]
            yn = yg
            fw.dve.op(lambda e: e.tensor_tensor(out=yn[:, :].rearrange("p (g d) -> p g d", g=8), in0=yg[:, :].rearrange("p (g d) -> p g d", g=8),
                                                in1=ssq[:, :].unsqueeze(2).to_broadcast([128, 8, 256]), op=ALU.mult), reads=[yg, ssq], writes=[yn])
            fw.pool.op(lambda e: e.tensor_tensor(out=ynb[:, :], in0=yn[:, :], in1=gnb[:, :], op=ALU.mult), reads=[yn, gnb], writes=[ynb])
            for hx in range(2):
                ps = K.psum.get()
                pv = ps[:, :].bitcast(BF16)
                for q in range(8):
                    cc = hx * 8 + q
                    fw.pe.op(lambda e: e.transpose(out=pv[:, q * 128:(q + 1) * 128], in_=ynb[:, cc * 128:(cc + 1) * 128], identity=K.ident_bf[:, :]),
                             reads=[ynb, K.ident_bf], writes=[ps], inc=(q == 7))
                src = pv[:, :].rearrange("p (c t) -> p c t", c=8)
                dst = ynT[:, hx * 8:(hx + 1) * 8, s * 128:(s + 1) * 128]
                if hx == 0:
                    fw.act.op(lambda e: e.activation(out=dst, in_=src, func=AF.Identity), reads=[ps], writes=[ynT])
                else:
                    fw.dve.op(lambda e: e.tensor_copy(out=dst, in_=src), reads=[ps], writes=[ynT])
            if s != N // 128 - 1:
                return
            xi = xin[bi % 2]
            for dc in range(8):
                ps = K.psum.get()
                for k in range(16):
                    fw.pe.op(lambda e: e.matmul(ps[:, :N], lhsT=wo[:, k, dc * 128:(dc + 1) * 128], rhs=ynT[:, k, :N], start=(k == 0), stop=(k == 15)),
                             reads=[wo, ynT], writes=[ps], inc=(k == 15))
                fw.dve.op(lambda e: e.scalar_tensor_tensor(out=xi[:, dc, :N], in0=ps[:, :N], scalar=gate[:, dc, v:v + 1], in1=xi[:, dc, :N],
                                                           op0=ALU.mult, op1=ALU.add),
                          reads=[ps, xi], writes=[xi])
            fw.sp.dma(xv[:, :, t0:t0 + N], xi[:, :, :N], reads=[xi])

        for si in range(3):
            load(si)
        stA(0)
        stA(1)
        for si in range(len(subs)):
            if si + 2 < len(subs):
                stA(si + 2)
            stB(si)
            if si + 3 < len(subs):
                load(si + 3)
        fw.barrier()


MIXERS[0] = mixer_mamba
```

```python
import math
from contextlib import ExitStack
import numpy as np
import concourse.bass as bass
import concourse.mybir as mybir
from concourse.bass_utils import run_bass_kernel_spmd

F32 = mybir.dt.float32
BF16 = mybir.dt.bfloat16
AF = mybir.ActivationFunctionType
ALU = mybir.AluOpType
AX = mybir.AxisListType

EPS = 1e-6
D = 1024
SEQ = 2048
CTX = 256
SEG = CTX + SEQ
T = 2 * SEG
NCORES = 8
FH = 2816
NEG = -30000.0


class Buf:
    def __init__(self, t, name, psum=False):
        self.t = t
        self.name = name
        self.psum = psum
        self.last_w = None
        self.readers = []

    def __getitem__(self, k):
        return self.t[k]


class Tag:
    __slots__ = ("sem", "val", "eng")

    def __init__(self, sem, val, eng):
        self.sem, self.val, self.eng = sem, val, eng


class Eng:
    def __init__(self, name, h, kind):
        self.name, self.h, self.kind = name, h, kind
        self.sem = None
        self.count = 0
        self.seen = {}
        self.dsems = []
        self.dnext = 0
        self.n_ins = 0

    def _wait(self, tag):
        key = id(tag.sem)
        if self.seen.get(key, 0) >= tag.val:
            return
        self.h.wait_ge(tag.sem, tag.val)
        self.seen[key] = tag.val

    def _deps(self, reads, writes):
        for b in reads:
            t = b.last_w
            if t is not None:
                if t.eng is self and self.kind == "pe":
                    continue
                self._wait(t)
            if b.psum:
                for r in b.readers:
                    if r.eng is not self:
                        self._wait(r)
        for b in writes:
            t = b.last_w
            if t is not None and not (t.eng is self and self.kind == "pe"):
                self._wait(t)
            for r in b.readers:
                self._wait(r)

    def _record(self, tag, reads, writes):
        for b in reads:
            if tag.eng is not None:
                b.readers = [r for r in b.readers if r.eng is not tag.eng]
            b.readers.append(tag)
        for b in writes:
            b.last_w = tag
            b.readers = []

    def op(self, fn, reads=(), writes=(), inc=True):
        self._deps(reads, writes)
        ins = fn(self.h)
        self.n_ins += 1
        if inc:
            self.count += 1
            ins.then_inc(self.sem, 1)
            tag = Tag(self.sem, self.count, self)
        else:
            tag = Tag(self.sem, self.count + 1, self)
        self._record(tag, reads, writes)
        return ins

    def dma(self, out, in_, reads=(), writes=(), **kw):
        self._deps(reads, writes)
        i = self.dnext % len(self.dsems)
        self.dnext += 1
        sem, cnt = self.dsems[i]
        if cnt > 0:
            self._wait(Tag(sem, cnt, None))
        cnt += 16
        self.dsems[i] = (sem, cnt)
        self.h.dma_start(out=out, in_=in_, **kw).then_inc(sem, 16)
        self.n_ins += 1
        tag = Tag(sem, cnt, None)
        self._record(tag, reads, writes)
        return tag


class FW:
    def __init__(self, nc, stack, n_dma_sems=8):
        self.nc = nc
        self.pe = Eng("pe", nc.tensor, "pe")
        self.act = Eng("act", nc.scalar, "act")
        self.dve = Eng("dve", nc.vector, "dve")
        self.pool = Eng("pool", nc.gpsimd, "pool")
        self.sp = Eng("sp", nc.sync, "sp")
        self.engs = [self.pe, self.act, self.dve, self.pool, self.sp]
        for e in self.engs:
            e.sem = stack.enter_context(nc.semaphore("s_" + e.name))
        for e in (self.sp, self.pool, self.act):
            for i in range(n_dma_sems):
                s = stack.enter_context(nc.semaphore(f"d_{e.name}{i}"))
                e.dsems.append((s, 0))
        self._uid = 0

    def sb(self, cm, shape, dtype, name="t"):
        self._uid += 1
        name = f"{name}_{self._uid}"
        return Buf(cm.enter_context(self.nc.sbuf_tensor(name, list(shape), dtype)), name)

    def ps(self, cm, shape, dtype, name="p"):
        self._uid += 1
        name = f"{name}_{self._uid}"
        return Buf(cm.enter_context(self.nc.psum_tensor(name, list(shape), dtype)), name, psum=True)

    def barrier(self):
        tags = []
        for e in self.engs:
            if e.count > 0:
                tags.append(Tag(e.sem, e.count, e))
            for (s, c) in e.dsems:
                if c > 0:
                    tags.append(Tag(s, c, None))
        for e in self.engs:
            for t in tags:
                if t.eng is e:
                    continue
                e._wait(t)


class PsumPool:
    def __init__(self, bufs):
        self.bufs = bufs
        self.i = 0

    def get(self):
        b = self.bufs[self.i % len(self.bufs)]
        self.i += 1
        return b


class Ctx:
    pass


def blocks(include_ctx=True):
    out = []
    for b in range(2):
        base = b * SEG
        if include_ctx:
            out.append((2, base, CTX, True))
        for q in range(4):
            out.append((b, base + CTX + q * 512, 512, False))
    return out


def bc_mid(ap2d, n):
    (ps, pn), (fs, fn) = ap2d.ap
    return bass.AP(ap2d.tensor, ap2d.offset, [[ps, pn], [0, n], [fs, fn]])


def load_vecT(K, cm, rows_ap, n, name="vT"):
    fw = K.fw
    dst = fw.sb(cm, [128, n], F32, name)
    with ExitStack() as st:
        tmp = fw.sb(st, [128, 128], F32, "vrow")
        fw.sp.dma(tmp[0:n, :], rows_ap, writes=[tmp])
        ps = K.psum.get()
        fw.pe.op(lambda e: e.transpose(out=ps[:, 0:n], in_=tmp[0:n, :], identity=K.ident[0:n, 0:n]),
                 reads=[tmp, K.ident], writes=[ps])
        fw.dve.op(lambda e: e.tensor_copy(out=dst[:, :], in_=ps[:, 0:n]), reads=[ps], writes=[dst])
        fw.barrier()
    return dst


def norm_block(K, W, xin, N, A, sh, v, hT, hoff=0):
    fw = K.fw
    fw.act.op(lambda e: e.activation(out=W.sq[:, :, :N], in_=xin[:, :, :N], func=AF.Square),
              reads=[xin], writes=[W.sq])
    ps = K.psum.get()
    for k in range(8):
        fw.pe.op(lambda e: e.matmul(ps[:, :N], lhsT=K.ones_bf[:, :], rhs=W.sq[:, k, :N], start=(k == 0), stop=(k == 7)),
                 reads=[W.sq, K.ones_bf], writes=[ps], inc=(k == 7))
    fw.act.op(lambda e: e.activation(out=W.rstd[:, :N], in_=ps[:, :N], func=AF.Ln, bias=K.epsc[:, 0:1], scale=1.0 / D),
              reads=[ps], writes=[W.rstd])
    fw.act.op(lambda e: e.activation(out=W.rstd[:, :N], in_=W.rstd[:, :N], func=AF.Exp, scale=-0.5),
              reads=[W.rstd], writes=[W.rstd])
    for k in range(8):
        tmp = W.ntmp[k % 2]
        fw.dve.op(lambda e: e.scalar_tensor_tensor(out=tmp[:, :N], in0=xin[:, k, :N], scalar=A[:, k, v:v + 1],
                                                   in1=W.rstd[:, :N], op0=ALU.mult, op1=ALU.mult),
                  reads=[xin, W.rstd], writes=[tmp])
        fw.act.op(lambda e: e.activation(out=hT[:, k, hoff:hoff + N], in_=tmp[:, :N], func=AF.Identity,
                                         bias=sh[:, k, v:v + 1], scale=1.0),
                  reads=[tmp], writes=[hT])


def alloc_norm_work(K, cm):
    fw = K.fw
    W = Ctx()
    W.sq = fw.sb(cm, [128, 8, 512], BF16, "sq")
    W.rstd = fw.sb(cm, [128, 512], F32, "rstd")
    W.ntmp = [fw.sb(cm, [128, 512], F32, "ntmp") for _ in range(2)]
    return W


def xT_view(K):
    return K.xT_d.rearrange("(k p) t -> p k t", p=128)


def phase_init(K):
    fw = K.fw
    xv = xT_view(K)
    with ExitStack() as st:
        tin = [fw.sb(st, [128, D], F32, "tin") for _ in range(2)]
        tout = [fw.sb(st, [128, 8, 128], F32, "tout") for _ in range(2)]
        it = 0
        for b in range(2):
            for (src, n, off) in ((K.ctx_in, CTX, 0), (K.x_in, SEQ, CTX)):
                for j in range(n // 128):
                    ti, to = tin[it % 2], tout[it % 2]
                    it += 1
                    fw.sp.dma(ti[:, :], src[b, j * 128:(j + 1) * 128, :], writes=[ti])
                    for half in range(2):
                        ps = K.psum.get()
                        for q in range(4):
                            k = half * 4 + q
                            fw.pe.op(lambda e: e.transpose(out=ps[:, q * 128:(q + 1) * 128], in_=ti[:, k * 128:(k + 1) * 128],
                                                           identity=K.ident[:, :]),
                                     reads=[ti, K.ident], writes=[ps], inc=(q == 3))
                        eng = fw.dve if half == 0 else fw.act
                        if half == 0:
                            fw.dve.op(lambda e: e.tensor_copy(out=to[:, 0:4, :], in_=ps[:, :].rearrange("p (q t) -> p q t", q=4)),
                                      reads=[ps], writes=[to])
                        else:
                            fw.act.op(lambda e: e.activation(out=to[:, 4:8, :], in_=ps[:, :].rearrange("p (q t) -> p q t", q=4),
                                                             func=AF.Identity),
                                      reads=[ps], writes=[to])
                    t0 = b * SEG + off + j * 128
                    fw.sp.dma(xv[:, :, t0:t0 + 128], to[:, :, :], reads=[to])
        fw.barrier()


def phase_mod(K, layers):
    fw = K.fw
    K.modT, K.A1, K.A2 = {}, {}, {}
    for i in layers:
        K.modT[i] = fw.sb(K.st, [128, 48, 3], F32, f"modT{i}")
        K.A1[i] = fw.sb(K.st, [128, 8, 3], F32, f"A1_{i}")
        K.A2[i] = fw.sb(K.st, [128, 8, 3], F32, f"A2_{i}")
    with ExitStack() as st:
        wsb = fw.sb(st, [128, 8, 6144], BF16, "modw")
        c24 = fw.sb(st, [24, 128], F32, "c24")
        sT = fw.sb(st, [128, 24], BF16, "sT")
        fw.sp.dma(c24[:, :], K.c3.rearrange("v (k p) -> (v k) p", p=128), writes=[c24])
        ps = K.psum.get()
        fw.pe.op(lambda e: e.transpose(out=ps[:, 0:24], in_=c24[:, :], identity=K.ident[0:24, 0:24]),
                 reads=[c24, K.ident], writes=[ps])
        fw.act.op(lambda e: e.activation(out=sT[:, :], in_=ps[:, 0:24], func=AF.Silu), reads=[ps], writes=[sT])
        for i in layers:
            with ExitStack() as st2:
                mb = load_vecT(K, st2, K.w["mod_b"][i].rearrange("(c p) -> c p", p=128), 48, "mb")
                g1 = load_vecT(K, st2, K.w["norm1_g"][i].rearrange("(c p) -> c p", p=128), 8, "g1")
                g2 = load_vecT(K, st2, K.w["norm2_g"][i].rearrange("(c p) -> c p", p=128), 8, "g2")
                tmp = fw.sb(st2, [128, 8, 3], F32, "mtmp")
                fw.pool.dma(wsb[:, :, :], K.w["mod_w"][i].rearrange("(k p) n -> p k n", p=128), writes=[wsb])
                ps = K.psum.get()
                for cc in range(48):
                    for k in range(8):
                        fw.pe.op(lambda e: e.matmul(ps[:, cc * 3:(cc + 1) * 3], lhsT=wsb[:, k, cc * 128:(cc + 1) * 128],
                                                    rhs=sT[:, k:24:8], start=(k == 0), stop=(k == 7)),
                                 reads=[wsb, sT], writes=[ps], inc=(k == 7))
                mT = K.modT[i]
                fw.dve.op(lambda e: e.tensor_tensor(out=mT[:, :, :], in0=ps[:, 0:144].rearrange("p (c v) -> p c v", v=3),
                                                    in1=mb[:, :].unsqueeze(2).to_broadcast([128, 48, 3]), op=ALU.add),
                          reads=[ps, mb], writes=[mT])
                for (A, g, c0) in ((K.A1[i], g1, 8), (K.A2[i], g2, 32)):
                    fw.dve.op(lambda e: e.tensor_scalar(out=tmp[:, :, :], in0=mT[:, c0:c0 + 8, :], scalar1=1.0, scalar2=None, op0=ALU.add),
                              reads=[mT], writes=[tmp])
                    fw.dve.op(lambda e: e.tensor_tensor(out=A[:, :, :], in0=tmp[:, :, :],
                                                        in1=g[:, :].unsqueeze(2).to_broadcast([128, 8, 3]), op=ALU.mult),
                              reads=[tmp, g], writes=[A])
                fw.barrier()
        fw.barrier()


def phase_ffn_up(K, i, include_ctx):
    fw = K.fw
    xv = xT_view(K)
    gv = K.gT_d.rearrange("(c p) t -> p c t", p=128)
    with ExitStack() as st:
        cw = [load_vecT(K, st, K.w["ffn_conv_w"][i, kk].rearrange("(c p) -> c p", p=128), 44, "fcw") for kk in range(3)]
        cb = load_vecT(K, st, K.w["ffn_conv_b"][i].rearrange("(c p) -> c p", p=128), 44, "fcb")
        hT = fw.sb(st, [128, 8, T], BF16, "hT")
        with ExitStack() as st2:
            W = alloc_norm_work(K, st2)
            xin = [fw.sb(st2, [128, 8, 512], F32, "xin") for _ in range(2)]
            for bi, (v, t0, N, isc) in enumerate(blocks(include_ctx)):
                xi = xin[bi % 2]
                fw.sp.dma(xi[:, :, :N], xv[:, :, t0:t0 + N], writes=[xi])
                norm_block(K, W, xi, N, K.A2[i], K.modT[i][:, 24:32, :], v, hT, hoff=t0)
            fw.barrier()
        wb = [fw.sb(st, [128, 8, 256], BF16, "wup") for _ in range(3)]
        up = [[fw.sb(st, [128, SEQ + 2], F32, "upre") for _ in range(2)] for _ in range(2)]
        accs = [[fw.sb(st, [128, SEQ], F32, "acc") for _ in range(2)] for _ in range(2)]
        sils = [fw.sb(st, [128, SEQ], F32, "sil") for _ in range(2)]
        gt = [fw.sb(st, [128, SEQ], BF16, "gt") for _ in range(2)]
        for u2 in up:
            for u in u2:
                fw.pool.op(lambda e: e.memset(u[:, :], 0.0), writes=[u])
        wsrc = K.w["ffn_w_up"][i].rearrange("(k p) n -> p k n", p=128)
        segs = []
        for b in range(2):
            if include_ctx:
                segs.append((b * SEG, CTX))
            segs.append((b * SEG + CTX, SEQ))
        it = 0

        def wload(c):
            w = wb[c % 3]
            fw.pool.dma(w[:, :, 0:128], wsrc[:, :, c * 128:(c + 1) * 128], writes=[w])
            fw.pool.dma(w[:, :, 128:256], wsrc[:, :, FH + c * 128:FH + (c + 1) * 128], writes=[w])

        wload(0)
        wload(1)
        for c in range(22):
            if c + 2 < 22:
                wload(c + 2)
            w = wb[c % 3]
            for (s0, L) in segs:
                ub = up[it % 2]
                g = gt[it % 2]
                acc = accs[it % 2]
                sil = sils[it % 2]
                if L == SEQ:
                    it += 1
                for half in range(2):
                    u = ub[half]
                    cc = c + 22 * half
                    if L != SEQ:
                        fw.pool.op(lambda e: e.memset(u[:, L + 1:L + 2], 0.0), writes=[u])
                    for n0 in range(0, L, 512):
                        n = min(512, L - n0)
                        ps = K.psum.get()
                        for k in range(8):
                            fw.pe.op(lambda e: e.matmul(ps[:, :n], lhsT=w[:, k, half * 128:(half + 1) * 128],
                                                        rhs=hT[:, k, s0 + n0:s0 + n0 + n], start=(k == 0), stop=(k == 7)),
                                     reads=[w, hT], writes=[ps], inc=(k == 7))
                        fw.act.op(lambda e: e.activation(out=u[:, 1 + n0:1 + n0 + n], in_=ps[:, :n], func=AF.Identity),
                                  reads=[ps], writes=[u])
                    a = acc[half]
                    fw.act.op(lambda e: e.activation(out=a[:, :L], in_=u[:, 0:L], func=AF.Identity, scale=cw[0][:, cc:cc + 1],
                                                     bias=cb[:, cc:cc + 1]),
                              reads=[u], writes=[a])
                    for kk in (1, 2):
                        fw.dve.op(lambda e: e.scalar_tensor_tensor(out=a[:, :L], in0=u[:, kk:kk + L], scalar=cw[kk][:, cc:cc + 1],
                                                                   in1=a[:, :L], op0=ALU.mult, op1=ALU.add),
                                  reads=[u, a], writes=[a])
                fw.act.op(lambda e: e.activation(out=sil[:, :L], in_=acc[0][:, :L], func=AF.Silu), reads=[acc[0]], writes=[sil])
                fw.pool.op(lambda e: e.tensor_tensor(out=g[:, :L], in0=sil[:, :L], in1=acc[1][:, :L], op=ALU.mult),
                           reads=[sil, acc[1]], writes=[g])
                fw.sp.dma(gv[:, c, s0:s0 + L], g[:, :L], reads=[g])
        fw.barrier()


def phase_ffn_down(K, i, include_ctx, final):
    fw = K.fw
    xv = xT_view(K)
    gv = K.gT_d.rearrange("(c p) t -> p c t", p=128)
    gate = K.modT[i][:, 40:48, :]
    with ExitStack() as st:
        wd = fw.sb(st, [128, 22, D], BF16, "wdown")
        fw.pool.dma(wd[:, :, :], K.w["ffn_w_down"][i].rearrange("(k p) n -> p k n", p=128), writes=[wd])
        xin = [fw.sb(st, [128, 8, 512], F32, "xin") for _ in range(2)]
        gin = [fw.sb(st, [128, 22, 512], BF16, "gin") for _ in range(2)]
        if final:
            W = alloc_norm_work(K, st)
            fg = load_vecT(K, st, K.w["final_g"].rearrange("(c p) -> c p", p=128), 8, "fg")
            xn = fw.sb(st, [128, 8, 512], F32, "xn")
            otile = [fw.sb(st, [128, D], F32, "otile") for _ in range(2)]
        blks = blocks(include_ctx)

        def load(bi):
            v, t0, N, isc = blks[bi]
            fw.sp.dma(xin[bi % 2][:, :, :N], xv[:, :, t0:t0 + N], writes=[xin[bi % 2]])
            fw.sp.dma(gin[bi % 2][:, :, :N], gv[:, :, t0:t0 + N], writes=[gin[bi % 2]])

        load(0)
        oi = 0
        for bi, (v, t0, N, isc) in enumerate(blks):
            if bi + 1 < len(blks):
                load(bi + 1)
            xi, gi = xin[bi % 2], gin[bi % 2]
            for dc in range(8):
                ps = K.psum.get()
                for k in range(22):
                    fw.pe.op(lambda e: e.matmul(ps[:, :N], lhsT=wd[:, k, dc * 128:(dc + 1) * 128], rhs=gi[:, k, :N],
                                                start=(k == 0), stop=(k == 21)),
                             reads=[wd, gi], writes=[ps], inc=(k == 21))
                fw.dve.op(lambda e: e.scalar_tensor_tensor(out=xi[:, dc, :N], in0=ps[:, :N], scalar=gate[:, dc, v:v + 1],
                                                           in1=xi[:, dc, :N], op0=ALU.mult, op1=ALU.add),
                          reads=[ps, xi], writes=[xi])
            if (not final) or isc:
                fw.sp.dma(xv[:, :, t0:t0 + N], xi[:, :, :N], reads=[xi])
                continue
            fw.act.op(lambda e: e.activation(out=W.sq[:, :, :N], in_=xi[:, :, :N], func=AF.Square), reads=[xi], writes=[W.sq])
            ps = K.psum.get()
            for k in range(8):
                fw.pe.op(lambda e: e.matmul(ps[:, :N], lhsT=K.ones_bf[:, :], rhs=W.sq[:, k, :N], start=(k == 0), stop=(k == 7)),
                         reads=[W.sq, K.ones_bf], writes=[ps], inc=(k == 7))
            fw.act.op(lambda e: e.activation(out=W.rstd[:, :N], in_=ps[:, :N], func=AF.Ln, bias=K.epsc[:, 0:1], scale=1.0 / D),
                      reads=[ps], writes=[W.rstd])
            fw.act.op(lambda e: e.activation(out=W.rstd[:, :N], in_=W.rstd[:, :N], func=AF.Exp, scale=-0.5),
                      reads=[W.rstd], writes=[W.rstd])
            for k in range(8):
                fw.dve.op(lambda e: e.scalar_tensor_tensor(out=xn[:, k, :N], in0=xi[:, k, :N], scalar=fg[:, k:k + 1],
                                                           in1=W.rstd[:, :N], op0=ALU.mult, op1=ALU.mult),
                          reads=[xi, W.rstd], writes=[xn])
            b = t0 // SEG
            tl = t0 - b * SEG - CTX
            for j in range(N // 128):
                ot = otile[oi % 2]
                oi += 1
                for half in range(2):
                    ps = K.psum.get()
                    for q in range(4):
                        k = half * 4 + q
                        fw.pe.op(lambda e: e.transpose(out=ps[:, q * 128:(q + 1) * 128], in_=xn[:, k, j * 128:(j + 1) * 128],
                                                       identity=K.ident[:, :]),
                                 reads=[xn, K.ident], writes=[ps], inc=(q == 3))
                    if half == 0:
                        fw.dve.op(lambda e: e.tensor_copy(out=ot[:, 0:512], in_=ps[:, :]), reads=[ps], writes=[ot])
                    else:
                        fw.act.op(lambda e: e.activation(out=ot[:, 512:1024], in_=ps[:, :], func=AF.Identity), reads=[ps], writes=[ot])
                fw.sp.dma(K.out[b, tl + j * 128:tl + (j + 1) * 128, :], ot[:, :], reads=[ot])
        fw.barrier()


W_SHAPES = {
    "mod_w": [4, D, 6144], "mod_b": [4, 6144], "norm1_g": [4, D], "norm2_g": [4, D],
    "ffn_w_up": [4, D, 2 * FH], "ffn_conv_w": [4, 3, 2 * FH], "ffn_conv_b": [4, 2 * FH], "ffn_w_down": [4, FH, D],
    "ssm_w_in": [2, D, 6208], "ssm_conv_w": [2, 2, 4, 4096], "ssm_conv_b": [2, 2, 4096], "ssm_dt_bias": [2, 2, 32],
    "ssm_a_log": [2, 2, 32], "ssm_d": [2, 2, 32], "ssm_norm_g": [2, 2048], "ssm_w_out": [2, 2048, D],
    "attn_w_in": [1, D, 3 * D], "attn_lambda": [1, 4, 64], "attn_norm_g": [1, 128], "attn_w_out": [1, D, D],
    "conf_w_pw1": [1, D, 2 * D], "conf_b_pw1": [1, 2 * D], "conf_dw_w": [1, 31, D], "conf_dw_b": [1, D],
    "conf_ln_g": [1, D], "conf_ln_b": [1, D], "conf_w_pw2": [1, D, D], "conf_b_pw2": [1, D],
    "final_g": [D],
}
CONST_SHAPES = {"ident_f": [128, 128], "rope_cos": [128, SEQ], "rope_sin": [128, SEQ], "attn_w_perm": [D, 2 * D],
                "tri": [2, 128, 128], "maskneg": [2, 128, 128], "selA": [128, 32, 128]}


def build_program(steps, dbg=False):
    nc = bass.Bass("TRN2", target_bir_lowering=False)
    K = Ctx()
    K.nc = nc
    K.x_in = nc.dram_tensor("x", [2, SEQ, D], F32, kind="ExternalInput").ap()
    K.ctx_in = nc.dram_tensor("ctx", [2, CTX, D], F32, kind="ExternalInput").ap()
    K.c3 = nc.dram_tensor("c3", [3, D], F32, kind="ExternalInput").ap()
    K.w = {n: nc.dram_tensor(n, s, F32, kind="ExternalInput").ap() for n, s in W_SHAPES.items()}
    K.cst = {n: nc.dram_tensor(n, s, F32, kind="ExternalInput").ap() for n, s in CONST_SHAPES.items()}
    K.out = nc.dram_tensor("out", [2, SEQ, D], F32, kind="ExternalOutput").ap()
    skind = "ExternalOutput" if dbg else "Internal"
    K.xT_d = nc.dram_tensor("xT_d", [D, T], F32, kind=skind).ap()
    K.gT_d = nc.dram_tensor("gT_d", [FH, T], BF16, kind="Internal").ap()
    K.uT_d = nc.dram_tensor("uT_d", [D, T], BF16, kind="Internal").ap()
    K.qT_d = nc.dram_tensor("qT_d", [D, T], BF16, kind="Internal").ap()
    K.xbc_d = nc.dram_tensor("xbc_d", [4096, T], BF16, kind=skind).ap()
    K.z_d = nc.dram_tensor("z_d", [T, 2048], BF16, kind=skind).ap()
    K.dtla_d = nc.dram_tensor("dtla_d", [T, 128], F32, kind=skind).ap()
    K.y_d = nc.dram_tensor("y_d", [2, T, 2048], F32, kind=skind).ap()
    K.kT_d = nc.dram_tensor("kT_d", [D, T], BF16, kind="Internal").ap()
    K.oT_d = nc.dram_tensor("oT_d", [D, T], BF16, kind="Internal").ap()
    K.v_d = nc.dram_tensor("v_d", [T, D], BF16, kind="Internal").ap()
    layers = sorted({i for (_, i) in steps})
    with ExitStack() as st:
        K.st = st
        fw = K.fw = FW(nc, st)
        st.enter_context(nc.Block())
        K.psum = PsumPool([fw.ps(st, [128, 512], F32, f"bank{j}") for j in range(8)])
        K.ident = fw.sb(st, [128, 128], F32, "ident")
        K.ident_bf = fw.sb(st, [128, 128], BF16, "identb")
        K.ones_bf = fw.sb(st, [128, 128], BF16, "onesb")
        K.epsc = fw.sb(st, [128, 1], F32, "epsc")
        fw.sp.dma(K.ident[:, :], K.cst["ident_f"], writes=[K.ident])
        fw.dve.op(lambda e: e.tensor_copy(out=K.ident_bf[:, :], in_=K.ident[:, :]), reads=[K.ident], writes=[K.ident_bf])
        fw.pool.op(lambda e: e.memset(K.ones_bf[:, :], 1.0), writes=[K.ones_bf])
        fw.pool.op(lambda e: e.memset(K.epsc[:, :], EPS), writes=[K.epsc])
        K.onec = fw.sb(st, [128, 1], F32, "onec")
        fw.pool.op(lambda e: e.memset(K.onec[:, :], 1.0), writes=[K.onec])
        fw.barrier()
        phase_init(K)
        phase_mod(K, layers)
        last_ffn = max([n for n, (kind, _) in enumerate(steps) if kind == "ffn"], default=-1)
        K.marks = [("start", 0), ("init", 0)]
        K.marks.append(("mod_done", fw.pe.n_ins))
        for n, (kind, i) in enumerate(steps):
            need_ctx = i < 3
            K.marks.append((f"{kind}{i}", fw.pe.n_ins))
            if kind == "mix":
                MIXERS[i % 3](K, i, need_ctx)
            else:
                phase_ffn_up(K, i, need_ctx)
                K.marks.append((f"  FD{i}", fw.pe.n_ins))
                phase_ffn_down(K, i, need_ctx, final=(n == last_ffn))
        fw.barrier()
        K.marks.append(("end", fw.pe.n_ins))
        K.n_ins = {e.name: e.n_ins for e in fw.engs}
    return nc, K


def mixer_todo(K, i, need_ctx):
    raise NotImplementedError


MIXERS = {0: mixer_todo, 1: mixer_todo, 2: mixer_todo}


ROPE_PERM64 = np.concatenate([np.arange(16, 32), np.arange(0, 16), np.arange(48, 64), np.arange(32, 48)])


def host_consts():
    t = np.arange(SEQ)
    row, col = (t // 64).astype(np.float32), (t % 64).astype(np.float32)
    inv = (10000.0 ** (-np.arange(16, dtype=np.float32) / 16)).astype(np.float32)
    ang = np.stack([row[None, :] * inv[:, None], col[None, :] * inv[:, None]], axis=0)
    cos64 = np.zeros((64, SEQ), np.float32)
    sin64 = np.zeros((64, SEQ), np.float32)
    for ax in range(2):
        c, s = np.cos(ang[ax]), np.sin(ang[ax])
        cos64[ax * 32:ax * 32 + 16] = c
        cos64[ax * 32 + 16:ax * 32 + 32] = c
        sin64[ax * 32:ax * 32 + 16] = -s
        sin64[ax * 32 + 16:ax * 32 + 32] = s
    r = np.arange(128)
    tri = np.stack([(r[:, None] <= r[None, :]), (r[:, None] >= r[None, :])]).astype(np.float32)
    mneg = np.stack([np.where(r[None, :] < r[:, None], NEG, 0.0), np.where(r[None, :] > r[:, None], NEG, 0.0)]).astype(np.float32)
    selA = np.zeros((128, 32, 128), np.float32)
    for h in range(32):
        selA[h, h, :] = 1.0
    return {"tri": tri, "maskneg": mneg, "selA": selA, "ident_f": np.eye(128, dtype=np.float32),
            "rope_cos": np.ascontiguousarray(np.tile(cos64, (2, 1))), "rope_sin": np.ascontiguousarray(np.tile(sin64, (2, 1)))}


FULL_STEPS = [("mix", 0), ("ffn", 0), ("mix", 1), ("ffn", 1), ("mix", 2), ("ffn", 2), ("mix", 3), ("ffn", 3)]


def make_in_maps(inputs, ncores=NCORES):
    cst = host_consts()
    wa = np.asarray(inputs["attn_w_in"][0], dtype=np.float32)
    perm = (np.arange(2 * D) // 64) * 64 + ROPE_PERM64[np.arange(2 * D) % 64]
    cst["attn_w_perm"] = np.ascontiguousarray(wa[:, perm])
    maps = []
    wts = {n: np.ascontiguousarray(inputs[n], dtype=np.float32) for n in W_SHAPES}
    for cidx in range(ncores):
        b0 = 2 * cidx
        m = dict(wts)
        m.update(cst)
        m["x"] = np.ascontiguousarray(inputs["x"][b0:b0 + 2])
        m["ctx"] = np.ascontiguousarray(inputs["ctx"][b0:b0 + 2])
        m["c3"] = np.ascontiguousarray(np.concatenate([inputs["c"][b0:b0 + 2], inputs["c_ctx"][None, :]], axis=0))
        maps.append(m)
    return maps


_CACHE = {}


def kernel(**inputs):
    if "nc" not in _CACHE:
        _CACHE["nc"] = build_program(FULL_STEPS)[0]
    nc = _CACHE["nc"]
    maps = make_in_maps(inputs)
    res = run_bass_kernel_spmd(nc, maps, core_ids=list(range(NCORES)))
    return np.concatenate([r["out"] for r in res.results], axis=0).astype(np.float32)


def load_rowsT(K, cm, mat_ap, nrows, name="rT"):
    fw = K.fw
    dst = fw.sb(cm, [128, nrows], F32, name)
    with ExitStack() as st:
        tmp = fw.sb(st, [128, 128], F32, "vrow")
        for r0 in range(0, nrows, 128):
            n = min(128, nrows - r0)
            fw.sp.dma(tmp[0:n, :], mat_ap[r0:r0 + n, :], writes=[tmp])
            ps = K.psum.get()
            fw.pe.op(lambda e: e.transpose(out=ps[:, 0:n], in_=tmp[0:n, :], identity=K.ident[0:n, 0:n]),
                     reads=[tmp, K.ident], writes=[ps])
            fw.dve.op(lambda e: e.tensor_copy(out=dst[:, r0:r0 + n], in_=ps[:, 0:n]), reads=[ps], writes=[dst])
        fw.barrier()
    return dst


def vecT(K, cm, vec_ap, name="vT"):
    n = vec_ap.shape[0] // 128
    return load_rowsT(K, cm, vec_ap.rearrange("(c p) -> c p", p=128), n, name)


def mixer_conf(K, i, need_ctx):
    fw = K.fw
    xv = xT_view(K)
    uv = K.uT_d.rearrange("(c p) t -> p c t", p=128)
    blks = blocks(need_ctx)
    with ExitStack() as st:
        w1 = fw.sb(st, [128, 8, 2048], BF16, "w1")
        fw.pool.dma(w1[:, :, :], K.w["conf_w_pw1"][0].rearrange("(k p) n -> p k n", p=128), writes=[w1])
        b1 = vecT(K, st, K.w["conf_b_pw1"][0], "b1")
        W = alloc_norm_work(K, st)
        xin = [fw.sb(st, [128, 8, 512], F32, "xin") for _ in range(2)]
        hT = [fw.sb(st, [128, 8, 512], BF16, "hT") for _ in range(2)]
        sig = [fw.sb(st, [128, 512], F32, "sig") for _ in range(2)]
        ust = [fw.sb(st, [128, 8, 512], BF16, "ust") for _ in range(2)]
        def xload(bi):
            _, tn, Nn, _ = blks[bi]
            fw.sp.dma(xin[bi % 2][:, :, :Nn], xv[:, :, tn:tn + Nn], writes=[xin[bi % 2]])

        def nrm(bi):
            v_, _, N_, _ = blks[bi]
            norm_block(K, W, xin[bi % 2], N_, K.A1[i], K.modT[i][:, 0:8, :], v_, hT[bi % 2])

        xload(0)
        xload(1)
        nrm(0)
        for bi, (v, t0, N, isc) in enumerate(blks):
            if bi + 1 < len(blks):
                nrm(bi + 1)
            if bi + 2 < len(blks):
                xload(bi + 2)
            h, us = hT[bi % 2], ust[bi % 2]
            for c in range(8):
                ps1, ps2 = K.psum.get(), K.psum.get()
                for (ps, cc) in ((ps1, c), (ps2, c + 8)):
                    for k in range(8):
                        fw.pe.op(lambda e: e.matmul(ps[:, :N], lhsT=w1[:, k, cc * 128:(cc + 1) * 128], rhs=h[:, k, :N],
                                                    start=(k == 0), stop=(k == 7)),
                                 reads=[w1, h], writes=[ps], inc=(k == 7))
                sg = sig[c % 2]
                fw.act.op(lambda e: e.activation(out=sg[:, :N], in_=ps2[:, :N], func=AF.Sigmoid, bias=b1[:, c + 8:c + 9], scale=1.0),
                          reads=[ps2], writes=[sg])
                fw.dve.op(lambda e: e.scalar_tensor_tensor(out=us[:, c, :N], in0=ps1[:, :N], scalar=b1[:, c:c + 1], in1=sg[:, :N],
                                                           op0=ALU.add, op1=ALU.mult),
                          reads=[ps1, sg], writes=[us])
            fw.sp.dma(uv[:, :, t0:t0 + N], us[:, :, :N], reads=[us])
        fw.barrier()
    K.marks.append((f"  CB{i}", fw.pe.n_ins))
    with ExitStack() as st:
        dwT = load_rowsT(K, st, K.w["conf_dw_w"][0].rearrange("k (c p) -> (k c) p", p=128), 248, "dwT")
        dwb = vecT(K, st, K.w["conf_dw_b"][0], "dwb")
        lng = vecT(K, st, K.w["conf_ln_g"][0], "lng")
        lnb = vecT(K, st, K.w["conf_ln_b"][0], "lnb")
        b2 = vecT(K, st, K.w["conf_b_pw2"][0], "b2")
        gate = K.modT[i][:, 16:24, :]
        gb = fw.sb(st, [128, 8, 3], F32, "gb")
        fw.dve.op(lambda e: e.tensor_tensor(out=gb[:, :, :], in0=gate, in1=b2[:, :].unsqueeze(2).to_broadcast([128, 8, 3]), op=ALU.mult),
                  reads=[b2], writes=[gb])
        diag = fw.sb(st, [128, 8, 31, 128], BF16, "diag")
        n = 0
        for c in range(8):
            for kk in range(31):
                eng = fw.dve if n % 2 == 0 else fw.pool
                n += 1
                col = kk * 8 + c
                eng.op(lambda e: e.tensor_scalar(out=diag[:, c, kk, :], in0=K.ident[:, :], scalar1=dwT[:, col:col + 1], scalar2=None,
                                                 op0=ALU.mult),
                       reads=[dwT, K.ident], writes=[diag])
        w2 = fw.sb(st, [128, 8, D], BF16, "w2")
        fw.pool.dma(w2[:, :, :], K.w["conf_w_pw2"][0].rearrange("(k p) n -> p k n", p=128), writes=[w2])
        xin = [fw.sb(st, [128, 8, 512], F32, "xin") for _ in range(2)]
        uin = [fw.sb(st, [128, 8, 542], BF16, "uin") for _ in range(2)]
        vt = fw.sb(st, [128, 8, 512], F32, "vt")
        vb = fw.sb(st, [128, 8, 512], BF16, "vb")
        sq = fw.sb(st, [128, 8, 512], BF16, "sq")
        sT = fw.sb(st, [128, 8, 512], BF16, "sT")
        mean = fw.sb(st, [128, 512], F32, "mean")
        msq = fw.sb(st, [128, 512], F32, "msq")
        rstd = fw.sb(st, [128, 512], F32, "rstd")
        tmp = [fw.sb(st, [128, 512], F32, "ctmp") for _ in range(2)]

        def load(bi):
            v, t0, N, isc = blks[bi]
            s0 = (t0 // SEG) * SEG + (0 if isc else CTX)
            s1 = s0 + (CTX if isc else SEQ)
            lo, hi = max(t0 - 15, s0), min(t0 + N + 15, s1)
            ui = uin[bi % 2]
            if lo != t0 - 15 or hi != t0 + N + 15:
                fw.pool.op(lambda e: e.memset(ui[:, :, :], 0.0), writes=[ui])
            fw.sp.dma(ui[:, :, lo - (t0 - 15):hi - (t0 - 15)], uv[:, :, lo:hi], writes=[ui])
            fw.sp.dma(xin[bi % 2][:, :, :N], xv[:, :, t0:t0 + N], writes=[xin[bi % 2]])

        load(0)
        for bi, (v, t0, N, isc) in enumerate(blks):
            if bi + 1 < len(blks):
                load(bi + 1)
            xi, ui = xin[bi % 2], uin[bi % 2]
            for c in range(8):
                ps = K.psum.get()
                for kk in range(31):
                    fw.pe.op(lambda e: e.matmul(ps[:, :N], lhsT=diag[:, c, kk, :], rhs=ui[:, c, kk:kk + N], start=(kk == 0), stop=(kk == 30)),
                             reads=[diag, ui], writes=[ps], inc=(kk == 30))
                fw.act.op(lambda e: e.activation(out=vt[:, c, :N], in_=ps[:, :N], func=AF.Identity, bias=dwb[:, c:c + 1], scale=1.0),
                          reads=[ps], writes=[vt])
            fw.pool.op(lambda e: e.tensor_copy(out=vb[:, :, :N], in_=vt[:, :, :N]), reads=[vt], writes=[vb])
            fw.act.op(lambda e: e.activation(out=sq[:, :, :N], in_=vt[:, :, :N], func=AF.Square), reads=[vt], writes=[sq])
            p1, p2 = K.psum.get(), K.psum.get()
            for (ps, src) in ((p1, vb), (p2, sq)):
                for k in range(8):
                    fw.pe.op(lambda e: e.matmul(ps[:, :N], lhsT=K.ones_bf[:, :], rhs=src[:, k, :N], start=(k == 0), stop=(k == 7)),
                             reads=[src, K.ones_bf], writes=[ps], inc=(k == 7))
            fw.dve.op(lambda e: e.tensor_scalar(out=mean[:, :N], in0=p1[:, :N], scalar1=1.0 / D, scalar2=None, op0=ALU.mult),
                      reads=[p1], writes=[mean])
            fw.dve.op(lambda e: e.tensor_tensor(out=msq[:, :N], in0=mean[:, :N], in1=mean[:, :N], op=ALU.mult), reads=[mean], writes=[msq])
            fw.dve.op(lambda e: e.scalar_tensor_tensor(out=rstd[:, :N], in0=p2[:, :N], scalar=1.0 / D, in1=msq[:, :N],
                                                       op0=ALU.mult, op1=ALU.subtract),
                      reads=[p2, msq], writes=[rstd])
            fw.act.op(lambda e: e.activation(out=rstd[:, :N], in_=rstd[:, :N], func=AF.Ln, bias=K.epsc[:, 0:1], scale=1.0),
                      reads=[rstd], writes=[rstd])
            fw.act.op(lambda e: e.activation(out=rstd[:, :N], in_=rstd[:, :N], func=AF.Exp, scale=-0.5), reads=[rstd], writes=[rstd])
            for k in range(8):
                ta, tb = tmp[0], tmp[1]
                fw.pool.op(lambda e: e.tensor_tensor(out=ta[:, :N], in0=vt[:, k, :N], in1=mean[:, :N], op=ALU.subtract),
                           reads=[vt, mean], writes=[ta])
                fw.dve.op(lambda e: e.scalar_tensor_tensor(out=tb[:, :N], in0=ta[:, :N], scalar=lng[:, k:k + 1], in1=rstd[:, :N],
                                                           op0=ALU.mult, op1=ALU.mult),
                          reads=[ta, rstd], writes=[tb])
                fw.act.op(lambda e: e.activation(out=sT[:, k, :N], in_=tb[:, :N], func=AF.Silu, bias=lnb[:, k:k + 1], scale=1.0),
                          reads=[tb], writes=[sT])
            for dc in range(8):
                ps = K.psum.get()
                for k in range(8):
                    fw.pe.op(lambda e: e.matmul(ps[:, :N], lhsT=w2[:, k, dc * 128:(dc + 1) * 128], rhs=sT[:, k, :N], start=(k == 0), stop=(k == 7)),
                             reads=[w2, sT], writes=[ps], inc=(k == 7))
                fw.dve.op(lambda e: e.scalar_tensor_tensor(out=xi[:, dc, :N], in0=ps[:, :N], scalar=gate[:, dc, v:v + 1], in1=xi[:, dc, :N],
                                                           op0=ALU.mult, op1=ALU.add),
                          reads=[ps, xi], writes=[xi])
                fw.pool.op(lambda e: e.tensor_scalar(out=xi[:, dc, :N], in0=xi[:, dc, :N], scalar1=gb[:, dc, v:v + 1], scalar2=None, op0=ALU.add),
                           reads=[xi, gb], writes=[xi])
            fw.sp.dma(xv[:, :, t0:t0 + N], xi[:, :, :N], reads=[xi])
        fw.barrier()


MIXERS[2] = mixer_conf


def mixer_attn(K, i, need_ctx):
    fw = K.fw
    xv = xT_view(K)
    qv = K.qT_d.rearrange("(c p) t -> p c t", p=128)
    kv = K.kT_d.rearrange("(c p) t -> p c t", p=128)
    ov = K.oT_d.rearrange("(c p) t -> p c t", p=128)
    vv = K.v_d.rearrange("(s p) d -> p s d", p=128)
    lam_init = 0.8 - 0.6 * math.exp(-0.3 * i)
    with ExitStack() as st:
        wq = fw.sb(st, [128, 8, 3072], BF16, "wq")
        wp = fw.sb(st, [128, 8, 2048], BF16, "wp")
        fw.pool.dma(wq[:, :, :], K.w["attn_w_in"][0].rearrange("(k p) n -> p k n", p=128), writes=[wq])
        fw.pool.dma(wp[:, :, :], K.cst["attn_w_perm"].rearrange("(k p) n -> p k n", p=128), writes=[wp])
        cosT = fw.sb(st, [128, SEQ], F32, "cosT")
        sinT = fw.sb(st, [128, SEQ], F32, "sinT")
        fw.sp.dma(cosT[:, :], K.cst["rope_cos"], writes=[cosT])
        fw.sp.dma(sinT[:, :], K.cst["rope_sin"], writes=[sinT])
        W = alloc_norm_work(K, st)
        xin = [fw.sb(st, [128, 8, 512], F32, "xin") for _ in range(2)]
        hTs = [fw.sb(st, [128, 8, 512], BF16, "hT") for _ in range(2)]
        qst = fw.sb(st, [128, 8, 512], BF16, "qst")
        kst = fw.sb(st, [128, 8, 512], BF16, "kst")
        vst = fw.sb(st, [128, 4, D], BF16, "vst")
        t1 = [fw.sb(st, [128, 512], F32, "rt1") for _ in range(2)]
        t2 = [fw.sb(st, [128, 512], F32, "rt2") for _ in range(2)]
        blks = blocks(True)

        def xload(bi):
            _, tn, Nn, _ = blks[bi]
            fw.sp.dma(xin[bi % 2][:, :, :Nn], xv[:, :, tn:tn + Nn], writes=[xin[bi % 2]])

        def nrm(bi):
            v_, _, N_, _ = blks[bi]
            norm_block(K, W, xin[bi % 2], N_, K.A1[i], K.modT[i][:, 0:8, :], v_, hTs[bi % 2])

        xload(0)
        xload(1)
        nrm(0)
        n = 0
        for bi, (v, t0, N, isc) in enumerate(blks):
            hT = hTs[bi % 2]
            if bi + 1 < len(blks):
                nrm(bi + 1)
            if bi + 2 < len(blks):
                xload(bi + 2)
            tl = t0 - (t0 // SEG) * SEG - CTX
            for c in range(8):
                for (cb, stg) in ((0, qst), (1024, kst)):
                    psA = K.psum.get()
                    for k in range(8):
                        fw.pe.op(lambda e: e.matmul(psA[:, :N], lhsT=wq[:, k, cb + c * 128:cb + (c + 1) * 128], rhs=hT[:, k, :N],
                                                    start=(k == 0), stop=(k == 7)),
                                 reads=[wq, hT], writes=[psA], inc=(k == 7))
                    if isc:
                        fw.act.op(lambda e: e.activation(out=stg[:, c, :N], in_=psA[:, :N], func=AF.Identity), reads=[psA], writes=[stg])
                        continue
                    psB = K.psum.get()
                    for k in range(8):
                        fw.pe.op(lambda e: e.matmul(psB[:, :N], lhsT=wp[:, k, cb + c * 128:cb + (c + 1) * 128], rhs=hT[:, k, :N],
                                                    start=(k == 0), stop=(k == 7)),
                                 reads=[wp, hT], writes=[psB], inc=(k == 7))
                    a, b_ = t1[n % 2], t2[n % 2]
                    n += 1
                    fw.dve.op(lambda e: e.tensor_tensor(out=a[:, :N], in0=psA[:, :N], in1=cosT[:, tl:tl + N], op=ALU.mult),
                              reads=[psA, cosT], writes=[a])
                    fw.dve.op(lambda e: e.tensor_tensor(out=b_[:, :N], in0=psB[:, :N], in1=sinT[:, tl:tl + N], op=ALU.mult),
                              reads=[psB, sinT], writes=[b_])
                    fw.pool.op(lambda e: e.tensor_tensor(out=stg[:, c, :N], in0=a[:, :N], in1=b_[:, :N], op=ALU.add),
                               reads=[a, b_], writes=[stg])
            for sub in range(N // 128):
                for half in range(2):
                    ps = K.psum.get()
                    for k in range(8):
                        fw.pe.op(lambda e: e.matmul(ps[:, :], lhsT=hT[:, k, sub * 128:(sub + 1) * 128],
                                                    rhs=wq[:, k, 2048 + half * 512:2048 + (half + 1) * 512], start=(k == 0), stop=(k == 7)),
                                 reads=[wq, hT], writes=[ps], inc=(k == 7))
                    fw.act.op(lambda e: e.activation(out=vst[:, sub, half * 512:(half + 1) * 512], in_=ps[:, :], func=AF.Identity),
                              reads=[ps], writes=[vst])
            fw.sp.dma(qv[:, :, t0:t0 + N], qst[:, :, :N], reads=[qst])
            fw.sp.dma(kv[:, :, t0:t0 + N], kst[:, :, :N], reads=[kst])
            fw.sp.dma(vv[:, t0 // 128:(t0 + N) // 128, :], vst[:, :N // 128, :], reads=[vst])
        fw.barrier()
    K.marks.append((f"  AB{i}", fw.pe.n_ins))
    with ExitStack() as st:
        hsel = [fw.sb(st, [128, 128], BF16, "hsel") for _ in range(2)]
        for e_ in range(2):
            fw.pool.op(lambda e: e.memset(hsel[e_][:, :], 0.0), writes=[hsel[e_]])
            fw.pool.op(lambda e: e.memset(hsel[e_][e_ * 64:(e_ + 1) * 64, :], 1.0), writes=[hsel[e_]])
        lp = fw.sb(st, [1, 256], F32, "lp")
        lpp = fw.sb(st, [1, 128], F32, "lpp")
        ls = fw.sb(st, [1, 4], F32, "ls")
        ones1 = fw.sb(st, [1, 128], F32, "ones1")
        neglam = fw.sb(st, [128, 1], F32, "neglam")
        gbc = fw.sb(st, [128, 128], F32, "gbc")
        fw.sp.dma(lp[:, :], K.w["attn_lambda"][0].rearrange("(o a) d -> o (a d)", o=1), writes=[lp])
        fw.sp.dma(gbc[:, :], K.w["attn_norm_g"][0].partition_broadcast(128), writes=[gbc])
        fw.pool.op(lambda e: e.memset(ones1[:, :], 1.0), writes=[ones1])
        fw.dve.op(lambda e: e.tensor_tensor(out=lpp[:, :].rearrange("o (a d) -> o a d", a=2),
                                            in0=lp[:, :].rearrange("o (a b d) -> o a b d", a=2, b=2)[:, :, 0, :],
                                            in1=lp[:, :].rearrange("o (a b d) -> o a b d", a=2, b=2)[:, :, 1, :], op=ALU.mult),
                  reads=[lp], writes=[lpp])
        fw.dve.op(lambda e: e.tensor_reduce(out=ls[:, 0:2], in_=lpp[:, :].rearrange("o (a d) -> o a d", a=2), axis=AX.X, op=ALU.add),
                  reads=[lpp], writes=[ls])
        fw.act.op(lambda e: e.activation(out=ls[:, 0:2], in_=ls[:, 0:2], func=AF.Exp), reads=[ls], writes=[ls])
        fw.dve.op(lambda e: e.tensor_tensor(out=ls[:, 2:3], in0=ls[:, 1:2], in1=ls[:, 0:1], op=ALU.subtract), reads=[ls], writes=[ls])
        fw.dve.op(lambda e: e.tensor_scalar(out=ls[:, 3:4], in0=ls[:, 2:3], scalar1=-lam_init, scalar2=None, op0=ALU.add),
                  reads=[ls], writes=[ls])
        ps = K.psum.get()
        fw.pe.op(lambda e: e.matmul(ps[:, 0:1], lhsT=ones1[:, :], rhs=ls[:, 3:4], start=True, stop=True), reads=[ones1, ls], writes=[ps])
        fw.dve.op(lambda e: e.tensor_copy(out=neglam[:, :], in_=ps[:, 0:1]), reads=[ps], writes=[neglam])
        fw.act.op(lambda e: e.activation(out=gbc[:, :], in_=gbc[:, :], func=AF.Identity, scale=(1.0 - lam_init)), reads=[gbc], writes=[gbc])

        kz = [[fw.sb(st, [128, SEG], BF16, "kz") for _ in range(2)] for _ in range(2)]
        qh = [fw.sb(st, [128, SEG], BF16, "qh") for _ in range(2)]
        va = [fw.sb(st, [128, 18, 129], BF16, "va") for _ in range(2)]
        for p_ in range(2):
            for e_ in range(2):
                fw.pool.op(lambda e: e.memset(kz[p_][e_][:, :], 0.0), writes=[kz[p_][e_]])
            fw.pool.op(lambda e: e.memset(va[p_][:, :, 128:129], 1.0), writes=[va[p_]])
        sqt = fw.sb(st, [128, SEG], BF16, "sqt")
        mx = fw.sb(st, [128, 4, 5], F32, "mx")
        mm = fw.sb(st, [128, 4], F32, "mm")
        negb = fw.sb(st, [128, 2], F32, "negb")
        E = [[fw.sb(st, [128, 18, 512], BF16, "E") for _ in range(2)] for _ in range(2)]
        rss = [fw.sb(st, [128, 2], F32, "rs") for _ in range(2)]
        tts = [fw.sb(st, [128, 128], F32, "tt") for _ in range(2)]
        os_ = [fw.sb(st, [128, 128], F32, "o") for _ in range(2)]
        junk = fw.sb(st, [128, 128], F32, "junk")
        sss = [fw.sb(st, [128, 1], F32, "ss") for _ in range(2)]
        ons = [fw.sb(st, [128, 128], BF16, "on") for _ in range(2)]
        oTst = [fw.sb(st, [128, 512], BF16, "oTst") for _ in range(3)]
        pend = []
        pp = [0]
        cblk = [(0, 512), (512, 512), (1024, 512), (1536, 512), (2048, 256)]

        def loads(it):
            b, hd = it // 8, it % 8
            p_ = it % 2
            s0 = b * SEG
            fw.sp.dma(kz[p_][0][0:64, :], kv[0:64, hd, s0:s0 + SEG], writes=[kz[p_][0]])
            fw.sp.dma(kz[p_][1][64:128, :], kv[64:128, hd, s0:s0 + SEG], writes=[kz[p_][1]])
            fw.sp.dma(qh[p_][:, :], qv[:, hd, s0:s0 + SEG], writes=[qh[p_]])
            fw.sp.dma(va[p_][:, :, 0:128], vv[:, s0 // 128:s0 // 128 + 18, hd * 128:(hd + 1) * 128], writes=[va[p_]])

        loads(0)
        oi = 0
        for it in range(16):
            if it + 1 < 16:
                loads(it + 1)
            b, hd = it // 8, it % 8
            p_ = it % 2
            s0 = b * SEG
            kz0, kz1, q_, v_ = kz[p_][0], kz[p_][1], qh[p_], va[p_]
            for (src, lh, col0) in ((q_, hsel, 0), (kz0, None, 2), (kz1, None, 3)):
                fw.act.op(lambda e: e.activation(out=sqt[:, :], in_=src[:, :], func=AF.Square), reads=[src], writes=[sqt])
                for e_ in (range(2) if lh is not None else range(1)):
                    for bj, (c0, cn) in enumerate(cblk):
                        ps = K.psum.get()
                        lhs = lh[e_] if lh is not None else K.ones_bf
                        fw.pe.op(lambda e: e.matmul(ps[:, :cn], lhsT=lhs[:, :], rhs=sqt[:, c0:c0 + cn], start=True, stop=True),
                                 reads=[lhs, sqt], writes=[ps])
                        fw.dve.op(lambda e: e.tensor_reduce(out=mx[:, col0 + e_, bj:bj + 1], in_=ps[:, :cn], axis=AX.X, op=ALU.max),
                                  reads=[ps], writes=[mx])
            fw.dve.op(lambda e: e.tensor_reduce(out=mm[:, :], in_=mx[:, :, :], axis=AX.X, op=ALU.max), reads=[mx], writes=[mm])
            fw.dve.op(lambda e: e.tensor_tensor(out=negb[:, :], in0=mm[:, 0:2], in1=mm[:, 2:4], op=ALU.mult), reads=[mm], writes=[negb])
            fw.act.op(lambda e: e.activation(out=negb[:, :], in_=negb[:, :], func=AF.Sqrt), reads=[negb], writes=[negb])
            fw.dve.op(lambda e: e.tensor_scalar(out=negb[:, :], in0=negb[:, :], scalar1=-0.125 * 1.02, scalar2=None, op0=ALU.mult),
                      reads=[negb], writes=[negb])
            qbs = ([(0, CTX, 2)] if need_ctx else []) + [(CTX + 512 * n_, 512, 18) for n_ in range(4)]

            def stage1(qi):
                q0, nq, nkt = qbs[qi]
                for e_ in range(2):
                    kz_e = kz0 if e_ == 0 else kz1
                    Eb = E[qi % 2][e_]
                    for j in range(nkt):
                        ps = K.psum.get()
                        fw.pe.op(lambda e: e.matmul(ps[:, :nq], lhsT=kz_e[:, j * 128:(j + 1) * 128], rhs=q_[:, q0:q0 + nq], start=True, stop=True),
                                 reads=[kz_e, q_], writes=[ps])
                        fw.act.op(lambda e: e.activation(out=Eb[:, j, :nq], in_=ps[:, :nq], func=AF.Exp, bias=negb[:, e_:e_ + 1], scale=0.125),
                                  reads=[ps, negb], writes=[Eb])

            def stage2(qi):
                nonlocal oi
                q0, nq, nkt = qbs[qi]
                ost = oTst[oi % 3]
                oi += 1
                ntile = nq // 128
                for i_ in range(ntile):
                    par = pp[0] % 2
                    pp[0] += 1
                    rs, tt, o_, ss, on = rss[par], tts[par], os_[par], sss[par], ons[par]
                    ps = K.psum.get()
                    for e_ in range(2):
                        Eb = E[qi % 2][e_]
                        for j in range(nkt):
                            fw.pe.op(lambda e: e.matmul(ps[:, e_ * 129:(e_ + 1) * 129], lhsT=Eb[:, j, i_ * 128:(i_ + 1) * 128], rhs=v_[:, j, :],
                                                        start=(j == 0), stop=(j == nkt - 1)),
                                     reads=[Eb, v_], writes=[ps], inc=(j == nkt - 1))
                    if pend:
                        pend.pop()()
                    fw.dve.op(lambda e: e.reciprocal(out=rs[:, 0:2], in_=ps[:, 128:258:129]), reads=[ps], writes=[rs])
                    fw.dve.op(lambda e: e.tensor_scalar(out=tt[:, :], in0=ps[:, 129:257], scalar1=rs[:, 1:2], scalar2=neglam[:, 0:1],
                                                        op0=ALU.mult, op1=ALU.mult),
                              reads=[ps, rs, neglam], writes=[tt])
                    fw.dve.op(lambda e: e.scalar_tensor_tensor(out=o_[:, :], in0=ps[:, 0:128], scalar=rs[:, 0:1], in1=tt[:, :],
                                                               op0=ALU.mult, op1=ALU.add),
                              reads=[ps, rs, tt], writes=[o_])
                    fw.act.op(lambda e: e.activation(out=junk[:, :], in_=o_[:, :], func=AF.Square, accum_out=ss[:, 0:1]),
                              reads=[o_], writes=[junk, ss])
                    fw.act.op(lambda e: e.activation(out=ss[:, :], in_=ss[:, :], func=AF.Ln, bias=K.epsc[:, 0:1], scale=1.0 / 128),
                              reads=[ss], writes=[ss])
                    fw.act.op(lambda e: e.activation(out=ss[:, :], in_=ss[:, :], func=AF.Exp, scale=-0.5), reads=[ss], writes=[ss])
                    fw.dve.op(lambda e: e.scalar_tensor_tensor(out=on[:, :], in0=o_[:, :], scalar=ss[:, 0:1], in1=gbc[:, :],
                                                               op0=ALU.mult, op1=ALU.mult),
                              reads=[o_, ss, gbc], writes=[on])

                    def fin(on=on, ost=ost, i_=i_, last=(i_ == ntile - 1), q0=q0, nq=nq, hd=hd, s0=s0):
                        ps2 = K.psum.get()
                        pv = ps2[:, :].bitcast(BF16)
                        fw.pe.op(lambda e: e.transpose(out=pv[:, 0:128], in_=on[:, :], identity=K.ident_bf[:, :]),
                                 reads=[on, K.ident_bf], writes=[ps2])
                        fw.act.op(lambda e: e.activation(out=ost[:, i_ * 128:(i_ + 1) * 128], in_=pv[:, 0:128], func=AF.Identity),
                                  reads=[ps2], writes=[ost])
                        if last:
                            fw.sp.dma(ov[:, hd, s0 + q0:s0 + q0 + nq], ost[:, :nq], reads=[ost])
                    pend.append(fin)

            stage1(0)
            for qi in range(len(qbs)):
                if qi + 1 < len(qbs):
                    stage1(qi + 1)
                stage2(qi)
        while pend:
            pend.pop()()
        fw.barrier()
    K.marks.append((f"  AC{i}", fw.pe.n_ins))
    with ExitStack() as st:
        wo = fw.sb(st, [128, 8, D], BF16, "wo")
        fw.pool.dma(wo[:, :, :], K.w["attn_w_out"][0].rearrange("(k p) n -> p k n", p=128), writes=[wo])
        xin = [fw.sb(st, [128, 8, 512], F32, "xin") for _ in range(2)]
        oin = [fw.sb(st, [128, 8, 512], BF16, "oin") for _ in range(2)]
        gate = K.modT[i][:, 16:24, :]
        blks = blocks(need_ctx)

        def load(bi):
            v, t0, N, isc = blks[bi]
            fw.sp.dma(xin[bi % 2][:, :, :N], xv[:, :, t0:t0 + N], writes=[xin[bi % 2]])
            fw.sp.dma(oin[bi % 2][:, :, :N], ov[:, :, t0:t0 + N], writes=[oin[bi % 2]])

        load(0)
        for bi, (v, t0, N, isc) in enumerate(blks):
            if bi + 1 < len(blks):
                load(bi + 1)
            xi, oi_ = xin[bi % 2], oin[bi % 2]
            for dc in range(8):
                ps = K.psum.get()
                for k in range(8):
                    fw.pe.op(lambda e: e.matmul(ps[:, :N], lhsT=wo[:, k, dc * 128:(dc + 1) * 128], rhs=oi_[:, k, :N], start=(k == 0), stop=(k == 7)),
                             reads=[wo, oi_], writes=[ps], inc=(k == 7))
                fw.dve.op(lambda e: e.scalar_tensor_tensor(out=xi[:, dc, :N], in0=ps[:, :N], scalar=gate[:, dc, v:v + 1], in1=xi[:, dc, :N],
                                                           op0=ALU.mult, op1=ALU.add),
                          reads=[ps, xi], writes=[xi])
            fw.sp.dma(xv[:, :, t0:t0 + N], xi[:, :, :N], reads=[xi])
        fw.barrier()


MIXERS[1] = mixer_attn


def mixer_mamba(K, i, need_ctx):
    fw = K.fw
    j = i // 3
    xv = xT_view(K)
    xbcv = K.xbc_d.rearrange("(c p) t -> p c t", p=128)
    zv = K.z_d
    with ExitStack() as st:
        w = fw.sb(st, [128, 8, 6208], BF16, "win")
        fw.pool.dma(w[:, :, :], K.w["ssm_w_in"][j].rearrange("(k p) n -> p k n", p=128), writes=[w])
        dtb = fw.sb(st, [128, 64], F32, "dtb")
        aneg = fw.sb(st, [128, 64], F32, "aneg")
        fw.sp.dma(dtb[:, :], K.w["ssm_dt_bias"][j].rearrange("a h -> (a h)").partition_broadcast(128), writes=[dtb])
        fw.sp.dma(aneg[:, :], K.w["ssm_a_log"][j].rearrange("a h -> (a h)").partition_broadcast(128), writes=[aneg])
        fw.act.op(lambda e: e.activation(out=aneg[:, :], in_=aneg[:, :], func=AF.Exp), reads=[aneg], writes=[aneg])
        fw.dve.op(lambda e: e.tensor_scalar(out=aneg[:, :], in0=aneg[:, :], scalar1=-1.0, scalar2=None, op0=ALU.mult), reads=[aneg], writes=[aneg])
        W = alloc_norm_work(K, st)
        xin = [fw.sb(st, [128, 8, 512], F32, "xin") for _ in range(2)]
        hTs = [fw.sb(st, [128, 8, 512], BF16, "hT") for _ in range(2)]
        xst = [fw.sb(st, [128, 4, 512], BF16, "xst") for _ in range(2)]
        zst = [fw.sb(st, [128, 2048], BF16, "zst") for _ in range(2)]
        dtl = [fw.sb(st, [128, 128], F32, "dtl") for _ in range(2)]
        dtmp = fw.sb(st, [128, 64], F32, "dtmp")
        blks = blocks(True)

        def xload(bi):
            _, tn, Nn, _ = blks[bi]
            fw.sp.dma(xin[bi % 2][:, :, :Nn], xv[:, :, tn:tn + Nn], writes=[xin[bi % 2]])

        def nrm(bi):
            v_, _, N_, _ = blks[bi]
            norm_block(K, W, xin[bi % 2], N_, K.A1[i], K.modT[i][:, 0:8, :], v_, hTs[bi % 2])

        xload(0)
        xload(1)
        nrm(0)
        zi = 0
        for bi, (v, t0, N, isc) in enumerate(blks):
            hT = hTs[bi % 2]
            if bi + 1 < len(blks):
                nrm(bi + 1)
            if bi + 2 < len(blks):
                xload(bi + 2)
            for cc in range(32):
                xs_ = xst[(cc // 4) % 2]
                ps = K.psum.get()
                for k in range(8):
                    fw.pe.op(lambda e: e.matmul(ps[:, :N], lhsT=w[:, k, 2048 + cc * 128:2048 + (cc + 1) * 128], rhs=hT[:, k, :N],
                                                start=(k == 0), stop=(k == 7)),
                             reads=[w, hT], writes=[ps], inc=(k == 7))
                if cc % 2 == 0:
                    fw.act.op(lambda e: e.activation(out=xs_[:, cc % 4, :N], in_=ps[:, :N], func=AF.Identity), reads=[ps], writes=[xs_])
                else:
                    fw.dve.op(lambda e: e.tensor_copy(out=xs_[:, cc % 4, :N], in_=ps[:, :N]), reads=[ps], writes=[xs_])
                if cc % 4 == 3:
                    fw.sp.dma(xbcv[:, cc - 3:cc + 1, t0:t0 + N], xs_[:, :, :N], reads=[xs_])
            for sub in range(N // 128):
                zs_ = zst[zi % 2]
                dl = dtl[zi % 2]
                zi += 1
                for q4 in range(4):
                    ps = K.psum.get()
                    for k in range(8):
                        fw.pe.op(lambda e: e.matmul(ps[:, :], lhsT=hT[:, k, sub * 128:(sub + 1) * 128], rhs=w[:, k, q4 * 512:(q4 + 1) * 512],
                                                    start=(k == 0), stop=(k == 7)),
                                 reads=[w, hT], writes=[ps], inc=(k == 7))
                    fw.act.op(lambda e: e.activation(out=zs_[:, q4 * 512:(q4 + 1) * 512], in_=ps[:, :], func=AF.Silu), reads=[ps], writes=[zs_])
                ps = K.psum.get()
                for k in range(8):
                    fw.pe.op(lambda e: e.matmul(ps[:, 0:64], lhsT=hT[:, k, sub * 128:(sub + 1) * 128], rhs=w[:, k, 6144:6208],
                                                start=(k == 0), stop=(k == 7)),
                             reads=[w, hT], writes=[ps], inc=(k == 7))
                fw.dve.op(lambda e: e.tensor_tensor(out=dtmp[:, :], in0=ps[:, 0:64], in1=dtb[:, :], op=ALU.add), reads=[ps, dtb], writes=[dtmp])
                fw.act.op(lambda e: e.activation(out=dtmp[:, :], in_=dtmp[:, :], func=AF.Exp), reads=[dtmp], writes=[dtmp])
                fw.act.op(lambda e: e.activation(out=dl[:, 0:64], in_=dtmp[:, :], func=AF.Ln, bias=K.onec[:, 0:1], scale=1.0), reads=[dtmp], writes=[dl])
                fw.dve.op(lambda e: e.tensor_tensor(out=dl[:, 64:128], in0=dl[:, 0:64], in1=aneg[:, :], op=ALU.mult), reads=[dl, aneg], writes=[dl])
                ts = t0 + sub * 128
                fw.sp.dma(zv[ts:ts + 128, :], zs_[:, :], reads=[zs_])
                fw.sp.dma(K.dtla_d[ts:ts + 128, :], dl[:, :], reads=[dl])
        fw.barrier()
    K.marks.append((f"  MB{i}", fw.pe.n_ins))
    with ExitStack() as st:
        tri = [fw.sb(st, [128, 128], F32, "tri") for _ in range(2)]
        mneg = [fw.sb(st, [128, 128], BF16, "mneg") for _ in range(2)]
        selA = fw.sb(st, [128, 32, 128], BF16, "selA")
        onesf = fw.sb(st, [128, 128], F32, "onesf")
        fw.pool.op(lambda e: e.memset(onesf[:, :], 1.0), writes=[onesf])
        for dr in range(2):
            fw.sp.dma(tri[dr][:, :], K.cst["tri"][dr], writes=[tri[dr]])
            fw.pool.dma(mneg[dr][:, :], K.cst["maskneg"][dr], writes=[mneg[dr]])
        fw.pool.dma(selA[:, :, :], K.cst["selA"], writes=[selA])
        diag, cbT, Dbc = [], [], []
        for dr in range(2):
            cwT = load_rowsT(K, st, K.w["ssm_conv_w"][j, dr].rearrange("k (c p) -> (k c) p", p=128), 128, "cwT")
            cbT.append(vecT(K, st, K.w["ssm_conv_b"][j, dr], "cbT"))
            dg = fw.sb(st, [128, 32, 4, 128], BF16, "sdiag")
            n = 0
            for cc in range(32):
                for kk in range(4):
                    eng = fw.dve if n % 2 == 0 else fw.pool
                    n += 1
                    col = kk * 32 + cc
                    eng.op(lambda e: e.tensor_scalar(out=dg[:, cc, kk, :], in0=K.ident[:, :], scalar1=cwT[:, col:col + 1], scalar2=None, op0=ALU.mult),
                           reads=[cwT, K.ident], writes=[dg])
            diag.append(dg)
            db = fw.sb(st, [128, 32], F32, "Dbc")
            fw.sp.dma(db[:, :], K.w["ssm_d"][j, dr].partition_broadcast(128), writes=[db])
            Dbc.append(db)
        xbi = [fw.sb(st, [128, 32, 131], BF16, "xbi") for _ in range(2)]
        dli = [fw.sb(st, [128, 128], F32, "dli") for _ in range(2)]

        def two(shape, dt_, nm):
            return [fw.sb(st, shape, dt_, nm) for _ in range(2)]
        uTs, xss, xrs, xds = two([128, 32, 128], BF16, "uT"), two([128, 2048], BF16, "xs"), two([128, 2048], BF16, "xr"), \
            two([128, 2048], BF16, "xd")
        Btoks = two([128, 8, 128], BF16, "Btok")
        acss, dsbs, dtes, expacs, decays, w2s = (two([128, 32], F32, nm) for nm in ("acs", "dsb", "dte", "expac", "decay", "w2"))
        aTs = [[fw.sb(st, [128, 128], BF16, "aT") for _ in range(4)] for _ in range(2)]
        hifs = two([32, 128], F32, "hif")
        for aT in aTs:
            for a_ in aT:
                fw.pool.op(lambda e: e.memset(a_[:, :], 0.0), writes=[a_])
        Lexp = [fw.sb(st, [128, 512], F32, "Lexp") for _ in range(2)]
        G = [fw.sb(st, [128, 4, 128], BF16, "G") for _ in range(2)]
        yt = [fw.sb(st, [128, 256], F32, "yt") for _ in range(2)]
        yt2 = [fw.sb(st, [128, 256], F32, "yt2") for _ in range(2)]
        ych = [fw.sb(st, [128, 2048], F32, "ych") for _ in range(2)]
        hst = [fw.sb(st, [128, 256], F32, "hst") for _ in range(8)]
        hb = [fw.sb(st, [128, 256], BF16, "hb") for _ in range(8)]

        pa_bank, pcb_bank = K.psum.bufs[0], K.psum.bufs[1]
        rot = PsumPool(K.psum.bufs[2:])
        sched = []
        for b in range(2):
            for dr in range(2):
                cl = [(b * SEG + c * 128, b * SEG, b * SEG + CTX) for c in range(2)] + \
                     [(b * SEG + CTX + c * 128, b * SEG + CTX, (b + 1) * SEG) for c in range(16)]
                if dr == 1:
                    cl = cl[0:2][::-1] + cl[2:][::-1]
                for ci, (t0, s0, s1) in enumerate(cl):
                    sched.append((b, dr, t0, s0, s1, ci == 0))
        NCH = len(sched)

        def load(n):
            b, dr, t0, s0, s1, first = sched[n]
            xb, dl = xbi[n % 2], dli[n % 2]
            if dr == 0:
                if t0 == s0:
                    fw.pool.op(lambda e: e.memset(xb[:, :, 0:3], 0.0), writes=[xb])
                    fw.sp.dma(xb[:, :, 3:131], xbcv[:, :, t0:t0 + 128], writes=[xb])
                else:
                    fw.sp.dma(xb[:, :, 0:131], xbcv[:, :, t0 - 3:t0 + 128], writes=[xb])
            else:
                if t0 + 128 == s1:
                    fw.pool.op(lambda e: e.memset(xb[:, :, 128:131], 0.0), writes=[xb])
                    fw.sp.dma(xb[:, :, 0:128], xbcv[:, :, t0:t0 + 128], writes=[xb])
                else:
                    fw.sp.dma(xb[:, :, 0:131], xbcv[:, :, t0:t0 + 131], writes=[xb])
            fw.sp.dma(dl[:, :], K.dtla_d[t0:t0 + 128, :], writes=[dl])

        def P1(n):
            b, dr, t0, s0, s1, first = sched[n]
            p = n % 2
            xb, dl = xbi[p], dli[p]
            acs, dsb, dte, expac, decay, w2, aT, hif, uT = acss[p], dsbs[p], dtes[p], expacs[p], decays[p], w2s[p], aTs[p], hifs[p], uTs[p]
            dt = dl[:, dr * 32:(dr + 1) * 32]
            la = dl[:, 64 + dr * 32:64 + (dr + 1) * 32]
            pa = pa_bank
            fw.pe.op(lambda e: e.matmul(pa[:, 0:32], lhsT=tri[dr][:, :], rhs=la, start=True, stop=True), reads=[tri[dr], dl], writes=[pa])
            fw.pe.op(lambda e: e.matmul(pa[:, 32:64], lhsT=onesf[:, :], rhs=la, start=True, stop=True), reads=[onesf, dl], writes=[pa])
            fw.pe.op(lambda e: e.matmul(pa[0:32, 64:192], lhsT=la, rhs=tri[dr][:, :], start=True, stop=True), reads=[tri[dr], dl], writes=[pa])
            fw.dve.op(lambda e: e.tensor_copy(out=acs[:, :], in_=pa[:, 0:32]), reads=[pa], writes=[acs])
            fw.dve.op(lambda e: e.tensor_tensor(out=dsb[:, :], in0=pa[:, 32:64], in1=acs[:, :], op=ALU.subtract), reads=[pa, acs], writes=[dsb])
            fw.dve.op(lambda e: e.tensor_copy(out=aT[0][0:32, :], in_=pa[0:32, 64:192]), reads=[pa], writes=[aT[0]])
            fw.dve.op(lambda e: e.tensor_copy(out=hif[:, :], in_=aT[0][0:32, :]), reads=[aT[0]], writes=[hif])
            fw.dve.op(lambda e: e.tensor_tensor(out=aT[1][0:32, :], in0=pa[0:32, 64:192], in1=hif[:, :], op=ALU.subtract), reads=[pa, hif], writes=[aT[1]])
            fw.dve.op(lambda e: e.tensor_scalar(out=aT[2][0:32, :], in0=aT[0][0:32, :], scalar1=-1.0, scalar2=None, op0=ALU.mult), reads=[aT[0]], writes=[aT[2]])
            fw.dve.op(lambda e: e.tensor_scalar(out=aT[3][0:32, :], in0=aT[1][0:32, :], scalar1=-1.0, scalar2=None, op0=ALU.mult), reads=[aT[1]], writes=[aT[3]])
            fw.act.op(lambda e: e.activation(out=decay[:, :], in_=pa[:, 32:64], func=AF.Exp), reads=[pa], writes=[decay])
            fw.act.op(lambda e: e.activation(out=dte[:, :], in_=dsb[:, :], func=AF.Exp), reads=[dsb], writes=[dte])
            fw.act.op(lambda e: e.activation(out=expac[:, :], in_=acs[:, :], func=AF.Exp), reads=[acs], writes=[expac])
            fw.dve.op(lambda e: e.tensor_tensor(out=w2[:, :], in0=dt, in1=dte[:, :], op=ALU.mult), reads=[dl, dte], writes=[w2])
            for c4 in range(8):
                ps = rot.get()
                for q in range(4):
                    cc = c4 * 4 + q
                    for kk in range(4):
                        off = kk if dr == 0 else 3 - kk
                        fw.pe.op(lambda e: e.matmul(ps[:, q * 128:(q + 1) * 128], lhsT=diag[dr][:, cc, kk, :], rhs=xb[:, cc, off:off + 128],
                                                    start=(kk == 0), stop=(kk == 3)),
                                 reads=[diag[dr], xb], writes=[ps], inc=(kk == 3))
                for q in range(4):
                    cc = c4 * 4 + q
                    fw.act.op(lambda e: e.activation(out=uT[:, cc, :], in_=ps[:, q * 128:(q + 1) * 128], func=AF.Silu, bias=cbT[dr][:, cc:cc + 1], scale=1.0),
                              reads=[ps], writes=[uT])

        def P2(n):
            b, dr, t0, s0, s1, first = sched[n]
            p = n % 2
            dl = dli[p]
            dt = dl[:, dr * 32:(dr + 1) * 32]
            uT, xs, xr, xd, Btok, w2 = uTs[p], xss[p], xrs[p], xds[p], Btoks[p], w2s[p]
            xsD = ych[p]
            for hx in range(2):
                ps = rot.get()
                pv = ps[:, :].bitcast(BF16)
                for q in range(8):
                    fw.pe.op(lambda e: e.transpose(out=pv[:, q * 128:(q + 1) * 128], in_=uT[:, hx * 8 + q, :], identity=K.ident_bf[:, :]),
                             reads=[uT, K.ident_bf], writes=[ps], inc=(q == 7))
                fw.dve.op(lambda e: e.tensor_copy(out=xs[:, hx * 1024:(hx + 1) * 1024], in_=pv[:, :]), reads=[ps], writes=[xs])
            ps = rot.get()
            pv = ps[:, :].bitcast(BF16)
            for q in range(8):
                fw.pe.op(lambda e: e.transpose(out=pv[:, q * 128:(q + 1) * 128], in_=uT[:, 16 + q, :], identity=K.ident_bf[:, :]),
                         reads=[uT, K.ident_bf], writes=[ps], inc=(q == 7))
            fw.act.op(lambda e: e.activation(out=Btok[:, :, :], in_=pv[:, :].rearrange("p (g n) -> p g n", g=8), func=AF.Identity), reads=[ps], writes=[Btok])
            xs3 = xs[:, :].rearrange("p (h d) -> p h d", h=32)
            fw.dve.op(lambda e: e.tensor_tensor(out=xr[:, :].rearrange("p (h d) -> p h d", h=32), in0=xs3,
                                                in1=dt.unsqueeze(2).to_broadcast([128, 32, 64]), op=ALU.mult), reads=[xs, dl], writes=[xr])
            fw.pool.op(lambda e: e.tensor_tensor(out=xd[:, :].rearrange("p (h d) -> p h d", h=32), in0=xs3,
                                                 in1=w2[:, :].unsqueeze(2).to_broadcast([128, 32, 64]), op=ALU.mult), reads=[xs, w2], writes=[xd])
            fw.pool.op(lambda e: e.tensor_tensor(out=xsD[:, :].rearrange("p (h d) -> p h d", h=32), in0=xs3,
                                                 in1=Dbc[dr][:, :].unsqueeze(2).to_broadcast([128, 32, 64]), op=ALU.mult), reads=[xs, Dbc[dr]], writes=[xsD])

        def CBm(n, half):
            uT = uTs[n % 2]
            for q in range(4):
                g = half * 4 + q
                fw.pe.op(lambda e: e.matmul(pcb_bank[:, q * 128:(q + 1) * 128], lhsT=uT[:, 16 + g, :], rhs=uT[:, 24 + g, :], start=True, stop=True),
                         reads=[uT], writes=[pcb_bank], inc=(q == 3))

        def SEGm(n, g):
            b, dr, t0, s0, s1, first = sched[n]
            aT = aTs[n % 2]
            pS = rot.get()
            fw.pe.op(lambda e: e.matmul(pS[:, :].rearrange("p (j l) -> p j l", j=4), lhsT=K.ident_bf[:, :], rhs=bc_mid(mneg[dr][:, :], 4),
                                        start=True, stop=False, skip_group_check=True),
                     reads=[K.ident_bf, mneg[dr]], writes=[pS], inc=False)
            for jj in range(4):
                hj = 4 * g + jj
                o_ap = pS[:, jj * 128:(jj + 1) * 128]
                fw.pe.op(lambda e: e.matmul(o_ap, lhsT=selA[:, hj, :], rhs=aT[0][:, :], start=False, stop=False, skip_group_check=True),
                         reads=[selA, aT[0]], writes=[pS], inc=False)
                fw.pe.op(lambda e: e.matmul(o_ap, lhsT=selA[:, hj, :], rhs=aT[1][:, :], start=False, stop=False, skip_group_check=True),
                         reads=[selA, aT[1]], writes=[pS], inc=False)
                fw.pe.op(lambda e: e.matmul(o_ap, lhsT=aT[2][:, :], rhs=selA[:, hj, :], start=False, stop=False, skip_group_check=True),
                         reads=[selA, aT[2]], writes=[pS], inc=False)
                fw.pe.op(lambda e: e.matmul(o_ap, lhsT=aT[3][:, :], rhs=selA[:, hj, :], start=False, stop=True, skip_group_check=True),
                         reads=[selA, aT[3]], writes=[pS], inc=(jj == 3))
            Le, Gg = Lexp[g % 2], G[g % 2]
            fw.act.op(lambda e: e.activation(out=Le[:, :], in_=pS[:, :], func=AF.Exp), reads=[pS], writes=[Le])
            fw.dve.op(lambda e: e.tensor_tensor(out=Gg[:, :, :], in0=Le[:, :].rearrange("p (j l) -> p j l", j=4),
                                                in1=bc_mid(pcb_bank[:, (g % 4) * 128:(g % 4 + 1) * 128], 4), op=ALU.mult),
                      reads=[Le, pcb_bank], writes=[Gg])

        def Ym(n, g):
            b, dr, t0, s0, s1, first = sched[n]
            p = n % 2
            uT, xr, xd, Btok, expac, decay = uTs[p], xrs[p], xds[p], Btoks[p], expacs[p], decays[p]
            yc = ych[p]
            Gg = G[g % 2]
            pY = rot.get()
            for jj in range(4):
                hj = 4 * g + jj
                fw.pe.op(lambda e: e.matmul(pY[:, jj * 64:(jj + 1) * 64], lhsT=Gg[:, jj, :], rhs=xr[:, hj * 64:(hj + 1) * 64], start=True, stop=True),
                         reads=[Gg, xr], writes=[pY], inc=False)
            fw.pe.op(lambda e: e.matmul(pY[:, 256:512], lhsT=uT[:, 24 + g, :], rhs=hb[g][:, :], start=True, stop=True),
                     reads=[uT, hb[g]], writes=[pY])
            pT = rot.get()
            fw.pe.op(lambda e: e.matmul(pT[:, 0:256], lhsT=Btok[:, g, :], rhs=xd[:, g * 256:(g + 1) * 256], start=True, stop=True),
                     reads=[Btok, xd], writes=[pT])
            ya, yb_ = yt[g % 2], yt2[g % 2]
            fw.dve.op(lambda e: e.tensor_tensor(out=ya[:, :].rearrange("p (j d) -> p j d", j=4), in0=pY[:, 256:512].rearrange("p (j d) -> p j d", j=4),
                                                in1=expac[:, 4 * g:4 * g + 4].unsqueeze(2).to_broadcast([128, 4, 64]), op=ALU.mult),
                      reads=[pY, expac], writes=[ya])
            fw.dve.op(lambda e: e.tensor_tensor(out=yb_[:, :], in0=pY[:, 0:256], in1=ya[:, :], op=ALU.add), reads=[pY, ya], writes=[yb_])
            fw.pool.op(lambda e: e.tensor_tensor(out=yc[:, g * 256:(g + 1) * 256], in0=yb_[:, :], in1=yc[:, g * 256:(g + 1) * 256], op=ALU.add),
                       reads=[yb_, yc], writes=[yc])
            fw.pool.op(lambda e: e.tensor_tensor(out=hst[g][:, :].rearrange("p (j d) -> p j d", j=4), in0=hst[g][:, :].rearrange("p (j d) -> p j d", j=4),
                                                 in1=decay[:, 4 * g:4 * g + 4].unsqueeze(2).to_broadcast([128, 4, 64]), op=ALU.mult),
                       reads=[hst[g], decay], writes=[hst[g]])
            fw.dve.op(lambda e: e.tensor_tensor(out=hst[g][:, :], in0=pT[:, 0:256], in1=hst[g][:, :], op=ALU.add), reads=[pT, hst[g]], writes=[hst[g]])
            fw.act.op(lambda e: e.activation(out=hb[g][:, :], in_=hst[g][:, :], func=AF.Identity), reads=[hst[g]], writes=[hb[g]])

        load(0)
        if NCH > 1:
            load(1)
        P1(0)
        P2(0)
        for n, (b, dr, t0, s0, s1, first) in enumerate(sched):
            if first:
                for g in range(8):
                    fw.pool.op(lambda e: e.memset(hst[g][:, :], 0.0), writes=[hst[g]])
                    fw.pool.op(lambda e: e.memset(hb[g][:, :], 0.0), writes=[hb[g]])
            CBm(n, 0)
            SEGm(n, 0)
            for g in range(8):
                if g + 1 < 8:
                    if g + 1 == 4:
                        CBm(n, 1)
                    SEGm(n, g + 1)
                Ym(n, g)
                if g == 1 and n + 1 < NCH:
                    P1(n + 1)
                if g == 5 and n + 1 < NCH:
                    P2(n + 1)
            if n + 2 < NCH:
                load(n + 2)
            fw.sp.dma(K.y_d[dr, t0:t0 + 128, :], ych[n % 2][:, :], reads=[ych[n % 2]])
        fw.barrier()
    K.marks.append((f"  MC{i}", fw.pe.n_ins))
    with ExitStack() as st:
        wo = fw.sb(st, [128, 16, D], BF16, "wo")
        fw.pool.dma(wo[:, :, :], K.w["ssm_w_out"][j].rearrange("(k p) n -> p k n", p=128), writes=[wo])
        gnb = fw.sb(st, [128, 2048], F32, "gnb")
        fw.sp.dma(gnb[:, :], K.w["ssm_norm_g"][j].partition_broadcast(128), writes=[gnb])
        xin = [fw.sb(st, [128, 8, 512], F32, "xin") for _ in range(2)]
        yf = [fw.sb(st, [128, 2048], F32, "yf") for _ in range(3)]
        yb = [fw.sb(st, [128, 2048], F32, "yb") for _ in range(3)]
        zs = [fw.sb(st, [128, 2048], BF16, "zs") for _ in range(3)]
        ygs = [fw.sb(st, [128, 2048], F32, "yg") for _ in range(3)]
        sqjs = [fw.sb(st, [128, 2048], F32, "sqj") for _ in range(1)]
        ssqs = [fw.sb(st, [128, 8], F32, "ssq") for _ in range(3)]
        ynbs = [fw.sb(st, [128, 2048], BF16, "ynb") for _ in range(2)]
        ynT = fw.sb(st, [128, 16, 512], BF16, "ynT")
        gate = K.modT[i][:, 16:24, :]
        blks = blocks(need_ctx)
        subs = [(bi, s) for bi, (v, t0, N, isc) in enumerate(blks) for s in range(N // 128)]

        def load(si):
            bi, s = subs[si]
            v, t0, N, isc = blks[bi]
            ts = t0 + s * 128
            fw.sp.dma(yf[si % 3][:, :], K.y_d[0, ts:ts + 128, :], writes=[yf[si % 3]])
            fw.sp.dma(yb[si % 3][:, :], K.y_d[1, ts:ts + 128, :], writes=[yb[si % 3]])
            fw.sp.dma(zs[si % 3][:, :], zv[ts:ts + 128, :], writes=[zs[si % 3]])
            if s == 0:
                fw.sp.dma(xin[bi % 2][:, :, :N], xv[:, :, t0:t0 + N], writes=[xin[bi % 2]])

        def stA(si):
            a, b_, z_ = yf[si % 3], yb[si % 3], zs[si % 3]
            yg, ssq, sqj = ygs[si % 3], ssqs[si % 3], sqjs[0]
            fw.pool.op(lambda e: e.tensor_tensor(out=a[:, :], in0=a[:, :], in1=b_[:, :], op=ALU.add), reads=[a, b_], writes=[a])
            fw.dve.op(lambda e: e.tensor_tensor(out=yg[:, :], in0=a[:, :], in1=z_[:, :], op=ALU.mult), reads=[a, z_], writes=[yg])
            fw.act.op(lambda e: e.activation(out=sqj[:, :], in_=yg[:, :], func=AF.Square), reads=[yg], writes=[sqj])
            fw.dve.op(lambda e: e.tensor_reduce(out=ssq[:, :], in_=sqj[:, :].rearrange("p (g d) -> p g d", g=8), axis=AX.X, op=ALU.add),
                      reads=[sqj], writes=[ssq])
            fw.act.op(lambda e: e.activation(out=ssq[:, :], in_=ssq[:, :], func=AF.Ln, bias=K.epsc[:, 0:1], scale=1.0 / 256), reads=[ssq], writes=[ssq])
            fw.act.op(lambda e: e.activation(out=ssq[:, :], in_=ssq[:, :], func=AF.Exp, scale=-0.5), reads=[ssq], writes=[ssq])

        def stB(si):
            bi, s = subs[si]
            v, t0, N, isc = blks[bi]
            yg, ssq, ynb = ygs[si % 3], ssqs[si % 3], ynbs[si % 2]
            yn = yg
            fw.dve.op(lambda e: e.tensor_tensor(out=yn[:, :].rearrange("p (g d) -> p g d", g=8), in0=yg[:, :].rearrange("p (g d) -> p g d", g=8),
                                                in1=ssq[:, :].unsqueeze(2).to_broadcast([128, 8, 256]), op=ALU.mult), reads=[yg, ssq], writes=[yn])
            fw.pool.op(lambda e: e.tensor_tensor(out=ynb[:, :], in0=yn[:, :], in1=gnb[:, :], op=ALU.mult), reads=[yn, gnb], writes=[ynb])
            for hx in range(2):
                ps = K.psum.get()
                pv = ps[:, :].bitcast(BF16)
                for q in range(8):
                    cc = hx * 8 + q
                    fw.pe.op(lambda e: e.transpose(out=pv[:, q * 128:(q + 1) * 128], in_=ynb[:, cc * 128:(cc + 1) * 128], identity=K.ident_bf[:, :]),
                             reads=[ynb, K.ident_bf], writes=[ps], inc=(q == 7))
                src = pv[:, :].rearrange("p (c t) -> p c t", c=8)
                dst = ynT[:, hx * 8:(hx + 1) * 8, s * 128:(s + 1) * 128]
                if hx == 0:
                    fw.act.op(lambda e: e.activation(out=dst, in_=src, func=AF.Identity), reads=[ps], writes=[ynT])
                else:
                    fw.dve.op(lambda e: e.tensor_copy(out=dst, in_=src), reads=[ps], writes=[ynT])
            if s != N // 128 - 1:
                return
            xi = xin[bi % 2]
            for dc in range(8):
                ps = K.psum.get()
                for k in range(16):
                    fw.pe.op(lambda e: e.matmul(ps[:, :N], lhsT=wo[:, k, dc * 128:(dc + 1) * 128], rhs=ynT[:, k, :N], start=(k == 0), stop=(k == 15)),
                             reads=[wo, ynT], writes=[ps], inc=(k == 15))
                fw.dve.op(lambda e: e.scalar_tensor_tensor(out=xi[:, dc, :N], in0=ps[:, :N], scalar=gate[:, dc, v:v + 1], in1=xi[:, dc, :N],
                                                           op0=ALU.mult, op1=ALU.add),
                          reads=[ps, xi], writes=[xi])
            fw.sp.dma(xv[:, :, t0:t0 + N], xi[:, :, :N], reads=[xi])

        for si in range(3):
            load(si)
        stA(0)
        stA(1)
        for si in range(len(subs)):
            if si + 2 < len(subs):
                stA(si + 2)
            stB(si)
            if si + 3 < len(subs):
                load(si + 3)
        fw.barrier()


MIXERS[0] = mixer_mamba
```

```python
import math
from contextlib import ExitStack
import numpy as np
import concourse.bass as bass
import concourse.mybir as mybir
from concourse.bass_utils import run_bass_kernel_spmd

F32 = mybir.dt.float32
BF16 = mybir.dt.bfloat16
AF = mybir.ActivationFunctionType
ALU = mybir.AluOpType
AX = mybir.AxisListType

EPS = 1e-6
D = 1024
SEQ = 2048
CTX = 256
SEG = CTX + SEQ
T = 2 * SEG
NCORES = 8
FH = 2816
NEG = -30000.0


class Buf:
    def __init__(self, t, name, psum=False):
        self.t = t
        self.name = name
        self.psum = psum
        self.last_w = None
        self.readers = []

    def __getitem__(self, k):
        return self.t[k]


class Tag:
    __slots__ = ("sem", "val", "eng")

    def __init__(self, sem, val, eng):
        self.sem, self.val, self.eng = sem, val, eng


class Eng:
    def __init__(self, name, h, kind):
        self.name, self.h, self.kind = name, h, kind
        self.sem = None
        self.count = 0
        self.seen = {}
        self.dsems = []
        self.dnext = 0
        self.n_ins = 0

    def _wait(self, tag):
        key = id(tag.sem)
        if self.seen.get(key, 0) >= tag.val:
            return
        self.h.wait_ge(tag.sem, tag.val)
        self.seen[key] = tag.val

    def _deps(self, reads, writes):
        for b in reads:
            t = b.last_w
            if t is not None:
                if t.eng is self and self.kind == "pe":
                    continue
                self._wait(t)
            if b.psum:
                for r in b.readers:
                    if r.eng is not self:
                        self._wait(r)
        for b in writes:
            t = b.last_w
            if t is not None and not (t.eng is self and self.kind == "pe"):
                self._wait(t)
            for r in b.readers:
                self._wait(r)

    def _record(self, tag, reads, writes):
        for b in reads:
            if tag.eng is not None:
                b.readers = [r for r in b.readers if r.eng is not tag.eng]
            b.readers.append(tag)
        for b in writes:
            b.last_w = tag
            b.readers = []

    def op(self, fn, reads=(), writes=(), inc=True):
        self._deps(reads, writes)
        ins = fn(self.h)
        self.n_ins += 1
        if inc:
            self.count += 1
            ins.then_inc(self.sem, 1)
            tag = Tag(self.sem, self.count, self)
        else:
            tag = Tag(self.sem, self.count + 1, self)
        self._record(tag, reads, writes)
        return ins

    def dma(self, out, in_, reads=(), writes=(), **kw):
        self._deps(reads, writes)
        i = self.dnext % len(self.dsems)
        self.dnext += 1
        sem, cnt = self.dsems[i]
        if cnt > 0:
            self._wait(Tag(sem, cnt, None))
        cnt += 16
        self.dsems[i] = (sem, cnt)
        self.h.dma_start(out=out, in_=in_, **kw).then_inc(sem, 16)
        self.n_ins += 1
        tag = Tag(sem, cnt, None)
        self._record(tag, reads, writes)
        return tag


class FW:
    def __init__(self, nc, stack, n_dma_sems=8):
        self.nc = nc
        self.pe = Eng("pe", nc.tensor, "pe")
        self.act = Eng("act", nc.scalar, "act")
        self.dve = Eng("dve", nc.vector, "dve")
        self.pool = Eng("pool", nc.gpsimd, "pool")
        self.sp = Eng("sp", nc.sync, "sp")
        self.engs = [self.pe, self.act, self.dve, self.pool, self.sp]
        for e in self.engs:
            e.sem = stack.enter_context(nc.semaphore("s_" + e.name))
        for e in (self.sp, self.pool, self.act):
            for i in range(n_dma_sems):
                s = stack.enter_context(nc.semaphore(f"d_{e.name}{i}"))
                e.dsems.append((s, 0))
        self._uid = 0

    def sb(self, cm, shape, dtype, name="t"):
        self._uid += 1
        name = f"{name}_{self._uid}"
        return Buf(cm.enter_context(self.nc.sbuf_tensor(name, list(shape), dtype)), name)

    def ps(self, cm, shape, dtype, name="p"):
        self._uid += 1
        name = f"{name}_{self._uid}"
        return Buf(cm.enter_context(self.nc.psum_tensor(name, list(shape), dtype)), name, psum=True)

    def barrier(self):
        tags = []
        for e in self.engs:
            if e.count > 0:
                tags.append(Tag(e.sem, e.count, e))
            for (s, c) in e.dsems:
                if c > 0:
                    tags.append(Tag(s, c, None))
        for e in self.engs:
            for t in tags:
                if t.eng is e:
                    continue
                e._wait(t)


class PsumPool:
    def __init__(self, bufs):
        self.bufs = bufs
        self.i = 0

    def get(self):
        b = self.bufs[self.i % len(self.bufs)]
        self.i += 1
        return b


class Ctx:
    pass


def blocks(include_ctx=True):
    out = []
    for b in range(2):
        base = b * SEG
        if include_ctx:
            out.append((2, base, CTX, True))
        for q in range(4):
            out.append((b, base + CTX + q * 512, 512, False))
    return out


def bc_mid(ap2d, n):
    (ps, pn), (fs, fn) = ap2d.ap
    return bass.AP(ap2d.tensor, ap2d.offset, [[ps, pn], [0, n], [fs, fn]])


def load_vecT(K, cm, rows_ap, n, name="vT"):
    fw = K.fw
    dst = fw.sb(cm, [128, n], F32, name)
    with ExitStack() as st:
        tmp = fw.sb(st, [128, 128], F32, "vrow")
        fw.sp.dma(tmp[0:n, :], rows_ap, writes=[tmp])
        ps = K.psum.get()
        fw.pe.op(lambda e: e.transpose(out=ps[:, 0:n], in_=tmp[0:n, :], identity=K.ident[0:n, 0:n]),
                 reads=[tmp, K.ident], writes=[ps])
        fw.dve.op(lambda e: e.tensor_copy(out=dst[:, :], in_=ps[:, 0:n]), reads=[ps], writes=[dst])
        fw.barrier()
    return dst


def norm_block(K, W, xin, N, A, sh, v, hT, hoff=0):
    fw = K.fw
    fw.act.op(lambda e: e.activation(out=W.sq[:, :, :N], in_=xin[:, :, :N], func=AF.Square),
              reads=[xin], writes=[W.sq])
    ps = K.psum.get()
    for k in range(8):
        fw.pe.op(lambda e: e.matmul(ps[:, :N], lhsT=K.ones_bf[:, :], rhs=W.sq[:, k, :N], start=(k == 0), stop=(k == 7)),
                 reads=[W.sq, K.ones_bf], writes=[ps], inc=(k == 7))
    fw.act.op(lambda e: e.activation(out=W.rstd[:, :N], in_=ps[:, :N], func=AF.Ln, bias=K.epsc[:, 0:1], scale=1.0 / D),
              reads=[ps], writes=[W.rstd])
    fw.act.op(lambda e: e.activation(out=W.rstd[:, :N], in_=W.rstd[:, :N], func=AF.Exp, scale=-0.5),
              reads=[W.rstd], writes=[W.rstd])
    for k in range(8):
        tmp = W.ntmp[k % 2]
        fw.dve.op(lambda e: e.scalar_tensor_tensor(out=tmp[:, :N], in0=xin[:, k, :N], scalar=A[:, k, v:v + 1],
                                                   in1=W.rstd[:, :N], op0=ALU.mult, op1=ALU.mult),
                  reads=[xin, W.rstd], writes=[tmp])
        fw.act.op(lambda e: e.activation(out=hT[:, k, hoff:hoff + N], in_=tmp[:, :N], func=AF.Identity,
                                         bias=sh[:, k, v:v + 1], scale=1.0),
                  reads=[tmp], writes=[hT])


def alloc_norm_work(K, cm):
    fw = K.fw
    W = Ctx()
    W.sq = fw.sb(cm, [128, 8, 512], BF16, "sq")
    W.rstd = fw.sb(cm, [128, 512], F32, "rstd")
    W.ntmp = [fw.sb(cm, [128, 512], F32, "ntmp") for _ in range(2)]
    return W


def xT_view(K):
    return K.xT_d.rearrange("(k p) t -> p k t", p=128)


def phase_init(K):
    fw = K.fw
    xv = xT_view(K)
    with ExitStack() as st:
        tin = [fw.sb(st, [128, D], F32, "tin") for _ in range(2)]
        tout = [fw.sb(st, [128, 8, 128], F32, "tout") for _ in range(2)]
        it = 0
        for b in range(2):
            for (src, n, off) in ((K.ctx_in, CTX, 0), (K.x_in, SEQ, CTX)):
                for j in range(n // 128):
                    ti, to = tin[it % 2], tout[it % 2]
                    it += 1
                    fw.sp.dma(ti[:, :], src[b, j * 128:(j + 1) * 128, :], writes=[ti])
                    for half in range(2):
                        ps = K.psum.get()
                        for q in range(4):
                            k = half * 4 + q
                            fw.pe.op(lambda e: e.transpose(out=ps[:, q * 128:(q + 1) * 128], in_=ti[:, k * 128:(k + 1) * 128],
                                                           identity=K.ident[:, :]),
                                     reads=[ti, K.ident], writes=[ps], inc=(q == 3))
                        eng = fw.dve if half == 0 else fw.act
                        if half == 0:
                            fw.dve.op(lambda e: e.tensor_copy(out=to[:, 0:4, :], in_=ps[:, :].rearrange("p (q t) -> p q t", q=4)),
                                      reads=[ps], writes=[to])
                        else:
                            fw.act.op(lambda e: e.activation(out=to[:, 4:8, :], in_=ps[:, :].rearrange("p (q t) -> p q t", q=4),
                                                             func=AF.Identity),
                                      reads=[ps], writes=[to])
                    t0 = b * SEG + off + j * 128
                    fw.sp.dma(xv[:, :, t0:t0 + 128], to[:, :, :], reads=[to])
        fw.barrier()


def phase_mod(K, layers):
    fw = K.fw
    K.modT, K.A1, K.A2 = {}, {}, {}
    for i in layers:
        K.modT[i] = fw.sb(K.st, [128, 48, 3], F32, f"modT{i}")
        K.A1[i] = fw.sb(K.st, [128, 8, 3], F32, f"A1_{i}")
        K.A2[i] = fw.sb(K.st, [128, 8, 3], F32, f"A2_{i}")
    with ExitStack() as st:
        wsb = fw.sb(st, [128, 8, 6144], BF16, "modw")
        c24 = fw.sb(st, [24, 128], F32, "c24")
        sT = fw.sb(st, [128, 24], BF16, "sT")
        fw.sp.dma(c24[:, :], K.c3.rearrange("v (k p) -> (v k) p", p=128), writes=[c24])
        ps = K.psum.get()
        fw.pe.op(lambda e: e.transpose(out=ps[:, 0:24], in_=c24[:, :], identity=K.ident[0:24, 0:24]),
                 reads=[c24, K.ident], writes=[ps])
        fw.act.op(lambda e: e.activation(out=sT[:, :], in_=ps[:, 0:24], func=AF.Silu), reads=[ps], writes=[sT])
        for i in layers:
            with ExitStack() as st2:
                mb = load_vecT(K, st2, K.w["mod_b"][i].rearrange("(c p) -> c p", p=128), 48, "mb")
                g1 = load_vecT(K, st2, K.w["norm1_g"][i].rearrange("(c p) -> c p", p=128), 8, "g1")
                g2 = load_vecT(K, st2, K.w["norm2_g"][i].rearrange("(c p) -> c p", p=128), 8, "g2")
                tmp = fw.sb(st2, [128, 8, 3], F32, "mtmp")
                fw.pool.dma(wsb[:, :, :], K.w["mod_w"][i].rearrange("(k p) n -> p k n", p=128), writes=[wsb])
                ps = K.psum.get()
                for cc in range(48):
                    for k in range(8):
                        fw.pe.op(lambda e: e.matmul(ps[:, cc * 3:(cc + 1) * 3], lhsT=wsb[:, k, cc * 128:(cc + 1) * 128],
                                                    rhs=sT[:, k:24:8], start=(k == 0), stop=(k == 7)),
                                 reads=[wsb, sT], writes=[ps], inc=(k == 7))
                mT = K.modT[i]
                fw.dve.op(lambda e: e.tensor_tensor(out=mT[:, :, :], in0=ps[:, 0:144].rearrange("p (c v) -> p c v", v=3),
                                                    in1=mb[:, :].unsqueeze(2).to_broadcast([128, 48, 3]), op=ALU.add),
                          reads=[ps, mb], writes=[mT])
                for (A, g, c0) in ((K.A1[i], g1, 8), (K.A2[i], g2, 32)):
                    fw.dve.op(lambda e: e.tensor_scalar(out=tmp[:, :, :], in0=mT[:, c0:c0 + 8, :], scalar1=1.0, scalar2=None, op0=ALU.add),
                              reads=[mT], writes=[tmp])
                    fw.dve.op(lambda e: e.tensor_tensor(out=A[:, :, :], in0=tmp[:, :, :],
                                                        in1=g[:, :].unsqueeze(2).to_broadcast([128, 8, 3]), op=ALU.mult),
                              reads=[tmp, g], writes=[A])
                fw.barrier()
        fw.barrier()


def phase_ffn_up(K, i, include_ctx):
    fw = K.fw
    xv = xT_view(K)
    gv = K.gT_d.rearrange("(c p) t -> p c t", p=128)
    with ExitStack() as st:
        cw = [load_vecT(K, st, K.w["ffn_conv_w"][i, kk].rearrange("(c p) -> c p", p=128), 44, "fcw") for kk in range(3)]
        cb = load_vecT(K, st, K.w["ffn_conv_b"][i].rearrange("(c p) -> c p", p=128), 44, "fcb")
        hT = fw.sb(st, [128, 8, T], BF16, "hT")
        with ExitStack() as st2:
            W = alloc_norm_work(K, st2)
            xin = [fw.sb(st2, [128, 8, 512], F32, "xin") for _ in range(2)]
            for bi, (v, t0, N, isc) in enumerate(blocks(include_ctx)):
                xi = xin[bi % 2]
                fw.sp.dma(xi[:, :, :N], xv[:, :, t0:t0 + N], writes=[xi])
                norm_block(K, W, xi, N, K.A2[i], K.modT[i][:, 24:32, :], v, hT, hoff=t0)
            fw.barrier()
        wb = [fw.sb(st, [128, 8, 256], BF16, "wup") for _ in range(3)]
        up = [[fw.sb(st, [128, SEQ + 2], F32, "upre") for _ in range(2)] for _ in range(2)]
        accs = [[fw.sb(st, [128, SEQ], F32, "acc") for _ in range(2)] for _ in range(2)]
        sils = [fw.sb(st, [128, SEQ], F32, "sil") for _ in range(2)]
        gt = [fw.sb(st, [128, SEQ], BF16, "gt") for _ in range(2)]
        for u2 in up:
            for u in u2:
                fw.pool.op(lambda e: e.memset(u[:, :], 0.0), writes=[u])
        wsrc = K.w["ffn_w_up"][i].rearrange("(k p) n -> p k n", p=128)
        segs = []
        for b in range(2):
            if include_ctx:
                segs.append((b * SEG, CTX))
            segs.append((b * SEG + CTX, SEQ))
        it = 0

        def wload(c):
            w = wb[c % 3]
            fw.pool.dma(w[:, :, 0:128], wsrc[:, :, c * 128:(c + 1) * 128], writes=[w])
            fw.pool.dma(w[:, :, 128:256], wsrc[:, :, FH + c * 128:FH + (c + 1) * 128], writes=[w])

        wload(0)
        wload(1)
        for c in range(22):
            if c + 2 < 22:
                wload(c + 2)
            w = wb[c % 3]
            for (s0, L) in segs:
                ub = up[it % 2]
                g = gt[it % 2]
                acc = accs[it % 2]
                sil = sils[it % 2]
                it += 1
                for half in range(2):
                    u = ub[half]
                    cc = c + 22 * half
                    if L != SEQ:
                        fw.pool.op(lambda e: e.memset(u[:, L + 1:L + 2], 0.0), writes=[u])
                    for n0 in range(0, L, 512):
                        n = min(512, L - n0)
                        ps = K.psum.get()
                        for k in range(8):
                            fw.pe.op(lambda e: e.matmul(ps[:, :n], lhsT=w[:, k, half * 128:(half + 1) * 128],
                                                        rhs=hT[:, k, s0 + n0:s0 + n0 + n], start=(k == 0), stop=(k == 7)),
                                     reads=[w, hT], writes=[ps], inc=(k == 7))
                        fw.act.op(lambda e: e.activation(out=u[:, 1 + n0:1 + n0 + n], in_=ps[:, :n], func=AF.Identity),
                                  reads=[ps], writes=[u])
                    a = acc[half]
                    fw.act.op(lambda e: e.activation(out=a[:, :L], in_=u[:, 0:L], func=AF.Identity, scale=cw[0][:, cc:cc + 1],
                                                     bias=cb[:, cc:cc + 1]),
                              reads=[u], writes=[a])
                    for kk in (1, 2):
                        fw.dve.op(lambda e: e.scalar_tensor_tensor(out=a[:, :L], in0=u[:, kk:kk + L], scalar=cw[kk][:, cc:cc + 1],
                                                                   in1=a[:, :L], op0=ALU.mult, op1=ALU.add),
                                  reads=[u, a], writes=[a])
                fw.act.op(lambda e: e.activation(out=sil[:, :L], in_=acc[0][:, :L], func=AF.Silu), reads=[acc[0]], writes=[sil])
                fw.pool.op(lambda e: e.tensor_tensor(out=g[:, :L], in0=sil[:, :L], in1=acc[1][:, :L], op=ALU.mult),
                           reads=[sil, acc[1]], writes=[g])
                fw.sp.dma(gv[:, c, s0:s0 + L], g[:, :L], reads=[g])
        fw.barrier()


def phase_ffn_down(K, i, include_ctx, final):
    fw = K.fw
    xv = xT_view(K)
    gv = K.gT_d.rearrange("(c p) t -> p c t", p=128)
    gate = K.modT[i][:, 40:48, :]
    with ExitStack() as st:
        wd = fw.sb(st, [128, 22, D], BF16, "wdown")
        fw.pool.dma(wd[:, :, :], K.w["ffn_w_down"][i].rearrange("(k p) n -> p k n", p=128), writes=[wd])
        xin = [fw.sb(st, [128, 8, 512], F32, "xin") for _ in range(2)]
        gin = [fw.sb(st, [128, 22, 512], BF16, "gin") for _ in range(2)]
        if final:
            W = alloc_norm_work(K, st)
            fg = load_vecT(K, st, K.w["final_g"].rearrange("(c p) -> c p", p=128), 8, "fg")
            xn = fw.sb(st, [128, 8, 512], F32, "xn")
            otile = [fw.sb(st, [128, D], F32, "otile") for _ in range(2)]
        blks = blocks(include_ctx)

        def load(bi):
            v, t0, N, isc = blks[bi]
            fw.sp.dma(xin[bi % 2][:, :, :N], xv[:, :, t0:t0 + N], writes=[xin[bi % 2]])
            fw.sp.dma(gin[bi % 2][:, :, :N], gv[:, :, t0:t0 + N], writes=[gin[bi % 2]])

        load(0)
        oi = 0
        for bi, (v, t0, N, isc) in enumerate(blks):
            if bi + 1 < len(blks):
                load(bi + 1)
            xi, gi = xin[bi % 2], gin[bi % 2]
            for dc in range(8):
                ps = K.psum.get()
                for k in range(22):
                    fw.pe.op(lambda e: e.matmul(ps[:, :N], lhsT=wd[:, k, dc * 128:(dc + 1) * 128], rhs=gi[:, k, :N],
                                                start=(k == 0), stop=(k == 21)),
                             reads=[wd, gi], writes=[ps], inc=(k == 21))
                fw.dve.op(lambda e: e.scalar_tensor_tensor(out=xi[:, dc, :N], in0=ps[:, :N], scalar=gate[:, dc, v:v + 1],
                                                           in1=xi[:, dc, :N], op0=ALU.mult, op1=ALU.add),
                          reads=[ps, xi], writes=[xi])
            if (not final) or isc:
                fw.sp.dma(xv[:, :, t0:t0 + N], xi[:, :, :N], reads=[xi])
                continue
            fw.act.op(lambda e: e.activation(out=W.sq[:, :, :N], in_=xi[:, :, :N], func=AF.Square), reads=[xi], writes=[W.sq])
            ps = K.psum.get()
            for k in range(8):
                fw.pe.op(lambda e: e.matmul(ps[:, :N], lhsT=K.ones_bf[:, :], rhs=W.sq[:, k, :N], start=(k == 0), stop=(k == 7)),
                         reads=[W.sq, K.ones_bf], writes=[ps], inc=(k == 7))
            fw.act.op(lambda e: e.activation(out=W.rstd[:, :N], in_=ps[:, :N], func=AF.Ln, bias=K.epsc[:, 0:1], scale=1.0 / D),
                      reads=[ps], writes=[W.rstd])
            fw.act.op(lambda e: e.activation(out=W.rstd[:, :N], in_=W.rstd[:, :N], func=AF.Exp, scale=-0.5),
                      reads=[W.rstd], writes=[W.rstd])
            for k in range(8):
                fw.dve.op(lambda e: e.scalar_tensor_tensor(out=xn[:, k, :N], in0=xi[:, k, :N], scalar=fg[:, k:k + 1],
                                                           in1=W.rstd[:, :N], op0=ALU.mult, op1=ALU.mult),
                          reads=[xi, W.rstd], writes=[xn])
            b = t0 // SEG
            tl = t0 - b * SEG - CTX
            for j in range(N // 128):
                ot = otile[oi % 2]
                oi += 1
                for half in range(2):
                    ps = K.psum.get()
                    for q in range(4):
                        k = half * 4 + q
                        fw.pe.op(lambda e: e.transpose(out=ps[:, q * 128:(q + 1) * 128], in_=xn[:, k, j * 128:(j + 1) * 128],
                                                       identity=K.ident[:, :]),
                                 reads=[xn, K.ident], writes=[ps], inc=(q == 3))
                    if half == 0:
                        fw.dve.op(lambda e: e.tensor_copy(out=ot[:, 0:512], in_=ps[:, :]), reads=[ps], writes=[ot])
                    else:
                        fw.act.op(lambda e: e.activation(out=ot[:, 512:1024], in_=ps[:, :], func=AF.Identity), reads=[ps], writes=[ot])
                fw.sp.dma(K.out[b, tl + j * 128:tl + (j + 1) * 128, :], ot[:, :], reads=[ot])
        fw.barrier()


W_SHAPES = {
    "mod_w": [4, D, 6144], "mod_b": [4, 6144], "norm1_g": [4, D], "norm2_g": [4, D],
    "ffn_w_up": [4, D, 2 * FH], "ffn_conv_w": [4, 3, 2 * FH], "ffn_conv_b": [4, 2 * FH], "ffn_w_down": [4, FH, D],
    "ssm_w_in": [2, D, 6208], "ssm_conv_w": [2, 2, 4, 4096], "ssm_conv_b": [2, 2, 4096], "ssm_dt_bias": [2, 2, 32],
    "ssm_a_log": [2, 2, 32], "ssm_d": [2, 2, 32], "ssm_norm_g": [2, 2048], "ssm_w_out": [2, 2048, D],
    "attn_w_in": [1, D, 3 * D], "attn_lambda": [1, 4, 64], "attn_norm_g": [1, 128], "attn_w_out": [1, D, D],
    "conf_w_pw1": [1, D, 2 * D], "conf_b_pw1": [1, 2 * D], "conf_dw_w": [1, 31, D], "conf_dw_b": [1, D],
    "conf_ln_g": [1, D], "conf_ln_b": [1, D], "conf_w_pw2": [1, D, D], "conf_b_pw2": [1, D],
    "final_g": [D],
}
CONST_SHAPES = {"ident_f": [128, 128], "rope_cos": [128, SEQ], "rope_sin": [128, SEQ], "attn_w_perm": [D, 2 * D],
                "tri": [2, 128, 128], "maskneg": [2, 128, 128], "selA": [128, 32, 128]}


def build_program(steps, dbg=False):
    nc = bass.Bass("TRN2", target_bir_lowering=False)
    K = Ctx()
    K.nc = nc
    K.x_in = nc.dram_tensor("x", [2, SEQ, D], F32, kind="ExternalInput").ap()
    K.ctx_in = nc.dram_tensor("ctx", [2, CTX, D], F32, kind="ExternalInput").ap()
    K.c3 = nc.dram_tensor("c3", [3, D], F32, kind="ExternalInput").ap()
    K.w = {n: nc.dram_tensor(n, s, F32, kind="ExternalInput").ap() for n, s in W_SHAPES.items()}
    K.cst = {n: nc.dram_tensor(n, s, F32, kind="ExternalInput").ap() for n, s in CONST_SHAPES.items()}
    K.out = nc.dram_tensor("out", [2, SEQ, D], F32, kind="ExternalOutput").ap()
    skind = "ExternalOutput" if dbg else "Internal"
    K.xT_d = nc.dram_tensor("xT_d", [D, T], F32, kind=skind).ap()
    K.gT_d = nc.dram_tensor("gT_d", [FH, T], BF16, kind="Internal").ap()
    K.uT_d = nc.dram_tensor("uT_d", [D, T], BF16, kind="Internal").ap()
    K.qT_d = nc.dram_tensor("qT_d", [D, T], BF16, kind="Internal").ap()
    K.xbc_d = nc.dram_tensor("xbc_d", [4096, T], BF16, kind=skind).ap()
    K.z_d = nc.dram_tensor("z_d", [T, 2048], BF16, kind=skind).ap()
    K.dtla_d = nc.dram_tensor("dtla_d", [T, 128], F32, kind=skind).ap()
    K.y_d = nc.dram_tensor("y_d", [2, T, 2048], F32, kind=skind).ap()
    K.kT_d = nc.dram_tensor("kT_d", [D, T], BF16, kind="Internal").ap()
    K.oT_d = nc.dram_tensor("oT_d", [D, T], BF16, kind="Internal").ap()
    K.v_d = nc.dram_tensor("v_d", [T, D], BF16, kind="Internal").ap()
    layers = sorted({i for (_, i) in steps})
    with ExitStack() as st:
        K.st = st
        fw = K.fw = FW(nc, st)
        st.enter_context(nc.Block())
        K.psum = PsumPool([fw.ps(st, [128, 512], F32, f"bank{j}") for j in range(8)])
        K.ident = fw.sb(st, [128, 128], F32, "ident")
        K.ident_bf = fw.sb(st, [128, 128], BF16, "identb")
        K.ones_bf = fw.sb(st, [128, 128], BF16, "onesb")
        K.epsc = fw.sb(st, [128, 1], F32, "epsc")
        fw.sp.dma(K.ident[:, :], K.cst["ident_f"], writes=[K.ident])
        fw.dve.op(lambda e: e.tensor_copy(out=K.ident_bf[:, :], in_=K.ident[:, :]), reads=[K.ident], writes=[K.ident_bf])
        fw.pool.op(lambda e: e.memset(K.ones_bf[:, :], 1.0), writes=[K.ones_bf])
        fw.pool.op(lambda e: e.memset(K.epsc[:, :], EPS), writes=[K.epsc])
        K.onec = fw.sb(st, [128, 1], F32, "onec")
        fw.pool.op(lambda e: e.memset(K.onec[:, :], 1.0), writes=[K.onec])
        fw.barrier()
        phase_init(K)
        phase_mod(K, layers)
        last_ffn = max([n for n, (kind, _) in enumerate(steps) if kind == "ffn"], default=-1)
        K.marks = [("start", 0), ("init", 0)]
        K.marks.append(("mod_done", fw.pe.n_ins))
        for n, (kind, i) in enumerate(steps):
            need_ctx = i < 3
            K.marks.append((f"{kind}{i}", fw.pe.n_ins))
            if kind == "mix":
                MIXERS[i % 3](K, i, need_ctx)
            else:
                phase_ffn_up(K, i, need_ctx)
                K.marks.append((f"  FD{i}", fw.pe.n_ins))
                phase_ffn_down(K, i, need_ctx, final=(n == last_ffn))
        fw.barrier()
        K.marks.append(("end", fw.pe.n_ins))
        K.n_ins = {e.name: e.n_ins for e in fw.engs}
    return nc, K


def mixer_todo(K, i, need_ctx):
    raise NotImplementedError


MIXERS = {0: mixer_todo, 1: mixer_todo, 2: mixer_todo}


ROPE_PERM64 = np.concatenate([np.arange(16, 32), np.arange(0, 16), np.arange(48, 64), np.arange(32, 48)])


def host_consts():
    t = np.arange(SEQ)
    row, col = (t // 64).astype(np.float32), (t % 64).astype(np.float32)
    inv = (10000.0 ** (-np.arange(16, dtype=np.float32) / 16)).astype(np.float32)
    ang = np.stack([row[None, :] * inv[:, None], col[None, :] * inv[:, None]], axis=0)
    cos64 = np.zeros((64, SEQ), np.float32)
    sin64 = np.zeros((64, SEQ), np.float32)
    for ax in range(2):
        c, s = np.cos(ang[ax]), np.sin(ang[ax])
        cos64[ax * 32:ax * 32 + 16] = c
        cos64[ax * 32 + 16:ax * 32 + 32] = c
        sin64[ax * 32:ax * 32 + 16] = -s
        sin64[ax * 32 + 16:ax * 32 + 32] = s
    r = np.arange(128)
    tri = np.stack([(r[:, None] <= r[None, :]), (r[:, None] >= r[None, :])]).astype(np.float32)
    mneg = np.stack([np.where(r[None, :] < r[:, None], NEG, 0.0), np.where(r[None, :] > r[:, None], NEG, 0.0)]).astype(np.float32)
    selA = np.zeros((128, 32, 128), np.float32)
    for h in range(32):
        selA[h, h, :] = 1.0
    return {"tri": tri, "maskneg": mneg, "selA": selA, "ident_f": np.eye(128, dtype=np.float32),
            "rope_cos": np.ascontiguousarray(np.tile(cos64, (2, 1))), "rope_sin": np.ascontiguousarray(np.tile(sin64, (2, 1)))}


FULL_STEPS = [("mix", 0), ("ffn", 0), ("mix", 1), ("ffn", 1), ("mix", 2), ("ffn", 2), ("mix", 3), ("ffn", 3)]


def make_in_maps(inputs, ncores=NCORES):
    cst = host_consts()
    wa = np.asarray(inputs["attn_w_in"][0], dtype=np.float32)
    perm = (np.arange(2 * D) // 64) * 64 + ROPE_PERM64[np.arange(2 * D) % 64]
    cst["attn_w_perm"] = np.ascontiguousarray(wa[:, perm])
    maps = []
    wts = {n: np.ascontiguousarray(inputs[n], dtype=np.float32) for n in W_SHAPES}
    for cidx in range(ncores):
        b0 = 2 * cidx
        m = dict(wts)
        m.update(cst)
        m["x"] = np.ascontiguousarray(inputs["x"][b0:b0 + 2])
        m["ctx"] = np.ascontiguousarray(inputs["ctx"][b0:b0 + 2])
        m["c3"] = np.ascontiguousarray(np.concatenate([inputs["c"][b0:b0 + 2], inputs["c_ctx"][None, :]], axis=0))
        maps.append(m)
    return maps


_CACHE = {}


def kernel(**inputs):
    if "nc" not in _CACHE:
        _CACHE["nc"] = build_program(FULL_STEPS)[0]
    nc = _CACHE["nc"]
    maps = make_in_maps(inputs)
    res = run_bass_kernel_spmd(nc, maps, core_ids=list(range(NCORES)))
    return np.concatenate([r["out"] for r in res.results], axis=0).astype(np.float32)


def load_rowsT(K, cm, mat_ap, nrows, name="rT"):
    fw = K.fw
    dst = fw.sb(cm, [128, nrows], F32, name)
    with ExitStack() as st:
        tmp = fw.sb(st, [128, 128], F32, "vrow")
        for r0 in range(0, nrows, 128):
            n = min(128, nrows - r0)
            fw.sp.dma(tmp[0:n, :], mat_ap[r0:r0 + n, :], writes=[tmp])
            ps = K.psum.get()
            fw.pe.op(lambda e: e.transpose(out=ps[:, 0:n], in_=tmp[0:n, :], identity=K.ident[0:n, 0:n]),
                     reads=[tmp, K.ident], writes=[ps])
            fw.dve.op(lambda e: e.tensor_copy(out=dst[:, r0:r0 + n], in_=ps[:, 0:n]), reads=[ps], writes=[dst])
        fw.barrier()
    return dst


def vecT(K, cm, vec_ap, name="vT"):
    n = vec_ap.shape[0] // 128
    return load_rowsT(K, cm, vec_ap.rearrange("(c p) -> c p", p=128), n, name)


def mixer_conf(K, i, need_ctx):
    fw = K.fw
    xv = xT_view(K)
    uv = K.uT_d.rearrange("(c p) t -> p c t", p=128)
    blks = blocks(need_ctx)
    with ExitStack() as st:
        w1 = fw.sb(st, [128, 8, 2048], BF16, "w1")
        fw.pool.dma(w1[:, :, :], K.w["conf_w_pw1"][0].rearrange("(k p) n -> p k n", p=128), writes=[w1])
        b1 = vecT(K, st, K.w["conf_b_pw1"][0], "b1")
        W = alloc_norm_work(K, st)
        xin = [fw.sb(st, [128, 8, 512], F32, "xin") for _ in range(2)]
        hT = [fw.sb(st, [128, 8, 512], BF16, "hT") for _ in range(2)]
        sig = [fw.sb(st, [128, 512], F32, "sig") for _ in range(2)]
        ust = [fw.sb(st, [128, 8, 512], BF16, "ust") for _ in range(2)]
        def xload(bi):
            _, tn, Nn, _ = blks[bi]
            fw.sp.dma(xin[bi % 2][:, :, :Nn], xv[:, :, tn:tn + Nn], writes=[xin[bi % 2]])

        def nrm(bi):
            v_, _, N_, _ = blks[bi]
            norm_block(K, W, xin[bi % 2], N_, K.A1[i], K.modT[i][:, 0:8, :], v_, hT[bi % 2])

        xload(0)
        xload(1)
        nrm(0)
        for bi, (v, t0, N, isc) in enumerate(blks):
            if bi + 1 < len(blks):
                nrm(bi + 1)
            if bi + 2 < len(blks):
                xload(bi + 2)
            h, us = hT[bi % 2], ust[bi % 2]
            for c in range(8):
                ps1, ps2 = K.psum.get(), K.psum.get()
                for (ps, cc) in ((ps1, c), (ps2, c + 8)):
                    for k in range(8):
                        fw.pe.op(lambda e: e.matmul(ps[:, :N], lhsT=w1[:, k, cc * 128:(cc + 1) * 128], rhs=h[:, k, :N],
                                                    start=(k == 0), stop=(k == 7)),
                                 reads=[w1, h], writes=[ps], inc=(k == 7))
                sg = sig[c % 2]
                fw.act.op(lambda e: e.activation(out=sg[:, :N], in_=ps2[:, :N], func=AF.Sigmoid, bias=b1[:, c + 8:c + 9], scale=1.0),
                          reads=[ps2], writes=[sg])
                fw.dve.op(lambda e: e.scalar_tensor_tensor(out=us[:, c, :N], in0=ps1[:, :N], scalar=b1[:, c:c + 1], in1=sg[:, :N],
                                                           op0=ALU.add, op1=ALU.mult),
                          reads=[ps1, sg], writes=[us])
            fw.sp.dma(uv[:, :, t0:t0 + N], us[:, :, :N], reads=[us])
        fw.barrier()
    K.marks.append((f"  CB{i}", fw.pe.n_ins))
    with ExitStack() as st:
        dwT = load_rowsT(K, st, K.w["conf_dw_w"][0].rearrange("k (c p) -> (k c) p", p=128), 248, "dwT")
        dwb = vecT(K, st, K.w["conf_dw_b"][0], "dwb")
        lng = vecT(K, st, K.w["conf_ln_g"][0], "lng")
        lnb = vecT(K, st, K.w["conf_ln_b"][0], "lnb")
        b2 = vecT(K, st, K.w["conf_b_pw2"][0], "b2")
        gate = K.modT[i][:, 16:24, :]
        gb = fw.sb(st, [128, 8, 3], F32, "gb")
        fw.dve.op(lambda e: e.tensor_tensor(out=gb[:, :, :], in0=gate, in1=b2[:, :].unsqueeze(2).to_broadcast([128, 8, 3]), op=ALU.mult),
                  reads=[b2], writes=[gb])
        diag = fw.sb(st, [128, 8, 31, 128], BF16, "diag")
        n = 0
        for c in range(8):
            for kk in range(31):
                eng = fw.dve if n % 2 == 0 else fw.pool
                n += 1
                col = kk * 8 + c
                eng.op(lambda e: e.tensor_scalar(out=diag[:, c, kk, :], in0=K.ident[:, :], scalar1=dwT[:, col:col + 1], scalar2=None,
                                                 op0=ALU.mult),
                       reads=[dwT, K.ident], writes=[diag])
        w2 = fw.sb(st, [128, 8, D], BF16, "w2")
        fw.pool.dma(w2[:, :, :], K.w["conf_w_pw2"][0].rearrange("(k p) n -> p k n", p=128), writes=[w2])
        xin = [fw.sb(st, [128, 8, 512], F32, "xin") for _ in range(2)]
        uin = [fw.sb(st, [128, 8, 542], BF16, "uin") for _ in range(2)]
        vt = fw.sb(st, [128, 8, 512], F32, "vt")
        vb = fw.sb(st, [128, 8, 512], BF16, "vb")
        sq = fw.sb(st, [128, 8, 512], BF16, "sq")
        sT = fw.sb(st, [128, 8, 512], BF16, "sT")
        mean = fw.sb(st, [128, 512], F32, "mean")
        msq = fw.sb(st, [128, 512], F32, "msq")
        rstd = fw.sb(st, [128, 512], F32, "rstd")
        tmp = [fw.sb(st, [128, 512], F32, "ctmp") for _ in range(2)]

        def load(bi):
            v, t0, N, isc = blks[bi]
            s0 = (t0 // SEG) * SEG + (0 if isc else CTX)
            s1 = s0 + (CTX if isc else SEQ)
            lo, hi = max(t0 - 15, s0), min(t0 + N + 15, s1)
            ui = uin[bi % 2]
            if lo != t0 - 15 or hi != t0 + N + 15:
                fw.pool.op(lambda e: e.memset(ui[:, :, :], 0.0), writes=[ui])
            fw.sp.dma(ui[:, :, lo - (t0 - 15):hi - (t0 - 15)], uv[:, :, lo:hi], writes=[ui])
            fw.sp.dma(xin[bi % 2][:, :, :N], xv[:, :, t0:t0 + N], writes=[xin[bi % 2]])

        load(0)
        for bi, (v, t0, N, isc) in enumerate(blks):
            if bi + 1 < len(blks):
                load(bi + 1)
            xi, ui = xin[bi % 2], uin[bi % 2]
            for c in range(8):
                ps = K.psum.get()
                for kk in range(31):
                    fw.pe.op(lambda e: e.matmul(ps[:, :N], lhsT=diag[:, c, kk, :], rhs=ui[:, c, kk:kk + N], start=(kk == 0), stop=(kk == 30)),
                             reads=[diag, ui], writes=[ps], inc=(kk == 30))
                fw.act.op(lambda e: e.activation(out=vt[:, c, :N], in_=ps[:, :N], func=AF.Identity, bias=dwb[:, c:c + 1], scale=1.0),
                          reads=[ps], writes=[vt])
            fw.pool.op(lambda e: e.tensor_copy(out=vb[:, :, :N], in_=vt[:, :, :N]), reads=[vt], writes=[vb])
            fw.act.op(lambda e: e.activation(out=sq[:, :, :N], in_=vt[:, :, :N], func=AF.Square), reads=[vt], writes=[sq])
            p1, p2 = K.psum.get(), K.psum.get()
            for (ps, src) in ((p1, vb), (p2, sq)):
                for k in range(8):
                    fw.pe.op(lambda e: e.matmul(ps[:, :N], lhsT=K.ones_bf[:, :], rhs=src[:, k, :N], start=(k == 0), stop=(k == 7)),
                             reads=[src, K.ones_bf], writes=[ps], inc=(k == 7))
            fw.dve.op(lambda e: e.tensor_scalar(out=mean[:, :N], in0=p1[:, :N], scalar1=1.0 / D, scalar2=None, op0=ALU.mult),
                      reads=[p1], writes=[mean])
            fw.dve.op(lambda e: e.tensor_tensor(out=msq[:, :N], in0=mean[:, :N], in1=mean[:, :N], op=ALU.mult), reads=[mean], writes=[msq])
            fw.dve.op(lambda e: e.scalar_tensor_tensor(out=rstd[:, :N], in0=p2[:, :N], scalar=1.0 / D, in1=msq[:, :N],
                                                       op0=ALU.mult, op1=ALU.subtract),
                      reads=[p2, msq], writes=[rstd])
            fw.act.op(lambda e: e.activation(out=rstd[:, :N], in_=rstd[:, :N], func=AF.Ln, bias=K.epsc[:, 0:1], scale=1.0),
                      reads=[rstd], writes=[rstd])
            fw.act.op(lambda e: e.activation(out=rstd[:, :N], in_=rstd[:, :N], func=AF.Exp, scale=-0.5), reads=[rstd], writes=[rstd])
            for k in range(8):
                ta, tb = tmp[0], tmp[1]
                fw.pool.op(lambda e: e.tensor_tensor(out=ta[:, :N], in0=vt[:, k, :N], in1=mean[:, :N], op=ALU.subtract),
                           reads=[vt, mean], writes=[ta])
                fw.dve.op(lambda e: e.scalar_tensor_tensor(out=tb[:, :N], in0=ta[:, :N], scalar=lng[:, k:k + 1], in1=rstd[:, :N],
                                                           op0=ALU.mult, op1=ALU.mult),
                          reads=[ta, rstd], writes=[tb])
                fw.act.op(lambda e: e.activation(out=sT[:, k, :N], in_=tb[:, :N], func=AF.Silu, bias=lnb[:, k:k + 1], scale=1.0),
                          reads=[tb], writes=[sT])
            for dc in range(8):
                ps = K.psum.get()
                for k in range(8):
                    fw.pe.op(lambda e: e.matmul(ps[:, :N], lhsT=w2[:, k, dc * 128:(dc + 1) * 128], rhs=sT[:, k, :N], start=(k == 0), stop=(k == 7)),
                             reads=[w2, sT], writes=[ps], inc=(k == 7))
                fw.dve.op(lambda e: e.scalar_tensor_tensor(out=xi[:, dc, :N], in0=ps[:, :N], scalar=gate[:, dc, v:v + 1], in1=xi[:, dc, :N],
                                                           op0=ALU.mult, op1=ALU.add),
                          reads=[ps, xi], writes=[xi])
                fw.pool.op(lambda e: e.tensor_scalar(out=xi[:, dc, :N], in0=xi[:, dc, :N], scalar1=gb[:, dc, v:v + 1], scalar2=None, op0=ALU.add),
                           reads=[xi, gb], writes=[xi])
            fw.sp.dma(xv[:, :, t0:t0 + N], xi[:, :, :N], reads=[xi])
        fw.barrier()


MIXERS[2] = mixer_conf


def mixer_attn(K, i, need_ctx):
    fw = K.fw
    xv = xT_view(K)
    qv = K.qT_d.rearrange("(c p) t -> p c t", p=128)
    kv = K.kT_d.rearrange("(c p) t -> p c t", p=128)
    ov = K.oT_d.rearrange("(c p) t -> p c t", p=128)
    vv = K.v_d.rearrange("(s p) d -> p s d", p=128)
    lam_init = 0.8 - 0.6 * math.exp(-0.3 * i)
    with ExitStack() as st:
        wq = fw.sb(st, [128, 8, 3072], BF16, "wq")
        wp = fw.sb(st, [128, 8, 2048], BF16, "wp")
        fw.pool.dma(wq[:, :, :], K.w["attn_w_in"][0].rearrange("(k p) n -> p k n", p=128), writes=[wq])
        fw.pool.dma(wp[:, :, :], K.cst["attn_w_perm"].rearrange("(k p) n -> p k n", p=128), writes=[wp])
        cosT = fw.sb(st, [128, SEQ], F32, "cosT")
        sinT = fw.sb(st, [128, SEQ], F32, "sinT")
        fw.sp.dma(cosT[:, :], K.cst["rope_cos"], writes=[cosT])
        fw.sp.dma(sinT[:, :], K.cst["rope_sin"], writes=[sinT])
        W = alloc_norm_work(K, st)
        xin = [fw.sb(st, [128, 8, 512], F32, "xin") for _ in range(2)]
        hTs = [fw.sb(st, [128, 8, 512], BF16, "hT") for _ in range(2)]
        qst = fw.sb(st, [128, 8, 512], BF16, "qst")
        kst = fw.sb(st, [128, 8, 512], BF16, "kst")
        vst = fw.sb(st, [128, 4, D], BF16, "vst")
        t1 = [fw.sb(st, [128, 512], F32, "rt1") for _ in range(2)]
        t2 = [fw.sb(st, [128, 512], F32, "rt2") for _ in range(2)]
        blks = blocks(True)

        def xload(bi):
            _, tn, Nn, _ = blks[bi]
            fw.sp.dma(xin[bi % 2][:, :, :Nn], xv[:, :, tn:tn + Nn], writes=[xin[bi % 2]])

        def nrm(bi):
            v_, _, N_, _ = blks[bi]
            norm_block(K, W, xin[bi % 2], N_, K.A1[i], K.modT[i][:, 0:8, :], v_, hTs[bi % 2])

        xload(0)
        xload(1)
        nrm(0)
        n = 0
        for bi, (v, t0, N, isc) in enumerate(blks):
            hT = hTs[bi % 2]
            if bi + 1 < len(blks):
                nrm(bi + 1)
            if bi + 2 < len(blks):
                xload(bi + 2)
            tl = t0 - (t0 // SEG) * SEG - CTX
            for c in range(8):
                for (cb, stg) in ((0, qst), (1024, kst)):
                    psA = K.psum.get()
                    for k in range(8):
                        fw.pe.op(lambda e: e.matmul(psA[:, :N], lhsT=wq[:, k, cb + c * 128:cb + (c + 1) * 128], rhs=hT[:, k, :N],
                                                    start=(k == 0), stop=(k == 7)),
                                 reads=[wq, hT], writes=[psA], inc=(k == 7))
                    if isc:
                        fw.act.op(lambda e: e.activation(out=stg[:, c, :N], in_=psA[:, :N], func=AF.Identity), reads=[psA], writes=[stg])
                        continue
                    psB = K.psum.get()
                    for k in range(8):
                        fw.pe.op(lambda e: e.matmul(psB[:, :N], lhsT=wp[:, k, cb + c * 128:cb + (c + 1) * 128], rhs=hT[:, k, :N],
                                                    start=(k == 0), stop=(k == 7)),
                                 reads=[wp, hT], writes=[psB], inc=(k == 7))
                    a, b_ = t1[n % 2], t2[n % 2]
                    n += 1
                    fw.dve.op(lambda e: e.tensor_tensor(out=a[:, :N], in0=psA[:, :N], in1=cosT[:, tl:tl + N], op=ALU.mult),
                              reads=[psA, cosT], writes=[a])
                    fw.dve.op(lambda e: e.tensor_tensor(out=b_[:, :N], in0=psB[:, :N], in1=sinT[:, tl:tl + N], op=ALU.mult),
                              reads=[psB, sinT], writes=[b_])
                    fw.pool.op(lambda e: e.tensor_tensor(out=stg[:, c, :N], in0=a[:, :N], in1=b_[:, :N], op=ALU.add),
                               reads=[a, b_], writes=[stg])
            for sub in range(N // 128):
                for half in range(2):
                    ps = K.psum.get()
                    for k in range(8):
                        fw.pe.op(lambda e: e.matmul(ps[:, :], lhsT=hT[:, k, sub * 128:(sub + 1) * 128],
                                                    rhs=wq[:, k, 2048 + half * 512:2048 + (half + 1) * 512], start=(k == 0), stop=(k == 7)),
                                 reads=[wq, hT], writes=[ps], inc=(k == 7))
                    fw.act.op(lambda e: e.activation(out=vst[:, sub, half * 512:(half + 1) * 512], in_=ps[:, :], func=AF.Identity),
                              reads=[ps], writes=[vst])
            fw.sp.dma(qv[:, :, t0:t0 + N], qst[:, :, :N], reads=[qst])
            fw.sp.dma(kv[:, :, t0:t0 + N], kst[:, :, :N], reads=[kst])
            fw.sp.dma(vv[:, t0 // 128:(t0 + N) // 128, :], vst[:, :N // 128, :], reads=[vst])
        fw.barrier()
    K.marks.append((f"  AB{i}", fw.pe.n_ins))
    with ExitStack() as st:
        hsel = [fw.sb(st, [128, 128], BF16, "hsel") for _ in range(2)]
        for e_ in range(2):
            fw.pool.op(lambda e: e.memset(hsel[e_][:, :], 0.0), writes=[hsel[e_]])
            fw.pool.op(lambda e: e.memset(hsel[e_][e_ * 64:(e_ + 1) * 64, :], 1.0), writes=[hsel[e_]])
        lp = fw.sb(st, [1, 256], F32, "lp")
        lpp = fw.sb(st, [1, 128], F32, "lpp")
        ls = fw.sb(st, [1, 4], F32, "ls")
        ones1 = fw.sb(st, [1, 128], F32, "ones1")
        neglam = fw.sb(st, [128, 1], F32, "neglam")
        gbc = fw.sb(st, [128, 128], F32, "gbc")
        fw.sp.dma(lp[:, :], K.w["attn_lambda"][0].rearrange("(o a) d -> o (a d)", o=1), writes=[lp])
        fw.sp.dma(gbc[:, :], K.w["attn_norm_g"][0].partition_broadcast(128), writes=[gbc])
        fw.pool.op(lambda e: e.memset(ones1[:, :], 1.0), writes=[ones1])
        fw.dve.op(lambda e: e.tensor_tensor(out=lpp[:, :].rearrange("o (a d) -> o a d", a=2),
                                            in0=lp[:, :].rearrange("o (a b d) -> o a b d", a=2, b=2)[:, :, 0, :],
                                            in1=lp[:, :].rearrange("o (a b d) -> o a b d", a=2, b=2)[:, :, 1, :], op=ALU.mult),
                  reads=[lp], writes=[lpp])
        fw.dve.op(lambda e: e.tensor_reduce(out=ls[:, 0:2], in_=lpp[:, :].rearrange("o (a d) -> o a d", a=2), axis=AX.X, op=ALU.add),
                  reads=[lpp], writes=[ls])
        fw.act.op(lambda e: e.activation(out=ls[:, 0:2], in_=ls[:, 0:2], func=AF.Exp), reads=[ls], writes=[ls])
        fw.dve.op(lambda e: e.tensor_tensor(out=ls[:, 2:3], in0=ls[:, 1:2], in1=ls[:, 0:1], op=ALU.subtract), reads=[ls], writes=[ls])
        fw.dve.op(lambda e: e.tensor_scalar(out=ls[:, 3:4], in0=ls[:, 2:3], scalar1=-lam_init, scalar2=None, op0=ALU.add),
                  reads=[ls], writes=[ls])
        ps = K.psum.get()
        fw.pe.op(lambda e: e.matmul(ps[:, 0:1], lhsT=ones1[:, :], rhs=ls[:, 3:4], start=True, stop=True), reads=[ones1, ls], writes=[ps])
        fw.dve.op(lambda e: e.tensor_copy(out=neglam[:, :], in_=ps[:, 0:1]), reads=[ps], writes=[neglam])
        fw.act.op(lambda e: e.activation(out=gbc[:, :], in_=gbc[:, :], func=AF.Identity, scale=(1.0 - lam_init)), reads=[gbc], writes=[gbc])

        kz = [[fw.sb(st, [128, SEG], BF16, "kz") for _ in range(2)] for _ in range(2)]
        qh = [fw.sb(st, [128, SEG], BF16, "qh") for _ in range(2)]
        va = [fw.sb(st, [128, 18, 129], BF16, "va") for _ in range(2)]
        for p_ in range(2):
            for e_ in range(2):
                fw.pool.op(lambda e: e.memset(kz[p_][e_][:, :], 0.0), writes=[kz[p_][e_]])
            fw.pool.op(lambda e: e.memset(va[p_][:, :, 128:129], 1.0), writes=[va[p_]])
        sqt = fw.sb(st, [128, SEG], BF16, "sqt")
        mx = fw.sb(st, [128, 4, 5], F32, "mx")
        mm = fw.sb(st, [128, 4], F32, "mm")
        negb = fw.sb(st, [128, 2], F32, "negb")
        E = [[fw.sb(st, [128, 18, 512], BF16, "E") for _ in range(2)] for _ in range(2)]
        rss = [fw.sb(st, [128, 2], F32, "rs") for _ in range(2)]
        tts = [fw.sb(st, [128, 128], F32, "tt") for _ in range(2)]
        os_ = [fw.sb(st, [128, 128], F32, "o") for _ in range(2)]
        junk = fw.sb(st, [128, 128], F32, "junk")
        sss = [fw.sb(st, [128, 1], F32, "ss") for _ in range(2)]
        ons = [fw.sb(st, [128, 128], BF16, "on") for _ in range(2)]
        oTst = [fw.sb(st, [128, 512], BF16, "oTst") for _ in range(3)]
        pend = []
        pp = [0]
        cblk = [(0, 512), (512, 512), (1024, 512), (1536, 512), (2048, 256)]

        def loads(it):
            b, hd = it // 8, it % 8
            p_ = it % 2
            s0 = b * SEG
            fw.sp.dma(kz[p_][0][0:64, :], kv[0:64, hd, s0:s0 + SEG], writes=[kz[p_][0]])
            fw.sp.dma(kz[p_][1][64:128, :], kv[64:128, hd, s0:s0 + SEG], writes=[kz[p_][1]])
            fw.sp.dma(qh[p_][:, :], qv[:, hd, s0:s0 + SEG], writes=[qh[p_]])
            fw.sp.dma(va[p_][:, :, 0:128], vv[:, s0 // 128:s0 // 128 + 18, hd * 128:(hd + 1) * 128], writes=[va[p_]])

        loads(0)
        oi = 0
        for it in range(16):
            if it + 1 < 16:
                loads(it + 1)
            b, hd = it // 8, it % 8
            p_ = it % 2
            s0 = b * SEG
            kz0, kz1, q_, v_ = kz[p_][0], kz[p_][1], qh[p_], va[p_]
            for (src, lh, col0) in ((q_, hsel, 0), (kz0, None, 2), (kz1, None, 3)):
                fw.act.op(lambda e: e.activation(out=sqt[:, :], in_=src[:, :], func=AF.Square), reads=[src], writes=[sqt])
                for e_ in (range(2) if lh is not None else range(1)):
                    for bj, (c0, cn) in enumerate(cblk):
                        ps = K.psum.get()
                        lhs = lh[e_] if lh is not None else K.ones_bf
                        fw.pe.op(lambda e: e.matmul(ps[:, :cn], lhsT=lhs[:, :], rhs=sqt[:, c0:c0 + cn], start=True, stop=True),
                                 reads=[lhs, sqt], writes=[ps])
                        fw.dve.op(lambda e: e.tensor_reduce(out=mx[:, col0 + e_, bj:bj + 1], in_=ps[:, :cn], axis=AX.X, op=ALU.max),
                                  reads=[ps], writes=[mx])
            fw.dve.op(lambda e: e.tensor_reduce(out=mm[:, :], in_=mx[:, :, :], axis=AX.X, op=ALU.max), reads=[mx], writes=[mm])
            fw.dve.op(lambda e: e.tensor_tensor(out=negb[:, :], in0=mm[:, 0:2], in1=mm[:, 2:4], op=ALU.mult), reads=[mm], writes=[negb])
            fw.act.op(lambda e: e.activation(out=negb[:, :], in_=negb[:, :], func=AF.Sqrt), reads=[negb], writes=[negb])
            fw.dve.op(lambda e: e.tensor_scalar(out=negb[:, :], in0=negb[:, :], scalar1=-0.125 * 1.02, scalar2=None, op0=ALU.mult),
                      reads=[negb], writes=[negb])
            qbs = ([(0, CTX, 2)] if need_ctx else []) + [(CTX + 512 * n_, 512, 18) for n_ in range(4)]

            def stage1(qi):
                q0, nq, nkt = qbs[qi]
                for e_ in range(2):
                    kz_e = kz0 if e_ == 0 else kz1
                    Eb = E[qi % 2][e_]
                    for j in range(nkt):
                        ps = K.psum.get()
                        fw.pe.op(lambda e: e.matmul(ps[:, :nq], lhsT=kz_e[:, j * 128:(j + 1) * 128], rhs=q_[:, q0:q0 + nq], start=True, stop=True),
                                 reads=[kz_e, q_], writes=[ps])
                        fw.act.op(lambda e: e.activation(out=Eb[:, j, :nq], in_=ps[:, :nq], func=AF.Exp, bias=negb[:, e_:e_ + 1], scale=0.125),
                                  reads=[ps, negb], writes=[Eb])

            def stage2(qi):
                nonlocal oi
                q0, nq, nkt = qbs[qi]
                ost = oTst[oi % 3]
                oi += 1
                ntile = nq // 128
                for i_ in range(ntile):
                    par = pp[0] % 2
                    pp[0] += 1
                    rs, tt, o_, ss, on = rss[par], tts[par], os_[par], sss[par], ons[par]
                    ps = K.psum.get()
                    for e_ in range(2):
                        Eb = E[qi % 2][e_]
                        for j in range(nkt):
                            fw.pe.op(lambda e: e.matmul(ps[:, e_ * 129:(e_ + 1) * 129], lhsT=Eb[:, j, i_ * 128:(i_ + 1) * 128], rhs=v_[:, j, :],
                                                        start=(j == 0), stop=(j == nkt - 1)),
                                     reads=[Eb, v_], writes=[ps], inc=(j == nkt - 1))
                    if pend:
                        pend.pop()()
                    fw.dve.op(lambda e: e.reciprocal(out=rs[:, 0:2], in_=ps[:, 128:258:129]), reads=[ps], writes=[rs])
                    fw.dve.op(lambda e: e.tensor_scalar(out=tt[:, :], in0=ps[:, 129:257], scalar1=rs[:, 1:2], scalar2=neglam[:, 0:1],
                                                        op0=ALU.mult, op1=ALU.mult),
                              reads=[ps, rs, neglam], writes=[tt])
                    fw.dve.op(lambda e: e.scalar_tensor_tensor(out=o_[:, :], in0=ps[:, 0:128], scalar=rs[:, 0:1], in1=tt[:, :],
                                                               op0=ALU.mult, op1=ALU.add),
                              reads=[ps, rs, tt], writes=[o_])
                    fw.act.op(lambda e: e.activation(out=junk[:, :], in_=o_[:, :], func=AF.Square, accum_out=ss[:, 0:1]),
                              reads=[o_], writes=[junk, ss])
                    fw.act.op(lambda e: e.activation(out=ss[:, :], in_=ss[:, :], func=AF.Ln, bias=K.epsc[:, 0:1], scale=1.0 / 128),
                              reads=[ss], writes=[ss])
                    fw.act.op(lambda e: e.activation(out=ss[:, :], in_=ss[:, :], func=AF.Exp, scale=-0.5), reads=[ss], writes=[ss])
                    fw.dve.op(lambda e: e.scalar_tensor_tensor(out=on[:, :], in0=o_[:, :], scalar=ss[:, 0:1], in1=gbc[:, :],
                                                               op0=ALU.mult, op1=ALU.mult),
                              reads=[o_, ss, gbc], writes=[on])

                    def fin(on=on, ost=ost, i_=i_, last=(i_ == ntile - 1), q0=q0, nq=nq, hd=hd, s0=s0):
                        ps2 = K.psum.get()
                        pv = ps2[:, :].bitcast(BF16)
                        fw.pe.op(lambda e: e.transpose(out=pv[:, 0:128], in_=on[:, :], identity=K.ident_bf[:, :]),
                                 reads=[on, K.ident_bf], writes=[ps2])
                        fw.act.op(lambda e: e.activation(out=ost[:, i_ * 128:(i_ + 1) * 128], in_=pv[:, 0:128], func=AF.Identity),
                                  reads=[ps2], writes=[ost])
                        if last:
                            fw.sp.dma(ov[:, hd, s0 + q0:s0 + q0 + nq], ost[:, :nq], reads=[ost])
                    pend.append(fin)

            stage1(0)
            for qi in range(len(qbs)):
                if qi + 1 < len(qbs):
                    stage1(qi + 1)
                stage2(qi)
        while pend:
            pend.pop()()
        fw.barrier()
    K.marks.append((f"  AC{i}", fw.pe.n_ins))
    with ExitStack() as st:
        wo = fw.sb(st, [128, 8, D], BF16, "wo")
        fw.pool.dma(wo[:, :, :], K.w["attn_w_out"][0].rearrange("(k p) n -> p k n", p=128), writes=[wo])
        xin = [fw.sb(st, [128, 8, 512], F32, "xin") for _ in range(2)]
        oin = [fw.sb(st, [128, 8, 512], BF16, "oin") for _ in range(2)]
        gate = K.modT[i][:, 16:24, :]
        blks = blocks(need_ctx)

        def load(bi):
            v, t0, N, isc = blks[bi]
            fw.sp.dma(xin[bi % 2][:, :, :N], xv[:, :, t0:t0 + N], writes=[xin[bi % 2]])
            fw.sp.dma(oin[bi % 2][:, :, :N], ov[:, :, t0:t0 + N], writes=[oin[bi % 2]])

        load(0)
        for bi, (v, t0, N, isc) in enumerate(blks):
            if bi + 1 < len(blks):
                load(bi + 1)
            xi, oi_ = xin[bi % 2], oin[bi % 2]
            for dc in range(8):
                ps = K.psum.get()
                for k in range(8):
                    fw.pe.op(lambda e: e.matmul(ps[:, :N], lhsT=wo[:, k, dc * 128:(dc + 1) * 128], rhs=oi_[:, k, :N], start=(k == 0), stop=(k == 7)),
                             reads=[wo, oi_], writes=[ps], inc=(k == 7))
                fw.dve.op(lambda e: e.scalar_tensor_tensor(out=xi[:, dc, :N], in0=ps[:, :N], scalar=gate[:, dc, v:v + 1], in1=xi[:, dc, :N],
                                                           op0=ALU.mult, op1=ALU.add),
                          reads=[ps, xi], writes=[xi])
            fw.sp.dma(xv[:, :, t0:t0 + N], xi[:, :, :N], reads=[xi])
        fw.barrier()


MIXERS[1] = mixer_attn


def mixer_mamba(K, i, need_ctx):
    fw = K.fw
    j = i // 3
    xv = xT_view(K)
    xbcv = K.xbc_d.rearrange("(c p) t -> p c t", p=128)
    zv = K.z_d
    with ExitStack() as st:
        w = fw.sb(st, [128, 8, 6208], BF16, "win")
        fw.pool.dma(w[:, :, :], K.w["ssm_w_in"][j].rearrange("(k p) n -> p k n", p=128), writes=[w])
        dtb = fw.sb(st, [128, 64], F32, "dtb")
        aneg = fw.sb(st, [128, 64], F32, "aneg")
        fw.sp.dma(dtb[:, :], K.w["ssm_dt_bias"][j].rearrange("a h -> (a h)").partition_broadcast(128), writes=[dtb])
        fw.sp.dma(aneg[:, :], K.w["ssm_a_log"][j].rearrange("a h -> (a h)").partition_broadcast(128), writes=[aneg])
        fw.act.op(lambda e: e.activation(out=aneg[:, :], in_=aneg[:, :], func=AF.Exp), reads=[aneg], writes=[aneg])
        fw.dve.op(lambda e: e.tensor_scalar(out=aneg[:, :], in0=aneg[:, :], scalar1=-1.0, scalar2=None, op0=ALU.mult), reads=[aneg], writes=[aneg])
        W = alloc_norm_work(K, st)
        xin = [fw.sb(st, [128, 8, 512], F32, "xin") for _ in range(2)]
        hTs = [fw.sb(st, [128, 8, 512], BF16, "hT") for _ in range(2)]
        xst = [fw.sb(st, [128, 4, 512], BF16, "xst") for _ in range(2)]
        zst = [fw.sb(st, [128, 2048], BF16, "zst") for _ in range(2)]
        dtl = [fw.sb(st, [128, 128], F32, "dtl") for _ in range(2)]
        dtmp = fw.sb(st, [128, 64], F32, "dtmp")
        blks = blocks(True)

        def xload(bi):
            _, tn, Nn, _ = blks[bi]
            fw.sp.dma(xin[bi % 2][:, :, :Nn], xv[:, :, tn:tn + Nn], writes=[xin[bi % 2]])

        def nrm(bi):
            v_, _, N_, _ = blks[bi]
            norm_block(K, W, xin[bi % 2], N_, K.A1[i], K.modT[i][:, 0:8, :], v_, hTs[bi % 2])

        xload(0)
        xload(1)
        nrm(0)
        zi = 0
        for bi, (v, t0, N, isc) in enumerate(blks):
            hT = hTs[bi % 2]
            if bi + 1 < len(blks):
                nrm(bi + 1)
            if bi + 2 < len(blks):
                xload(bi + 2)
            for cc in range(32):
                xs_ = xst[(cc // 4) % 2]
                ps = K.psum.get()
                for k in range(8):
                    fw.pe.op(lambda e: e.matmul(ps[:, :N], lhsT=w[:, k, 2048 + cc * 128:2048 + (cc + 1) * 128], rhs=hT[:, k, :N],
                                                start=(k == 0), stop=(k == 7)),
                             reads=[w, hT], writes=[ps], inc=(k == 7))
                if cc % 2 == 0:
                    fw.act.op(lambda e: e.activation(out=xs_[:, cc % 4, :N], in_=ps[:, :N], func=AF.Identity), reads=[ps], writes=[xs_])
                else:
                    fw.dve.op(lambda e: e.tensor_copy(out=xs_[:, cc % 4, :N], in_=ps[:, :N]), reads=[ps], writes=[xs_])
                if cc % 4 == 3:
                    fw.sp.dma(xbcv[:, cc - 3:cc + 1, t0:t0 + N], xs_[:, :, :N], reads=[xs_])
            for sub in range(N // 128):
                zs_ = zst[zi % 2]
                dl = dtl[zi % 2]
                zi += 1
                for q4 in range(4):
                    ps = K.psum.get()
                    for k in range(8):
                        fw.pe.op(lambda e: e.matmul(ps[:, :], lhsT=hT[:, k, sub * 128:(sub + 1) * 128], rhs=w[:, k, q4 * 512:(q4 + 1) * 512],
                                                    start=(k == 0), stop=(k == 7)),
                                 reads=[w, hT], writes=[ps], inc=(k == 7))
                    fw.act.op(lambda e: e.activation(out=zs_[:, q4 * 512:(q4 + 1) * 512], in_=ps[:, :], func=AF.Silu), reads=[ps], writes=[zs_])
                ps = K.psum.get()
                for k in range(8):
                    fw.pe.op(lambda e: e.matmul(ps[:, 0:64], lhsT=hT[:, k, sub * 128:(sub + 1) * 128], rhs=w[:, k, 6144:6208],
                                                start=(k == 0), stop=(k == 7)),
                             reads=[w, hT], writes=[ps], inc=(k == 7))
                fw.dve.op(lambda e: e.tensor_tensor(out=dtmp[:, :], in0=ps[:, 0:64], in1=dtb[:, :], op=ALU.add), reads=[ps, dtb], writes=[dtmp])
                fw.act.op(lambda e: e.activation(out=dtmp[:, :], in_=dtmp[:, :], func=AF.Exp), reads=[dtmp], writes=[dtmp])
                fw.act.op(lambda e: e.activation(out=dl[:, 0:64], in_=dtmp[:, :], func=AF.Ln, bias=K.onec[:, 0:1], scale=1.0), reads=[dtmp], writes=[dl])
                fw.dve.op(lambda e: e.tensor_tensor(out=dl[:, 64:128], in0=dl[:, 0:64], in1=aneg[:, :], op=ALU.mult), reads=[dl, aneg], writes=[dl])
                ts = t0 + sub * 128
                fw.sp.dma(zv[ts:ts + 128, :], zs_[:, :], reads=[zs_])
                fw.sp.dma(K.dtla_d[ts:ts + 128, :], dl[:, :], reads=[dl])
        fw.barrier()
    K.marks.append((f"  MB{i}", fw.pe.n_ins))
    with ExitStack() as st:
        tri = [fw.sb(st, [128, 128], F32, "tri") for _ in range(2)]
        mneg = [fw.sb(st, [128, 128], BF16, "mneg") for _ in range(2)]
        selA = fw.sb(st, [128, 32, 128], BF16, "selA")
        onesf = fw.sb(st, [128, 128], F32, "onesf")
        fw.pool.op(lambda e: e.memset(onesf[:, :], 1.0), writes=[onesf])
        for dr in range(2):
            fw.sp.dma(tri[dr][:, :], K.cst["tri"][dr], writes=[tri[dr]])
            fw.pool.dma(mneg[dr][:, :], K.cst["maskneg"][dr], writes=[mneg[dr]])
        fw.pool.dma(selA[:, :, :], K.cst["selA"], writes=[selA])
        diag, cbT, Dbc = [], [], []
        for dr in range(2):
            cwT = load_rowsT(K, st, K.w["ssm_conv_w"][j, dr].rearrange("k (c p) -> (k c) p", p=128), 128, "cwT")
            cbT.append(vecT(K, st, K.w["ssm_conv_b"][j, dr], "cbT"))
            dg = fw.sb(st, [128, 32, 4, 128], BF16, "sdiag")
            n = 0
            for cc in range(32):
                for kk in range(4):
                    eng = fw.dve if n % 2 == 0 else fw.pool
                    n += 1
                    col = kk * 32 + cc
                    eng.op(lambda e: e.tensor_scalar(out=dg[:, cc, kk, :], in0=K.ident[:, :], scalar1=cwT[:, col:col + 1], scalar2=None, op0=ALU.mult),
                           reads=[cwT, K.ident], writes=[dg])
            diag.append(dg)
            db = fw.sb(st, [128, 32], F32, "Dbc")
            fw.sp.dma(db[:, :], K.w["ssm_d"][j, dr].partition_broadcast(128), writes=[db])
            Dbc.append(db)
        xbi = [fw.sb(st, [128, 32, 131], BF16, "xbi") for _ in range(2)]
        dli = [fw.sb(st, [128, 128], F32, "dli") for _ in range(2)]

        def two(shape, dt_, nm):
            return [fw.sb(st, shape, dt_, nm) for _ in range(2)]
        uTs, xss, xrs, xds = two([128, 32, 128], BF16, "uT"), two([128, 2048], BF16, "xs"), two([128, 2048], BF16, "xr"), \
            two([128, 2048], BF16, "xd")
        Btoks = two([128, 8, 128], BF16, "Btok")
        acss, dsbs, dtes, expacs, decays, w2s = (two([128, 32], F32, nm) for nm in ("acs", "dsb", "dte", "expac", "decay", "w2"))
        aTs = [[fw.sb(st, [128, 128], BF16, "aT") for _ in range(4)] for _ in range(2)]
        hifs = two([32, 128], F32, "hif")
        for aT in aTs:
            for a_ in aT:
                fw.pool.op(lambda e: e.memset(a_[:, :], 0.0), writes=[a_])
        Lexp = [fw.sb(st, [128, 512], F32, "Lexp") for _ in range(2)]
        G = [fw.sb(st, [128, 4, 128], BF16, "G") for _ in range(2)]
        yt = [fw.sb(st, [128, 256], F32, "yt") for _ in range(2)]
        yt2 = [fw.sb(st, [128, 256], F32, "yt2") for _ in range(2)]
        ych = [fw.sb(st, [128, 2048], F32, "ych") for _ in range(2)]
        hst = [fw.sb(st, [128, 256], F32, "hst") for _ in range(8)]
        hb = [fw.sb(st, [128, 256], BF16, "hb") for _ in range(8)]

        pa_bank, pcb_bank = K.psum.bufs[0], K.psum.bufs[1]
        rot = PsumPool(K.psum.bufs[2:])
        sched = []
        for b in range(2):
            for dr in range(2):
                cl = [(b * SEG + c * 128, b * SEG, b * SEG + CTX) for c in range(2)] + \
                     [(b * SEG + CTX + c * 128, b * SEG + CTX, (b + 1) * SEG) for c in range(16)]
                if dr == 1:
                    cl = cl[0:2][::-1] + cl[2:][::-1]
                for ci, (t0, s0, s1) in enumerate(cl):
                    sched.append((b, dr, t0, s0, s1, ci == 0))
        NCH = len(sched)

        def load(n):
            b, dr, t0, s0, s1, first = sched[n]
            xb, dl = xbi[n % 2], dli[n % 2]
            if dr == 0:
                if t0 == s0:
                    fw.pool.op(lambda e: e.memset(xb[:, :, 0:3], 0.0), writes=[xb])
                    fw.sp.dma(xb[:, :, 3:131], xbcv[:, :, t0:t0 + 128], writes=[xb])
                else:
                    fw.sp.dma(xb[:, :, 0:131], xbcv[:, :, t0 - 3:t0 + 128], writes=[xb])
            else:
                if t0 + 128 == s1:
                    fw.pool.op(lambda e: e.memset(xb[:, :, 128:131], 0.0), writes=[xb])
                    fw.sp.dma(xb[:, :, 0:128], xbcv[:, :, t0:t0 + 128], writes=[xb])
                else:
                    fw.sp.dma(xb[:, :, 0:131], xbcv[:, :, t0:t0 + 131], writes=[xb])
            fw.sp.dma(dl[:, :], K.dtla_d[t0:t0 + 128, :], writes=[dl])

        def P1(n):
            b, dr, t0, s0, s1, first = sched[n]
            p = n % 2
            xb, dl = xbi[p], dli[p]
            acs, dsb, dte, expac, decay, w2, aT, hif, uT = acss[p], dsbs[p], dtes[p], expacs[p], decays[p], w2s[p], aTs[p], hifs[p], uTs[p]
            dt = dl[:, dr * 32:(dr + 1) * 32]
            la = dl[:, 64 + dr * 32:64 + (dr + 1) * 32]
            pa = pa_bank
            fw.pe.op(lambda e: e.matmul(pa[:, 0:32], lhsT=tri[dr][:, :], rhs=la, start=True, stop=True), reads=[tri[dr], dl], writes=[pa])
            fw.pe.op(lambda e: e.matmul(pa[:, 32:64], lhsT=onesf[:, :], rhs=la, start=True, stop=True), reads=[onesf, dl], writes=[pa])
            fw.pe.op(lambda e: e.matmul(pa[0:32, 64:192], lhsT=la, rhs=tri[dr][:, :], start=True, stop=True), reads=[tri[dr], dl], writes=[pa])
            fw.dve.op(lambda e: e.tensor_copy(out=acs[:, :], in_=pa[:, 0:32]), reads=[pa], writes=[acs])
            fw.dve.op(lambda e: e.tensor_tensor(out=dsb[:, :], in0=pa[:, 32:64], in1=acs[:, :], op=ALU.subtract), reads=[pa, acs], writes=[dsb])
            fw.dve.op(lambda e: e.tensor_copy(out=aT[0][0:32, :], in_=pa[0:32, 64:192]), reads=[pa], writes=[aT[0]])
            fw.dve.op(lambda e: e.tensor_copy(out=hif[:, :], in_=aT[0][0:32, :]), reads=[aT[0]], writes=[hif])
            fw.dve.op(lambda e: e.tensor_tensor(out=aT[1][0:32, :], in0=pa[0:32, 64:192], in1=hif[:, :], op=ALU.subtract), reads=[pa, hif], writes=[aT[1]])
            fw.dve.op(lambda e: e.tensor_scalar(out=aT[2][0:32, :], in0=aT[0][0:32, :], scalar1=-1.0, scalar2=None, op0=ALU.mult), reads=[aT[0]], writes=[aT[2]])
            fw.dve.op(lambda e: e.tensor_scalar(out=aT[3][0:32, :], in0=aT[1][0:32, :], scalar1=-1.0, scalar2=None, op0=ALU.mult), reads=[aT[1]], writes=[aT[3]])
            fw.act.op(lambda e: e.activation(out=decay[:, :], in_=pa[:, 32:64], func=AF.Exp), reads=[pa], writes=[decay])
            fw.act.op(lambda e: e.activation(out=dte[:, :], in_=dsb[:, :], func=AF.Exp), reads=[dsb], writes=[dte])
            fw.act.op(lambda e: e.activation(out=expac[:, :], in_=acs[:, :], func=AF.Exp), reads=[acs], writes=[expac])
            fw.dve.op(lambda e: e.tensor_tensor(out=w2[:, :], in0=dt, in1=dte[:, :], op=ALU.mult), reads=[dl, dte], writes=[w2])
            for c4 in range(8):
                ps = rot.get()
                for q in range(4):
                    cc = c4 * 4 + q
                    for kk in range(4):
                        off = kk if dr == 0 else 3 - kk
                        fw.pe.op(lambda e: e.matmul(ps[:, q * 128:(q + 1) * 128], lhsT=diag[dr][:, cc, kk, :], rhs=xb[:, cc, off:off + 128],
                                                    start=(kk == 0), stop=(kk == 3)),
                                 reads=[diag[dr], xb], writes=[ps], inc=(kk == 3))
                for q in range(4):
                    cc = c4 * 4 + q
                    fw.act.op(lambda e: e.activation(out=uT[:, cc, :], in_=ps[:, q * 128:(q + 1) * 128], func=AF.Silu, bias=cbT[dr][:, cc:cc + 1], scale=1.0),
                              reads=[ps], writes=[uT])

        def P2(n):
            b, dr, t0, s0, s1, first = sched[n]
            p = n % 2
            dl = dli[p]
            dt = dl[:, dr * 32:(dr + 1) * 32]
            uT, xs, xr, xd, Btok, w2 = uTs[p], xss[p], xrs[p], xds[p], Btoks[p], w2s[p]
            xsD = ych[p]
            for hx in range(2):
                ps = rot.get()
                pv = ps[:, :].bitcast(BF16)
                for q in range(8):
                    fw.pe.op(lambda e: e.transpose(out=pv[:, q * 128:(q + 1) * 128], in_=uT[:, hx * 8 + q, :], identity=K.ident_bf[:, :]),
                             reads=[uT, K.ident_bf], writes=[ps], inc=(q == 7))
                fw.dve.op(lambda e: e.tensor_copy(out=xs[:, hx * 1024:(hx + 1) * 1024], in_=pv[:, :]), reads=[ps], writes=[xs])
            ps = rot.get()
            pv = ps[:, :].bitcast(BF16)
            for q in range(8):
                fw.pe.op(lambda e: e.transpose(out=pv[:, q * 128:(q + 1) * 128], in_=uT[:, 16 + q, :], identity=K.ident_bf[:, :]),
                         reads=[uT, K.ident_bf], writes=[ps], inc=(q == 7))
            fw.act.op(lambda e: e.activation(out=Btok[:, :, :], in_=pv[:, :].rearrange("p (g n) -> p g n", g=8), func=AF.Identity), reads=[ps], writes=[Btok])
            xs3 = xs[:, :].rearrange("p (h d) -> p h d", h=32)
            fw.dve.op(lambda e: e.tensor_tensor(out=xr[:, :].rearrange("p (h d) -> p h d", h=32), in0=xs3,
                                                in1=dt.unsqueeze(2).to_broadcast([128, 32, 64]), op=ALU.mult), reads=[xs, dl], writes=[xr])
            fw.pool.op(lambda e: e.tensor_tensor(out=xd[:, :].rearrange("p (h d) -> p h d", h=32), in0=xs3,
                                                 in1=w2[:, :].unsqueeze(2).to_broadcast([128, 32, 64]), op=ALU.mult), reads=[xs, w2], writes=[xd])
            fw.pool.op(lambda e: e.tensor_tensor(out=xsD[:, :].rearrange("p (h d) -> p h d", h=32), in0=xs3,
                                                 in1=Dbc[dr][:, :].unsqueeze(2).to_broadcast([128, 32, 64]), op=ALU.mult), reads=[xs, Dbc[dr]], writes=[xsD])

        def CBm(n, half):
            uT = uTs[n % 2]
            for q in range(4):
                g = half * 4 + q
                fw.pe.op(lambda e: e.matmul(pcb_bank[:, q * 128:(q + 1) * 128], lhsT=uT[:, 16 + g, :], rhs=uT[:, 24 + g, :], start=True, stop=True),
                         reads=[uT], writes=[pcb_bank], inc=(q == 3))

        def SEGm(n, g):
            b, dr, t0, s0, s1, first = sched[n]
            aT = aTs[n % 2]
            pS = rot.get()
            fw.pe.op(lambda e: e.matmul(pS[:, :].rearrange("p (j l) -> p j l", j=4), lhsT=K.ident_bf[:, :], rhs=bc_mid(mneg[dr][:, :], 4),
                                        start=True, stop=False, skip_group_check=True),
                     reads=[K.ident_bf, mneg[dr]], writes=[pS], inc=False)
            for jj in range(4):
                hj = 4 * g + jj
                o_ap = pS[:, jj * 128:(jj + 1) * 128]
                fw.pe.op(lambda e: e.matmul(o_ap, lhsT=selA[:, hj, :], rhs=aT[0][:, :], start=False, stop=False, skip_group_check=True),
                         reads=[selA, aT[0]], writes=[pS], inc=False)
                fw.pe.op(lambda e: e.matmul(o_ap, lhsT=selA[:, hj, :], rhs=aT[1][:, :], start=False, stop=False, skip_group_check=True),
                         reads=[selA, aT[1]], writes=[pS], inc=False)
                fw.pe.op(lambda e: e.matmul(o_ap, lhsT=aT[2][:, :], rhs=selA[:, hj, :], start=False, stop=False, skip_group_check=True),
                         reads=[selA, aT[2]], writes=[pS], inc=False)
                fw.pe.op(lambda e: e.matmul(o_ap, lhsT=aT[3][:, :], rhs=selA[:, hj, :], start=False, stop=True, skip_group_check=True),
                         reads=[selA, aT[3]], writes=[pS], inc=(jj == 3))
            Le, Gg = Lexp[g % 2], G[g % 2]
            fw.act.op(lambda e: e.activation(out=Le[:, :], in_=pS[:, :], func=AF.Exp), reads=[pS], writes=[Le])
            fw.dve.op(lambda e: e.tensor_tensor(out=Gg[:, :, :], in0=Le[:, :].rearrange("p (j l) -> p j l", j=4),
                                                in1=bc_mid(pcb_bank[:, (g % 4) * 128:(g % 4 + 1) * 128], 4), op=ALU.mult),
                      reads=[Le, pcb_bank], writes=[Gg])

        def Ym(n, g):
            b, dr, t0, s0, s1, first = sched[n]
            p = n % 2
            uT, xr, xd, Btok, expac, decay = uTs[p], xrs[p], xds[p], Btoks[p], expacs[p], decays[p]
            yc = ych[p]
            Gg = G[g % 2]
            pY = rot.get()
            for jj in range(4):
                hj = 4 * g + jj
                fw.pe.op(lambda e: e.matmul(pY[:, jj * 64:(jj + 1) * 64], lhsT=Gg[:, jj, :], rhs=xr[:, hj * 64:(hj + 1) * 64], start=True, stop=True),
                         reads=[Gg, xr], writes=[pY], inc=False)
            fw.pe.op(lambda e: e.matmul(pY[:, 256:512], lhsT=uT[:, 24 + g, :], rhs=hb[g][:, :], start=True, stop=True),
                     reads=[uT, hb[g]], writes=[pY])
            pT = rot.get()
            fw.pe.op(lambda e: e.matmul(pT[:, 0:256], lhsT=Btok[:, g, :], rhs=xd[:, g * 256:(g + 1) * 256], start=True, stop=True),
                     reads=[Btok, xd], writes=[pT])
            ya, yb_ = yt[g % 2], yt2[g % 2]
            fw.dve.op(lambda e: e.tensor_tensor(out=ya[:, :].rearrange("p (j d) -> p j d", j=4), in0=pY[:, 256:512].rearrange("p (j d) -> p j d", j=4),
                                                in1=expac[:, 4 * g:4 * g + 4].unsqueeze(2).to_broadcast([128, 4, 64]), op=ALU.mult),
                      reads=[pY, expac], writes=[ya])
            fw.dve.op(lambda e: e.tensor_tensor(out=yb_[:, :], in0=pY[:, 0:256], in1=ya[:, :], op=ALU.add), reads=[pY, ya], writes=[yb_])
            fw.pool.op(lambda e: e.tensor_tensor(out=yc[:, g * 256:(g + 1) * 256], in0=yb_[:, :], in1=yc[:, g * 256:(g + 1) * 256], op=ALU.add),
                       reads=[yb_, yc], writes=[yc])
            fw.pool.op(lambda e: e.tensor_tensor(out=hst[g][:, :].rearrange("p (j d) -> p j d", j=4), in0=hst[g][:, :].rearrange("p (j d) -> p j d", j=4),
                                                 in1=decay[:, 4 * g:4 * g + 4].unsqueeze(2).to_broadcast([128, 4, 64]), op=ALU.mult),
                       reads=[hst[g], decay], writes=[hst[g]])
            fw.dve.op(lambda e: e.tensor_tensor(out=hst[g][:, :], in0=pT[:, 0:256], in1=hst[g][:, :], op=ALU.add), reads=[pT, hst[g]], writes=[hst[g]])
            fw.act.op(lambda e: e.activation(out=hb[g][:, :], in_=hst[g][:, :], func=AF.Identity), reads=[hst[g]], writes=[hb[g]])

        load(0)
        if NCH > 1:
            load(1)
        P1(0)
        P2(0)
        for n, (b, dr, t0, s0, s1, first) in enumerate(sched):
            if first:
                for g in range(8):
                    fw.pool.op(lambda e: e.memset(hst[g][:, :], 0.0), writes=[hst[g]])
                    fw.pool.op(lambda e: e.memset(hb[g][:, :], 0.0), writes=[hb[g]])
            CBm(n, 0)
            SEGm(n, 0)
            for g in range(8):
                if g + 1 < 8:
                    if g + 1 == 4:
                        CBm(n, 1)
                    SEGm(n, g + 1)
                Ym(n, g)
                if g == 1 and n + 1 < NCH:
                    P1(n + 1)
                if g == 5 and n + 1 < NCH:
                    P2(n + 1)
            if n + 2 < NCH:
                load(n + 2)
            fw.sp.dma(K.y_d[dr, t0:t0 + 128, :], ych[n % 2][:, :], reads=[ych[n % 2]])
        fw.barrier()
    K.marks.append((f"  MC{i}", fw.pe.n_ins))
    with ExitStack() as st:
        wo = fw.sb(st, [128, 16, D], BF16, "wo")
        fw.pool.dma(wo[:, :, :], K.w["ssm_w_out"][j].rearrange("(k p) n -> p k n", p=128), writes=[wo])
        gnb = fw.sb(st, [128, 2048], F32, "gnb")
        fw.sp.dma(gnb[:, :], K.w["ssm_norm_g"][j].partition_broadcast(128), writes=[gnb])
        xin = [fw.sb(st, [128, 8, 512], F32, "xin") for _ in range(2)]
        yf = [fw.sb(st, [128, 2048], F32, "yf") for _ in range(2)]
        yb = [fw.sb(st, [128, 2048], F32, "yb") for _ in range(2)]
        zs = [fw.sb(st, [128, 2048], BF16, "zs") for _ in range(2)]
        ygs = [fw.sb(st, [128, 2048], F32, "yg") for _ in range(2)]
        sqjs = [fw.sb(st, [128, 2048], F32, "sqj") for _ in range(2)]
        ssqs = [fw.sb(st, [128, 8], F32, "ssq") for _ in range(2)]
        yns = [fw.sb(st, [128, 2048], F32, "yn") for _ in range(2)]
        ynbs = [fw.sb(st, [128, 2048], BF16, "ynb") for _ in range(2)]
        ynT = fw.sb(st, [128, 16, 512], BF16, "ynT")
        gate = K.modT[i][:, 16:24, :]
        blks = blocks(need_ctx)
        subs = [(bi, s) for bi, (v, t0, N, isc) in enumerate(blks) for s in range(N // 128)]

        def load(si):
            bi, s = subs[si]
            v, t0, N, isc = blks[bi]
            ts = t0 + s * 128
            fw.sp.dma(yf[si % 2][:, :], K.y_d[0, ts:ts + 128, :], writes=[yf[si % 2]])
            fw.sp.dma(yb[si % 2][:, :], K.y_d[1, ts:ts + 128, :], writes=[yb[si % 2]])
            fw.sp.dma(zs[si % 2][:, :], zv[ts:ts + 128, :], writes=[zs[si % 2]])
            if s == 0:
                fw.sp.dma(xin[bi % 2][:, :, :N], xv[:, :, t0:t0 + N], writes=[xin[bi % 2]])

        def stA(si):
            a, b_, z_ = yf[si % 2], yb[si % 2], zs[si % 2]
            yg, ssq, sqj = ygs[si % 2], ssqs[si % 2], sqjs[si % 2]
            fw.pool.op(lambda e: e.tensor_tensor(out=a[:, :], in0=a[:, :], in1=b_[:, :], op=ALU.add), reads=[a, b_], writes=[a])
            fw.dve.op(lambda e: e.tensor_tensor(out=yg[:, :], in0=a[:, :], in1=z_[:, :], op=ALU.mult), reads=[a, z_], writes=[yg])
            fw.act.op(lambda e: e.activation(out=sqj[:, :], in_=yg[:, :], func=AF.Square), reads=[yg], writes=[sqj])
            fw.dve.op(lambda e: e.tensor_reduce(out=ssq[:, :], in_=sqj[:, :].rearrange("p (g d) -> p g d", g=8), axis=AX.X, op=ALU.add),
                      reads=[sqj], writes=[ssq])
            fw.act.op(lambda e: e.activation(out=ssq[:, :], in_=ssq[:, :], func=AF.Ln, bias=K.epsc[:, 0:1], scale=1.0 / 256), reads=[ssq], writes=[ssq])
            fw.act.op(lambda e: e.activation(out=ssq[:, :], in_=ssq[:, :], func=AF.Exp, scale=-0.5), reads=[ssq], writes=[ssq])

        def stB(si):
            bi, s = subs[si]
            v, t0, N, isc = blks[bi]
            yg, ssq, yn, ynb = ygs[si % 2], ssqs[si % 2], yns[si % 2], ynbs[si % 2]
            fw.dve.op(lambda e: e.tensor_tensor(out=yn[:, :].rearrange("p (g d) -> p g d", g=8), in0=yg[:, :].rearrange("p (g d) -> p g d", g=8),
                                                in1=ssq[:, :].unsqueeze(2).to_broadcast([128, 8, 256]), op=ALU.mult), reads=[yg, ssq], writes=[yn])
            fw.pool.op(lambda e: e.tensor_tensor(out=ynb[:, :], in0=yn[:, :], in1=gnb[:, :], op=ALU.mult), reads=[yn, gnb], writes=[ynb])
            for hx in range(2):
                ps = K.psum.get()
                pv = ps[:, :].bitcast(BF16)
                for q in range(8):
                    cc = hx * 8 + q
                    fw.pe.op(lambda e: e.transpose(out=pv[:, q * 128:(q + 1) * 128], in_=ynb[:, cc * 128:(cc + 1) * 128], identity=K.ident_bf[:, :]),
                             reads=[ynb, K.ident_bf], writes=[ps], inc=(q == 7))
                src = pv[:, :].rearrange("p (c t) -> p c t", c=8)
                dst = ynT[:, hx * 8:(hx + 1) * 8, s * 128:(s + 1) * 128]
                if hx == 0:
                    fw.act.op(lambda e: e.activation(out=dst, in_=src, func=AF.Identity), reads=[ps], writes=[ynT])
                else:
                    fw.dve.op(lambda e: e.tensor_copy(out=dst, in_=src), reads=[ps], writes=[ynT])
            if s != N // 128 - 1:
                return
            xi = xin[bi % 2]
            for dc in range(8):
                ps = K.psum.get()
                for k in range(16):
                    fw.pe.op(lambda e: e.matmul(ps[:, :N], lhsT=wo[:, k, dc * 128:(dc + 1) * 128], rhs=ynT[:, k, :N], start=(k == 0), stop=(k == 15)),
                             reads=[wo, ynT], writes=[ps], inc=(k == 15))
                fw.dve.op(lambda e: e.scalar_tensor_tensor(out=xi[:, dc, :N], in0=ps[:, :N], scalar=gate[:, dc, v:v + 1], in1=xi[:, dc, :N],
                                                           op0=ALU.mult, op1=ALU.add),
                          reads=[ps, xi], writes=[xi])
            fw.sp.dma(xv[:, :, t0:t0 + N], xi[:, :, :N], reads=[xi])

        load(0)
        if len(subs) > 1:
            load(1)
        stA(0)
        for si in range(len(subs)):
            if si + 1 < len(subs):
                stA(si + 1)
            if si + 2 < len(subs):
                load(si + 2)
            stB(si)
        fw.barrier()


MIXERS[0] = mixer_mamba
```
